# Optimizing a Trainium2 kernel written in Bass

```python
import jax
import jax.numpy as jnp
from jax import lax
import numpy as np

D_MODEL = 1024
BATCH = 16
SEQ = 256
DEPTH = 2
DEC_BATCH = 2
DEC_SEQ = 1024
PAST_LEN = 256

GRID_W = 64
N_EVEN = (DEPTH + 1) // 2
N_ODD = DEPTH // 2
HEAD_DIM = 64
NA_WIDTH = D_MODEL // 2
NA_HEADS = NA_WIDTH // HEAD_DIM
WIN_H = 8
WIN_W = 16
LRU_WIDTH = D_MODEL // 2
LRU_BLOCKS = 8
LRU_BLOCK = LRU_WIDTH // LRU_BLOCKS
LRU_C = 8.0
CONV_W = 4
FOURIER_GROUPS = 4
D_FF = 2816
N_MOD = 9
IN_WIDTH = 3 * NA_WIDTH + 2 * LRU_WIDTH
CTX_Q_BLOCK = 128
EPS = 1e-6

kernel_name = 'hybrid_natten_rglru_fnet_diffusion_step'


def rms_norm(x, g):
    xf = x.astype(jnp.float32)
    y = xf * lax.rsqrt(jnp.mean(xf * xf, axis=-1, keepdims=True) + EPS)
    return (y * g.astype(jnp.float32)).astype(x.dtype)


def modulate(h, shift, scale):
    return h * (1 + scale[:, None, :]) + shift[:, None, :]


def swiglu(h, w_gate, w_up, w_down):
    return (jax.nn.silu(h @ w_gate) * (h @ w_up)) @ w_down


def depthwise_conv_centred(x, w, b):
    t = x.shape[1]
    left = (CONV_W - 1) // 2
    xp = jnp.pad(x, ((0, 0), (left, CONV_W - 1 - left), (0, 0)))
    out = b
    for j in range(CONV_W):
        out = out + xp[:, j:j + t] * w[j]
    return out


def block_diag_linear(x, w, b):
    bsz, t, _ = x.shape
    y = jnp.einsum('btni,nij->btnj', x.reshape(bsz, t, LRU_BLOCKS, LRU_BLOCK), w)
    return y.reshape(bsz, t, LRU_WIDTH) + b


def rglru_coeffs(xc, w_r, b_r, w_i, b_i, lam):
    r = jax.nn.sigmoid(block_diag_linear(xc, w_r, b_r).astype(jnp.float32))
    i = jax.nn.sigmoid(block_diag_linear(xc, w_i, b_i).astype(jnp.float32))
    log_a = LRU_C * r * jax.nn.log_sigmoid(lam.astype(jnp.float32))
    a = jnp.exp(log_a)
    u = jnp.sqrt(-jnp.expm1(2.0 * log_a)) * (i * xc.astype(jnp.float32))
    return a, u


def linear_scan(a, u, h0, reverse):
    def step(h, au):
        h = au[0] * h + au[1]
        return h, h
    h_last, hs = lax.scan(step, h0, (jnp.swapaxes(a, 0, 1), jnp.swapaxes(u, 0, 1)), reverse=reverse)
    return jnp.swapaxes(hs, 0, 1), h_last


def rglru_bidir(xb, gb, conv_w, conv_b, w_r, b_r, w_i, b_i, lam, h0_fwd, h0_bwd):
    xc = depthwise_conv_centred(xb, conv_w, conv_b)
    a_f, u_f = rglru_coeffs(xc, w_r[0], b_r[0], w_i[0], b_i[0], lam[0])
    a_b, u_b = rglru_coeffs(xc, w_r[1], b_r[1], w_i[1], b_i[1], lam[1])
    h_f, hl_f = linear_scan(a_f, u_f, h0_fwd.astype(jnp.float32), False)
    h_b, hl_b = linear_scan(a_b, u_b, h0_bwd.astype(jnp.float32), True)
    y = (h_f + h_b).astype(xb.dtype) * jax.nn.gelu(gb)
    return y, hl_f, hl_b


def context_attention(q, k, v):
    bsz, s, h, dh = q.shape
    nq = s // CTX_Q_BLOCK
    qb = jnp.moveaxis(q.reshape(bsz, nq, CTX_Q_BLOCK, h, dh), 1, 0)

    def one_block(qi):
        sc = jnp.einsum('bqhd,bkhd->bhqk', qi, k).astype(jnp.float32)
        p = jax.nn.softmax(sc, axis=-1).astype(v.dtype)
        return jnp.einsum('bhqk,bkhd->bqhd', p, v)

    o = lax.map(one_block, qb)
    return jnp.moveaxis(o, 0, 1).reshape(bsz, s, h, dh)


def neighbourhood_attention(q, k, v, k_ctx, v_ctx, rpb):
    bsz, t, h, dh = q.shape
    rows = t // GRID_W
    kh = min(WIN_H, rows)
    r = jnp.arange(rows)
    row_start = jnp.clip(r - kh // 2, 0, rows - kh)
    key_rows = row_start[:, None] + jnp.arange(kh)[None, :]
    cq = jnp.arange(GRID_W)
    col_start = jnp.clip(cq - WIN_W // 2, 0, GRID_W - WIN_W)
    col_in = (cq[None, :] >= col_start[:, None]) & (cq[None, :] < col_start[:, None] + WIN_W)
    dr_idx = key_rows - r[:, None] + (WIN_H - 1)
    dc_idx = jnp.clip(cq[None, :] - cq[:, None] + (WIN_W - 1), 0, 2 * WIN_W - 2)
    bias = rpb[:, dr_idx[:, None, :, None], dc_idx[None, :, None, :]].astype(jnp.float32)
    bias = jnp.where(col_in[None, None, :, None, :], bias, -jnp.inf)

    qg = q.reshape(bsz, rows, GRID_W, h, dh)
    kg = k.reshape(bsz, rows, GRID_W, h, dh)[:, key_rows]
    vg = v.reshape(bsz, rows, GRID_W, h, dh)[:, key_rows]
    s_loc = jnp.einsum('brqhd,brkchd->bhrqkc', qg, kg).astype(jnp.float32) + bias[None]
    s_ctx = jnp.einsum('brqhd,bphd->bhrqp', qg, k_ctx).astype(jnp.float32)
    n_loc = kh * GRID_W
    s_all = jnp.concatenate([s_loc.reshape(bsz, h, rows, GRID_W, n_loc), s_ctx], axis=-1)
    p = jax.nn.softmax(s_all, axis=-1).astype(v.dtype)
    p_loc = p[..., :n_loc].reshape(bsz, h, rows, GRID_W, kh, GRID_W)
    p_ctx = p[..., n_loc:]
    o = (jnp.einsum('bhrqkc,brkchd->brqhd', p_loc, vg)
         + jnp.einsum('bhrqp,bphd->brqhd', p_ctx, v_ctx))
    return o.reshape(bsz, t, h, dh)


def mix_ab(h, w_in, q_g, k_g, rpb, conv_w, conv_b, w_r, b_r, w_i, b_i, lam, w_out, past):
    bsz, t, _ = h.shape
    proj = h @ w_in
    q, k, v, xb, gb = jnp.split(
        proj, [NA_WIDTH, 2 * NA_WIDTH, 3 * NA_WIDTH, 3 * NA_WIDTH + LRU_WIDTH], axis=-1)
    q = rms_norm(q.reshape(bsz, t, NA_HEADS, HEAD_DIM), q_g) * (HEAD_DIM ** -0.5)
    k = rms_norm(k.reshape(bsz, t, NA_HEADS, HEAD_DIM), k_g)
    v = v.reshape(bsz, t, NA_HEADS, HEAD_DIM)
    if past is None:
        o = context_attention(q, k, v)
        zeros = jnp.zeros((bsz, LRU_WIDTH), jnp.float32)
        y_b, hl_f, hl_b = rglru_bidir(xb, gb, conv_w, conv_b, w_r, b_r, w_i, b_i, lam, zeros, zeros)
        new = (k, v, hl_f.astype(h.dtype), hl_b.astype(h.dtype))
    else:
        k_ctx, v_ctx, h0_f, h0_b = past
        o = neighbourhood_attention(q, k, v, k_ctx, v_ctx, rpb)
        y_b, _, _ = rglru_bidir(xb, gb, conv_w, conv_b, w_r, b_r, w_i, b_i, lam, h0_f, h0_b)
        new = None
    y = jnp.concatenate([o.reshape(bsz, t, NA_WIDTH), y_b], axis=-1) @ w_out
    return y, new


def fourier_mix(h, w_out):
    bsz, t, _ = h.shape
    hg = h.astype(jnp.float32).reshape(bsz, t, FOURIER_GROUPS, D_MODEL // FOURIER_GROUPS)
    f = jnp.fft.fft2(hg, axes=(1, 3), norm='ortho').real
    return f.reshape(bsz, t, D_MODEL).astype(h.dtype) @ w_out


def trunk(x, cond, prm, past):
    s_cond = jax.nn.silu(cond)
    new_k, new_v, new_hf, new_hb = [], [], [], []
    for l in range(DEPTH):
        mod = (s_cond @ prm['w_ada'][l] + prm['b_ada'][l]).reshape(-1, N_MOD, D_MODEL)
        g = prm['norm_g'][l]
        hh = modulate(rms_norm(x, g[0]), mod[:, 0], mod[:, 1])
        x = x + 0.5 * mod[:, 2][:, None, :] * swiglu(hh, prm['ffn1_gate'][l], prm['ffn1_up'][l], prm['ffn1_down'][l])
        hh = modulate(rms_norm(x, g[1]), mod[:, 3], mod[:, 4])
        if l % 2 == 0:
            e = l // 2
            layer_past = None if past is None else (past[0][:, e], past[1][:, e], past[2][:, e], past[3][:, e])
            y, new = mix_ab(hh, prm['w_in'][e], prm['q_norm_g'][e], prm['k_norm_g'][e], prm['rpb'][e],
                            prm['conv_w'][e], prm['conv_b'][e], prm['lru_w_r'][e], prm['lru_b_r'][e],
                            prm['lru_w_i'][e], prm['lru_b_i'][e], prm['lru_lambda'][e], prm['w_out_ab'][e],
                            layer_past)
            if new is not None:
                new_k.append(new[0])
                new_v.append(new[1])
                new_hf.append(new[2])
                new_hb.append(new[3])
        else:
            y = fourier_mix(hh, prm['w_out_c'][l // 2])
        x = x + mod[:, 5][:, None, :] * y
        hh = modulate(rms_norm(x, g[2]), mod[:, 6], mod[:, 7])
        x = x + 0.5 * mod[:, 8][:, None, :] * swiglu(hh, prm['ffn2_gate'][l], prm['ffn2_up'][l], prm['ffn2_down'][l])
    if past is None:
        states = (jnp.stack(new_k, axis=1), jnp.stack(new_v, axis=1),
                  jnp.stack(new_hf, axis=1), jnp.stack(new_hb, axis=1))
    else:
        states = None
    return x, states


def setup_inputs(seed: int = 0) -> dict:
    key = jax.random.key(seed)
    ks = jax.random.split(key, 32)

    def nrm(k, shape, s):
        return jax.random.normal(k, shape, jnp.float32) * s

    a0 = jax.random.uniform(ks[28], (N_EVEN, 2, LRU_WIDTH), jnp.float32, 0.9, 0.999)
    sig = a0 ** (1.0 / LRU_C)
    lru_lambda = jnp.log(sig) - jnp.log1p(-sig)
    return {
        'x_prompt': nrm(ks[0], (BATCH, SEQ, D_MODEL), 1.0),
        'x_sample': nrm(ks[1], (DEC_BATCH, DEC_SEQ, D_MODEL), 1.0),
        'cache_k': nrm(ks[2], (DEC_BATCH, N_EVEN, PAST_LEN, NA_HEADS, HEAD_DIM), 1.0),
        'cache_v': nrm(ks[3], (DEC_BATCH, N_EVEN, PAST_LEN, NA_HEADS, HEAD_DIM), 1.0),
        'state_lru_fwd': nrm(ks[4], (DEC_BATCH, N_EVEN, LRU_WIDTH), 0.5),
        'state_lru_bwd': nrm(ks[5], (DEC_BATCH, N_EVEN, LRU_WIDTH), 0.5),
        'c': nrm(ks[6], (DEC_BATCH, D_MODEL), 1.0),
        'c_ctx': nrm(ks[7], (D_MODEL,), 1.0),
        'w_ada': nrm(ks[8], (DEPTH, D_MODEL, N_MOD * D_MODEL), 0.5 * D_MODEL ** -0.5),
        'b_ada': nrm(ks[9], (DEPTH, N_MOD * D_MODEL), 0.01),
        'norm_g': 1.0 + nrm(ks[10], (DEPTH, 3, D_MODEL), 0.02),
        'ffn1_gate': nrm(ks[11], (DEPTH, D_MODEL, D_FF), D_MODEL ** -0.5),
        'ffn1_up': nrm(ks[12], (DEPTH, D_MODEL, D_FF), D_MODEL ** -0.5),
        'ffn1_down': nrm(ks[13], (DEPTH, D_FF, D_MODEL), D_FF ** -0.5),
        'ffn2_gate': nrm(ks[14], (DEPTH, D_MODEL, D_FF), D_MODEL ** -0.5),
        'ffn2_up': nrm(ks[15], (DEPTH, D_MODEL, D_FF), D_MODEL ** -0.5),
        'ffn2_down': nrm(ks[16], (DEPTH, D_FF, D_MODEL), D_FF ** -0.5),
        'w_in': nrm(ks[17], (N_EVEN, D_MODEL, IN_WIDTH), D_MODEL ** -0.5),
        'q_norm_g': 1.0 + nrm(ks[18], (N_EVEN, HEAD_DIM), 0.02),
        'k_norm_g': 1.0 + nrm(ks[19], (N_EVEN, HEAD_DIM), 0.02),
        'rpb': nrm(ks[20], (N_EVEN, NA_HEADS, 2 * WIN_H - 1, 2 * WIN_W - 1), 0.1),
        'conv_w': nrm(ks[21], (N_EVEN, CONV_W, LRU_WIDTH), CONV_W ** -0.5),
        'conv_b': nrm(ks[22], (N_EVEN, LRU_WIDTH), 0.01),
        'lru_w_r': nrm(ks[23], (N_EVEN, 2, LRU_BLOCKS, LRU_BLOCK, LRU_BLOCK), LRU_BLOCK ** -0.5),
        'lru_b_r': nrm(ks[24], (N_EVEN, 2, LRU_WIDTH), 0.01),
        'lru_w_i': nrm(ks[25], (N_EVEN, 2, LRU_BLOCKS, LRU_BLOCK, LRU_BLOCK), LRU_BLOCK ** -0.5),
        'lru_b_i': nrm(ks[26], (N_EVEN, 2, LRU_WIDTH), 0.01),
        'lru_lambda': lru_lambda,
        'w_out_ab': nrm(ks[29], (N_EVEN, NA_WIDTH + LRU_WIDTH, D_MODEL), (NA_WIDTH + LRU_WIDTH) ** -0.5),
        'w_out_c': nrm(ks[30], (N_ODD, D_MODEL, D_MODEL), D_MODEL ** -0.5),
    }


def reference(x_prompt, x_sample, cache_k, cache_v, state_lru_fwd, state_lru_bwd, c, c_ctx,
              w_ada, b_ada, norm_g, ffn1_gate, ffn1_up, ffn1_down, ffn2_gate, ffn2_up, ffn2_down,
              w_in, q_norm_g, k_norm_g, rpb, conv_w, conv_b, lru_w_r, lru_b_r, lru_w_i, lru_b_i,
              lru_lambda, w_out_ab, w_out_c):
    prm = {
        'w_ada': w_ada, 'b_ada': b_ada, 'norm_g': norm_g,
        'ffn1_gate': ffn1_gate, 'ffn1_up': ffn1_up, 'ffn1_down': ffn1_down,
        'ffn2_gate': ffn2_gate, 'ffn2_up': ffn2_up, 'ffn2_down': ffn2_down,
        'w_in': w_in, 'q_norm_g': q_norm_g, 'k_norm_g': k_norm_g, 'rpb': rpb,
        'conv_w': conv_w, 'conv_b': conv_b, 'lru_w_r': lru_w_r, 'lru_b_r': lru_b_r,
        'lru_w_i': lru_w_i, 'lru_b_i': lru_b_i, 'lru_lambda': lru_lambda,
        'w_out_ab': w_out_ab, 'w_out_c': w_out_c,
    }
    y_prompt, ctx_states = trunk(x_prompt, c_ctx[None, :], prm, None)
    new_cache_k, new_cache_v, new_state_lru_fwd, new_state_lru_bwd = ctx_states
    y_sample, _ = trunk(x_sample, c, prm, (cache_k, cache_v, state_lru_fwd, state_lru_bwd))
    return (y_prompt, y_sample, new_cache_k, new_cache_v, new_state_lru_fwd, new_state_lru_bwd)
```

```python
import contextlib
import os
import numpy as np
import ml_dtypes
import concourse.bass as bass
import concourse.mybir as mybir
from concourse.bass_utils import run_bass_kernel_spmd

F32 = mybir.dt.float32
BF16 = mybir.dt.bfloat16
AF = mybir.ActivationFunctionType
ALU = mybir.AluOpType

N_DMA_SLOTS = 6


class Prog:
    COMPUTE = ("pe", "act", "dve", "pool")

    def __init__(self, nc):
        self.nc = nc
        self.streams = {e: [] for e in ("pe", "act", "dve", "pool", "sp")}
        self.cnt = {}
        self.seen = {e: {} for e in self.streams}
        self.state = {}
        self.dma_i = {"sp": 0, "pool": 0}
        self.sem_names = list(self.COMPUTE) + [f"{q}d{i}" for q in ("sp", "pool") for i in range(N_DMA_SLOTS)]
        for s in self.sem_names:
            self.cnt[s] = 0
        self.sems = {}
        self.nops = 0

    def _need(self, stream, waits, sem, val):
        if sem == "pe" and stream == "pe":
            return
        if self.seen[stream].get(sem, 0) >= val:
            return
        assert val <= self.cnt[sem], ("dependency on a silent op whose carrier is not issued yet", stream, sem, val)
        self.seen[stream][sem] = val
        waits[sem] = max(waits.get(sem, 0), val)

    def _deps(self, stream, reads, writes):
        waits = {}
        for k in reads:
            st = self.state.get(k)
            if st and st[0]:
                self._need(stream, waits, *st[0])
        for k in writes:
            st = self.state.get(k)
            if st:
                if st[0]:
                    self._need(stream, waits, *st[0])
                for s, v in st[1].items():
                    self._need(stream, waits, s, v)
        return waits

    def _commit(self, sem, val, reads, writes):
        for k in reads:
            st = self.state.setdefault(k, [None, {}])
            st[1][sem] = max(st[1].get(sem, 0), val)
        for k in writes:
            self.state[k] = [(sem, val), {}]

    def op(self, eng, fn, reads=(), writes=(), silent=False):
        waits = self._deps(eng, reads, writes)
        if silent:
            val = self.cnt[eng] + 1
        else:
            self.cnt[eng] += 1
            val = self.cnt[eng]
        self._commit(eng, val, reads, writes)
        self.streams[eng].append((waits, fn, eng, 0 if silent else 1))
        self.nops += 1

    def dma(self, q, fn, reads=(), writes=()):
        i = self.dma_i[q]
        self.dma_i[q] += 1
        sem = f"{q}d{i % N_DMA_SLOTS}"
        waits = self._deps(q, reads, writes)
        if self.cnt[sem] > 0:
            self._need(q, waits, sem, self.cnt[sem])
        self.cnt[sem] += 16
        val = self.cnt[sem]
        self._commit(sem, val, reads, writes)
        self.streams[q].append((waits, fn, sem, 16))
        self.nops += 1

    def finish(self):
        waits = {}
        for s in self.sem_names:
            if self.cnt[s] > 0:
                self._need("sp", waits, s, self.cnt[s])
        self.streams["sp"].append((waits, None, None, 0))

    def emit(self):
        nc = self.nc
        with contextlib.ExitStack() as es:
            for s in self.sem_names:
                self.sems[s] = es.enter_context(nc.semaphore(s))
            blk = es.enter_context(nc.Block())

            def run(stream):
                def body(e):
                    for waits, fn, sem, inc in self.streams[stream]:
                        for s, v in waits.items():
                            e.wait_ge(self.sems[s], v)
                        if fn is not None:
                            ins = fn(e)
                            if inc:
                                ins.then_inc(self.sems[sem], inc)
                return body

            blk.tensor(run("pe"))
            blk.scalar(run("act"))
            blk.vector(run("dve"))
            blk.gpsimd(run("pool"))
            blk.sync(run("sp"))


D = 1024
T = 1536
NT = 3
DFF = 2816
NF = 22
SEQS = ((0, 256, 0), (256, 256, 0), (512, 1024, 1))
EPS = 1e-6
HALVES = ((0, 12), (12, 10))


def _l(fn, *a, **k):
    return lambda e: fn(e, *a, **k)


class Gen:
    def __init__(self, debug=0):
        self.debug = debug
        self.nc = nc = bass.Bass("TRN2", target_bir_lowering=False)
        self.P = Prog(nc)
        self.es = contextlib.ExitStack()
        self.din = {}
        self.dout = {}

    def i(self, name, shape, dt=F32):
        ap = self.nc.dram_tensor(name, list(shape), dt, kind="ExternalInput").ap()
        self.din[name] = ap
        return ap

    def o(self, name, shape):
        ap = self.nc.dram_tensor(name, list(shape), F32, kind="ExternalOutput").ap()
        self.dout[name] = ap
        return ap

    def sb(self, name, shape, dt):
        return self.es.enter_context(self.nc.sbuf_tensor(name, list(shape), dt))

    def mm(self, out, lhsT, rhs, start, stop, r, w, grp=True):
        self.P.op("pe", lambda e: e.matmul(out, lhsT, rhs, start=start, stop=stop), r, w, silent=(grp and not stop))

    def tr(self, out, in_, ident, r, w):
        self.P.op("pe", lambda e: e.transpose(out, in_, ident), r, w)

    def act(self, out, in_, func, r, w, bias=None, scale=None):
        kw = {}
        if bias is not None:
            kw["bias"] = bias
        if scale is not None:
            kw["scale"] = scale
        self.P.op("act", lambda e: e.activation(out=out, in_=in_, func=func, **kw), r, w)

    def tt(self, out, in0, in1, op, r, w, eng="dve"):
        self.P.op(eng, lambda e: e.tensor_tensor(out=out, in0=in0, in1=in1, op=op), r, w)

    def ts(self, out, in0, s1, s2, op0, op1, r, w, eng="dve"):
        if s2 is None:
            self.P.op(eng, lambda e: e.tensor_scalar(out=out, in0=in0, scalar1=s1, scalar2=None, op0=op0), r, w)
        else:
            self.P.op(eng, lambda e: e.tensor_scalar(out=out, in0=in0, scalar1=s1, scalar2=s2, op0=op0, op1=op1), r, w)

    def stt(self, out, in0, scalar, in1, op0, op1, r, w, eng="dve"):
        self.P.op(eng, lambda e: e.scalar_tensor_tensor(out=out, in0=in0, scalar=scalar, in1=in1, op0=op0, op1=op1), r, w)

    def cp(self, out, in_, r, w, eng="dve"):
        if eng == "act":
            self.P.op("act", lambda e: e.copy(out=out, in_=in_), r, w)
        else:
            self.P.op(eng, lambda e: e.tensor_copy(out=out, in_=in_), r, w)

    def ms(self, ap, val, w, eng="dve"):
        self.P.op(eng, lambda e: e.memset(ap, val), (), w)

    def dma(self, q, out, in_, r, w, nc_ok=False):
        if nc_ok:
            self.P.dma(q, lambda e: e.dma_start(out=out, in_=in_, allow_slow_non_contiguous=True), r, w)
        else:
            self.P.dma(q, lambda e: e.dma_start(out=out, in_=in_), r, w)


def kk(name, cs, t0, t1):
    return [(name, c, b) for c in cs for b in range(t0 // 256, (t1 - 1) // 256 + 1)]


def build(debug=0, stage=99):
    g = Gen(debug)
    nc, P = g.nc, g.P
    xin = g.i("xin", [T, D])
    cond = g.i("cond", [2, D])
    ck_d = g.i("ck", [256, 512])
    cv_d = g.i("cv", [256, 512])
    stf_d = g.i("stf", [512])
    stb_d = g.i("stb", [512])
    w_ada = g.i("w_ada", [2, D, 9 * D])
    b_ada = g.i("b_ada", [2, 9 * D])
    norm_g = g.i("norm_g", [2, 3, D])
    fw = {}
    for nm in ("ffn1_gate", "ffn1_up", "ffn2_gate", "ffn2_up"):
        fw[nm] = g.i(nm, [2, D, DFF])
    for nm in ("ffn1_down", "ffn2_down"):
        fw[nm] = g.i(nm, [2, DFF, D])
    w_in = g.i("w_in", [1, D, 2560])
    qg_d = g.i("q_norm_g", [1, 64])
    kg_d = g.i("k_norm_g", [1, 64])
    btab_d = g.i("btab", [8, 128, 2, 14 * 64])
    conv_w = g.i("conv_w", [1, 4, 512])
    conv_b = g.i("conv_b", [1, 512])
    lru_w_r = g.i("lru_w_r", [1, 2, 8, 64, 64])
    lru_b_r = g.i("lru_b_r", [1, 2, 512])
    lru_w_i = g.i("lru_w_i", [1, 2, 8, 64, 64])
    lru_b_i = g.i("lru_b_i", [1, 2, 512])
    lru_lam = g.i("lru_lambda", [1, 2, 512])
    w_out_ab = g.i("w_out_ab", [1, D, D])
    w_out_c = g.i("w_out_c", [1, D, D])
    ident_d = g.i("ident", [128, 128])
    cs1_d = g.i("cs1", [256, 512])
    ct1k_d = g.i("ct1k", [1024, 1024])
    st1k_d = g.i("st1k", [1024, 1024])
    ct256_d = g.i("ct256", [256, 256])
    st256_d = g.i("st256", [256, 256])

    yout = g.o("yout", [T, D])
    nk_o = g.o("nk", [512, 512])
    nv_o = g.o("nv", [512, 512])
    nst_o = g.o("nst", [16, 128])
    if debug:
        dbg_o = g.o("dbg", [debug, 128, 8, T])

    X = g.sb("X", [128, 8, T], F32)
    H = g.sb("H", [128, 8, T], BF16)
    SCR = g.sb("SCR", [128, 38 * 1024], BF16)
    WG = [g.sb(f"WG{i}", [128, 16, 256], BF16) for i in range(2)]
    WD = [g.sb(f"WD{i}", [128, 12, 256], BF16) for i in range(2)]
    IDENT = g.sb("IDENT", [128, 128], F32)
    ONESB = g.sb("ONESB", [128, 128], BF16)
    BLK = g.sb("BLK", [128, 128], BF16)
    PAR = g.sb("PAR", [128, 384], F32)
    SC = g.sb("SC", [128, 8, 2], BF16)
    MOD = g.sb("MOD", [128, 2, 9, 8, 2], F32)
    COEF = g.sb("COEF", [128, 2, 3, 3, 8, 2], F32)
    SM = g.sb("SM", [128, 64], F32)
    ST = g.sb("ST", [128, 16], F32)
    BD = g.sb("BD", [128, 16, 128], BF16)
    PS = [g.es.enter_context(nc.psum_tensor(f"ps{i}", [128, 512], F32)) for i in range(8)]

    class Scr:
        def __init__(self, off_kib, shape, dt):
            self.off = int(off_kib * 1024)
            self.shape = list(shape)
            self.eb = 4 if dt == F32 else 2
            n = int(np.prod(shape))
            e0 = self.off // 2
            if dt == F32:
                v = SCR[:, e0: e0 + 2 * n].bitcast(F32)
            else:
                v = SCR[:, e0: e0 + n]
            if len(shape) > 1:
                names = " ".join(f"d{i}" for i in range(len(shape)))
                kw = {f"d{i}": s for i, s in enumerate(shape[:-1])}
                v = v.rearrange(f"p ({names}) -> p {names}", **kw)
            self.v = v

        def k(self, *idx, e0=0, e1=None):
            rest = int(np.prod(self.shape[len(idx):])) if len(idx) < len(self.shape) else 1
            base = 0
            for i, ix in enumerate(idx):
                base = base * self.shape[i] + ix
            base *= rest
            if e1 is None:
                e1 = rest
            b0 = self.off + (base + e0) * self.eb
            b1 = self.off + (base + e1) * self.eb
            return [("SCR", b) for b in range(b0 // 1024, (b1 - 1) // 1024 + 1)]

    def scr(off_kib, shape, dt):
        return Scr(off_kib, shape, dt)

    psk = lambda i: [("ps", i)]

    g.dma("sp", IDENT[:], ident_d, (), ["IDENT"])
    g.ms(ONESB[:], 1.0, ["ONESB"])
    g.ms(BLK[:], 0.0, ["BLK"])
    g.ms(BLK[0:64, 0:64], 1.0, ["BLK"])
    g.ms(BLK[64:128, 64:128], 1.0, ["BLK"])
    g.ms(BD[:], 0.0, ["BD"])
    g.ms(ST[:], 0.0, ["ST"])

    PSTs = scr(0, [3, 128], F32)
    PST = PSTs.v
    PSTk = PSTs.k()
    g.ms(PST, 0.0, PSTk)
    rows = {}
    r0 = [0]

    def prow(name, ap, n, tile=0):
        base = r0[0] if tile == 0 else 0
        g.dma("sp", PST[base:base + n, tile, :], ap, (), PSTk)
        rows[name] = tile * 128 + base
        if tile == 0:
            r0[0] += n

    prow("cond", cond.rearrange("n (c p) -> (n c) p", p=128), 16)
    prow("norm_g", norm_g.rearrange("l i (c p) -> (l i c) p", p=128), 48)
    prow("conv_w", conv_w[0].rearrange("j (c p) -> (j c) p", p=128), 16)
    prow("conv_b", conv_b[0].rearrange("(c p) -> c p", p=128), 4)
    prow("b_r", lru_b_r[0].rearrange("d (c p) -> (d c) p", p=128), 8)
    prow("b_i", lru_b_i[0].rearrange("d (c p) -> (d c) p", p=128), 8)
    prow("lam", lru_lam[0].rearrange("d (c p) -> (d c) p", p=128), 8)
    prow("stf", stf_d.rearrange("(c p) -> c p", p=128), 4)
    prow("stb", stb_d.rearrange("(c p) -> c p", p=128), 4)
    prow("b_ada0", b_ada[0].rearrange("(m p) -> m p", p=128), 72, tile=1)
    prow("b_ada1", b_ada[1].rearrange("(m p) -> m p", p=128), 72, tile=2)
    for t3 in range(3):
        g.tr(PS[7][:, t3 * 128:(t3 + 1) * 128], PST[:, t3, :], IDENT[:], PSTk + ["IDENT"], psk(7))
    g.cp(PAR[:], PS[7][:, 0:384], psk(7), ["PAR"])
    par = lambda name, k=0: PAR[:, rows[name] + k: rows[name] + k + 1]
    parn = lambda name, k, n: PAR[:, rows[name] + k: rows[name] + k + n]

    for hlf in range(2):
        g.dma("sp", SM[hlf * 64:(hlf + 1) * 64, 0:1], qg_d.rearrange("o d -> d o"), (), ["SM"], nc_ok=True)
        g.dma("sp", SM[hlf * 64:(hlf + 1) * 64, 1:2], kg_d.rearrange("o d -> d o"), (), ["SM"], nc_ok=True)
    g.ts(SM[:, 0:1], SM[:, 0:1], 0.125, None, ALU.mult, None, ["SM"], ["SM"])
    g.act(SM[:, 16:24], parn("lam", 0, 8), AF.Exp, ["PAR"], ["SM"], scale=-1.0)
    g.ts(SM[:, 24:32], SM[:, 16:24], 1.0, None, ALU.add, None, ["SM"], ["SM"])
    g.act(SM[:, 32:40], SM[:, 24:32], AF.Ln, ["SM"], ["SM"])
    g.ts(SM[:, 24:32], SM[:, 24:32], -1.0, 1e-30, ALU.add, ALU.max, ["SM"], ["SM"])
    g.P.op("dve", lambda e: e.reciprocal(out=SM[:, 24:32], in_=SM[:, 24:32]), ["SM"], ["SM"])
    g.tt(SM[:, 16:24], SM[:, 16:24], SM[:, 24:32], ALU.mult, ["SM"], ["SM"])
    g.tt(SM[:, 16:24], SM[:, 16:24], SM[:, 32:40], ALU.mult, ["SM"], ["SM"])
    g.ts(SM[:, 8:16], SM[:, 16:24], -8.0, None, ALU.mult, None, ["SM"], ["SM"])
    C1 = lambda d, c: SM[:, 8 + d * 4 + c: 9 + d * 4 + c]
    g.ts(SM[:, 48:56], SM[:, 8:16], 2.0, None, ALU.mult, None, ["SM"], ["SM"])
    C2 = lambda d, c: SM[:, 48 + d * 4 + c: 49 + d * 4 + c]

    for n in range(2):
        g.act(SC[:, :, n], parn("cond", n * 8, 8), AF.Silu, ["PAR"], ["SC"])

    XS = [scr(o_, [1024], F32) for o_ in (8, 12, 60, 64, 68, 72)]
    for tt_ in range(12):
        xs = XS[tt_ % 6].v
        xsk = XS[tt_ % 6].k()
        g.dma("sp", xs, xin[tt_ * 128:(tt_ + 1) * 128, :], (), xsk)
        for cg in range(2):
            pb = 4 + (tt_ * 2 + cg) % 4
            for ci in range(4):
                c = cg * 4 + ci
                g.tr(PS[pb][:, ci * 128:(ci + 1) * 128], xs[:, c * 128:(c + 1) * 128], IDENT[:],
                     xsk + ["IDENT"], psk(pb))
            g.cp(X[:, cg * 4:(cg + 1) * 4, tt_ * 128:(tt_ + 1) * 128],
                 PS[pb][:].rearrange("p (c t) -> p c t", c=4), psk(pb),
                 kk("X", range(cg * 4, cg * 4 + 4), tt_ * 128, tt_ * 128 + 128), eng=("dve" if cg else "act"))

    dbg_n = [0]

    def dbg_dump():
        if debug and dbg_n[0] < debug:
            g.dma("sp", dbg_o[dbg_n[0]], X[:], kk("X", range(8), 0, T), ())
            dbg_n[0] += 1

    wg_i = [0]
    wd_i = [0]

    def wg_next():
        i = wg_i[0] % 2
        wg_i[0] += 1
        return i

    def wd_next():
        i = wd_i[0] % 2
        wd_i[0] += 1
        return i

    def load_slab(dram2d, c0, ncols):
        i = wg_next()
        v = WG[i][:].rearrange("p a b -> p (a b)")[:, 0:8 * ncols].rearrange("p (k n) -> p k n", k=8)
        g.dma("pool", v, dram2d[:, c0:c0 + ncols].rearrange("(k p) n -> p k n", p=128), (), [("WG", i)])
        return v, ("WG", i)

    WA = [scr(60 + 8 * i, [8, 512], BF16) for i in range(2)]
    WAX = [scr(16 + 8 * i, [8, 512], BF16) for i in range(6)]

    def ada_gen(l):
        for s in range(18):
            wa = WAX[s] if (l == 0 and s < 6) else WA[s % 2]
            g.dma("pool", wa.v, w_ada[l][:, s * 512:(s + 1) * 512].rearrange("(k p) n -> p k n", p=128), (), wa.k())
            pb = 6 + s % 2
            for cc in range(4):
                for k in range(8):
                    g.mm(PS[pb][:, cc * 2:cc * 2 + 2], wa.v[:, k, cc * 128:(cc + 1) * 128], SC[:, k, :],
                         k == 0, k == 7, wa.k() + ["SC"], psk(pb))
            j, c0 = s // 2, (s % 2) * 4
            for n in range(2):
                g.tt(MOD[:, l, j, c0:c0 + 4, n], PS[pb][:, 0:8].rearrange("p (c n) -> p c n", n=2)[:, :, n],
                     parn(f"b_ada{l}", j * 8 + c0, 4), ALU.add, psk(pb) + ["PAR"], [("MOD", l, j)])
            if s % 6 == 5:
                sub = s // 6
                mk = [("MOD", l, 3 * sub + q) for q in range(3)]
                for n in range(2):
                    gn = parn("norm_g", l * 24 + sub * 8, 8)
                    g.stt(COEF[:, l, sub, 0, :, n], MOD[:, l, 3 * sub + 1, :, n], 1.0, gn, ALU.add, ALU.mult,
                          mk + ["PAR"], [("COEF", l, sub)])
                    g.cp(COEF[:, l, sub, 1, :, n], MOD[:, l, 3 * sub + 0, :, n], mk, [("COEF", l, sub)])
                    g.ts(COEF[:, l, sub, 2, :, n], MOD[:, l, 3 * sub + 2, :, n], (1.0 if sub == 1 else 0.5), None,
                         ALU.mult, None, mk, [("COEF", l, sub)])
            yield

    cf = lambda l, sub, which, c, n: COEF[:, l, sub, which, c, n:n + 1]


    def norm_mod(l, sub, presum=False):
        for t in range(NT):
            norm_tile(l, sub, t, presum=presum)

    def norm_tile(l, sub, t, base=0, sq_on_act=False, presum=False):
        SQ = scr(base, [8, 512], BF16)
        RS = [scr(base + 8 + 2 * i, [512], F32) for i in range(2)]
        TMP = [scr(base + 12 + 2 * i, [512], F32) for i in range(4)]
        if True:
            n = 0 if t == 0 else 1
            tok = slice(t * 512, (t + 1) * 512)
            for c in (() if presum else range(8)):
                xk = kk("X", [c], t * 512, t * 512 + 512)
                if sq_on_act:
                    g.act(SQ.v[:, c, :], X[:, c, tok], AF.Square, xk, SQ.k(c))
                else:
                    g.tt(SQ.v[:, c, :], X[:, c, tok], X[:, c, tok], ALU.mult, xk, SQ.k(c))
            pb = t if presum else 6 + t % 2
            for c in (() if presum else range(8)):
                g.mm(PS[pb][:], ONESB[:], SQ.v[:, c, :], c == 0, c == 7, ["ONESB"] + SQ.k(c), psk(pb))
            rs = RS[t % 2].v
            rsk = RS[t % 2].k()
            g.act(rs, PS[pb][:], AF.Ln, psk(pb) + ["SM"], rsk, bias=SM[:, 40:41], scale=1.0 / D)
            g.act(rs, rs, AF.Exp, rsk, rsk, scale=-0.5)
            for c in range(8):
                tm = TMP[c % 4].v
                tmk = TMP[c % 4].k()
                g.stt(tm, X[:, c, tok], cf(l, sub, 0, c, n), rs, ALU.mult, ALU.mult,
                      kk("X", [c], t * 512, t * 512 + 512) + rsk + [("COEF", l, sub)], tmk)
                g.act(H[:, c, tok], tm, AF.Identity, tmk + [("COEF", l, sub)], kk("H", [c], t * 512, t * 512 + 512),
                      bias=cf(l, sub, 1, c, n))

    g.ms(SM[:, 40:41], EPS, ["SM"])

    def ffn(l, sub, wgate, wup, wdown, bg=None, final=False, next_norm=False):
        HID = scr(20, [12, T], BF16)
        SG = [scr(56 + 2 * i, [512], F32) for i in range(2)]
        pair = [0]
        dbank = [0]
        fin_i = [0]
        YS = [scr(60 + 2 * i, [512], F32) for i in range(4)]
        SQN = [scr(i, [512], BF16) for i in range(4)]
        pend_sq = []

        pend_out = []

        def flush_sq():
            while pend_sq:
                t_, d_, sq_ = pend_sq.pop(0)
                g.mm(PS[t_][:], ONESB[:], sq_.v, d_ == 0, d_ == 7, ["ONESB"] + sq_.k(), psk(t_), grp=False)
            while pend_out:
                pend_out.pop(0)()

        for (f0, nf) in HALVES:
            for fg in range(nf // 2):
                i = wg_next()
                fa = f0 + fg * 2
                g.dma("pool", WG[i][:, 0:8, :], wgate[l][:, fa * 128:(fa + 2) * 128].rearrange("(k p) n -> p k n", p=128),
                      (), [("WG", i)])
                g.dma("pool", WG[i][:, 8:16, :], wup[l][:, fa * 128:(fa + 2) * 128].rearrange("(k p) n -> p k n", p=128),
                      (), [("WG", i)])
                for fi in range(2):
                    fl = fg * 2 + fi
                    for t in range(NT):
                        pg = (pair[0] % 2) * 2
                        pair[0] += 1
                        hk = kk("H", range(8), t * 512, t * 512 + 512)
                        for k in range(8):
                            g.mm(PS[pg][:], WG[i][:, k, fi * 128:(fi + 1) * 128], H[:, k, t * 512:(t + 1) * 512],
                                 k == 0, k == 7, [("WG", i)] + hk, psk(pg))
                        for k in range(8):
                            g.mm(PS[pg + 1][:], WG[i][:, 8 + k, fi * 128:(fi + 1) * 128], H[:, k, t * 512:(t + 1) * 512],
                                 k == 0, k == 7, [("WG", i)] + hk, psk(pg + 1))
                        sg = SG[(pair[0]) % 2].v
                        sgk = SG[(pair[0]) % 2].k()
                        g.act(sg, PS[pg][:], AF.Silu, psk(pg), sgk)
                        g.tt(HID.v[:, fl, t * 512:(t + 1) * 512], sg, PS[pg + 1][:], ALU.mult,
                             sgk + psk(pg + 1), HID.k(fl, e0=t * 512, e1=t * 512 + 512))
                if bg is not None:
                    next(bg, None)
            for dg in range(4):
                i = wd_next()
                g.dma("pool", WD[i][:, 0:nf, :],
                      wdown[l][f0 * 128:(f0 + nf) * 128, dg * 256:(dg + 1) * 256].rearrange("(f p) n -> p f n", p=128),
                      (), [("WD", i, s_) for s_ in range(3)])
                for di in range(2):
                    d = dg * 2 + di
                    for t in range(NT):
                        n = 0 if t == 0 else 1
                        pb = 4 + dbank[0] % 2
                        dbank[0] += 1
                        for f in range(nf):
                            g.mm(PS[pb][:], WD[i][:, f, di * 128:(di + 1) * 128], HID.v[:, f, t * 512:(t + 1) * 512],
                                 f == 0, f == nf - 1, [("WD", i, s_) for s_ in range(3)] + HID.k(f, e0=t * 512, e1=t * 512 + 512), psk(pb))
                        flush_sq()
                        xk = kk("X", [d], t * 512, t * 512 + 512)
                        g.stt(X[:, d, t * 512:(t + 1) * 512], PS[pb][:], cf(l, sub, 2, d, n), X[:, d, t * 512:(t + 1) * 512],
                              ALU.mult, ALU.add, psk(pb) + xk + [("COEF", l, sub)], xk)
                        if next_norm and f0 > 0:
                            sq_ = SQN[(d * NT + t) % 4]
                            g.tt(sq_.v, X[:, d, t * 512:(t + 1) * 512], X[:, d, t * 512:(t + 1) * 512], ALU.mult, xk, sq_.k())
                            pend_sq.append((t, d, sq_))
                        if final and f0 > 0:
                            def emit_out(d=d, t=t, xk=xk):
                                oi = fin_i[0]
                                fin_i[0] += 1
                                po_ = 6 + oi % 2
                                ys = YS[oi % 4]
                                for a_ in range(4):
                                    g.tr(PS[po_][:, a_ * 128:(a_ + 1) * 128], X[:, d, t * 512 + a_ * 128:t * 512 + (a_ + 1) * 128],
                                         IDENT[:], xk + ["IDENT"], psk(po_))
                                g.cp(ys.v, PS[po_][:], psk(po_), ys.k(), eng="act")
                                g.dma("sp", yout[t * 512:(t + 1) * 512, d * 128:(d + 1) * 128].rearrange("(a p) f -> p a f", p=128),
                                      ys.v.rearrange("p (a f) -> p a f", a=4), ys.k(), ())
                            pend_out.append(emit_out)
                if bg is not None:
                    next(bg, None)

        flush_sq()

    def out_proj(l, wdram, in_fn, after_tile=None):
        bank = [0]
        slabs = [load_slab(wdram, dg * 512, 512) for dg in range(2)]
        for t in range(NT):
            n = 0 if t == 0 else 1
            for dg in range(2):
                wv, wk = slabs[dg]
                for di in range(4):
                    d = dg * 4 + di
                    pb = bank[0] % 6
                    bank[0] += 1
                    for k in range(8):
                        a, ks = in_fn(k, t)
                        g.mm(PS[pb][:], wv[:, k, di * 128:(di + 1) * 128], a, k == 0, k == 7, [wk] + ks, psk(pb))
                    xk = kk("X", [d], t * 512, t * 512 + 512)
                    g.stt(X[:, d, t * 512:(t + 1) * 512], PS[pb][:], cf(l, 1, 2, d, n), X[:, d, t * 512:(t + 1) * 512],
                          ALU.mult, ALU.add, psk(pb) + xk + [("COEF", l, 1)], xk)
            if after_tile is not None:
                after_tile(t)

    def mix_ab(l):
        Vz = scr(0, [14, 512], BF16)
        Qz = scr(14, [4, 6, 2, 256], BF16)
        Kb = scr(38, [4, T + 256], BF16)
        Ob = scr(52, [4, T], BF16)
        KOUT = scr(52, [4, 512], F32)
        CK = scr(60, [2, 512], F32)
        SQq = [scr(64 + i, [512], BF16) for i in range(2)]
        RSq = [scr(66 + 2 * i, [512], F32) for i in range(2)]
        QRAW = [scr(70 + 2 * i, [512], F32) for i in range(2)]
        wdv = [WD[i][:].rearrange("p a b -> p (a b)").bitcast(F32) for i in range(2)]
        wdk = lambda i, s: [("WD", i, s)]
        KFv, KFk = wdv[0][:, 0:512], wdk(0, 0)
        VFv = [wdv[1][:, 0:512], wdv[1][:, 512:1024]]
        VFk = [wdk(1, 0), wdk(1, 1)]
        hk_all = lambda t0, t1: kk("H", range(8), t0, t1)
        w2 = w_in[0]

        for j_ in range(4):
            g.ms(Qz.v[64:128, j_, :, 0, :], 0.0, Qz.k(j_))
            g.ms(Qz.v[0:64, j_, :, 1, :], 0.0, Qz.k(j_))

        wv, wk = load_slab(w2, 1024, 512)
        for tt_ in range(12):
            pb = tt_ % 4
            for k in range(8):
                g.mm(PS[pb][:], H[:, k, tt_ * 128:(tt_ + 1) * 128], wv[:, k, :], k == 0, k == 7,
                     [wk] + hk_all(tt_ * 128, tt_ * 128 + 128), psk(pb))
            if tt_ < 4:
                g.cp(VFv[tt_ % 2], PS[pb][:], psk(pb), VFk[tt_ % 2])
                g.cp(Vz.v[:, tt_, :], VFv[tt_ % 2], VFk[tt_ % 2], Vz.k(tt_), eng="act")
                g.dma("sp", nv_o[tt_ * 128:(tt_ + 1) * 128, :], VFv[tt_ % 2], VFk[tt_ % 2], ())
            else:
                g.cp(Vz.v[:, tt_, :], PS[pb][:], psk(pb), Vz.k(tt_), eng="act")
        g.dma("pool", Vz.v[:, 12:14, :], cv_d.rearrange("(a p) f -> p a f", p=128), (), Vz.k(12) + Vz.k(13))

        g.dma("sp", CK.v, ck_d.rearrange("(a p) f -> p a f", p=128), (), CK.k())
        for j in range(4):
            pb = 4 + j % 2
            for a in range(2):
                g.tr(PS[pb][:, a * 128:(a + 1) * 128], CK.v[:, a, j * 128:(j + 1) * 128], IDENT[:], CK.k() + ["IDENT"], psk(pb))
            g.cp(Kb.v[:, j, T:T + 256], PS[pb][:, 0:256], psk(pb), Kb.k(j, e0=T, e1=T + 256))

        slabs_qk = [load_slab(w2, 0, 512), load_slab(w2, 512, 512)]
        its = [(which, j, t) for which in range(2) for j in range(4) for t in range(NT)]

        def qk_a(n):
            which, j, t = its[n]
            wv, wk = slabs_qk[which]
            tok = slice(t * 512, (t + 1) * 512)
            pb = n % 4
            for k in range(8):
                g.mm(PS[pb][:], wv[:, k, j * 128:(j + 1) * 128], H[:, k, tok], k == 0, k == 7,
                     [wk] + hk_all(t * 512, t * 512 + 512), psk(pb))
            sq = SQq[n % 2]
            g.act(sq.v, PS[pb][:], AF.Square, psk(pb), sq.k())
            p2 = 4 + n % 2
            g.mm(PS[p2][:], BLK[:], sq.v, True, True, ["BLK"] + sq.k(), psk(p2))

        def qk_b(n):
            which, j, t = its[n]
            tok = slice(t * 512, (t + 1) * 512)
            pb, p2 = n % 4, 4 + n % 2
            rs = RSq[n % 2]
            g.act(rs.v, PS[p2][:], AF.Ln, psk(p2) + ["SM"], rs.k(), bias=SM[:, 40:41], scale=1.0 / 64)
            g.act(rs.v, rs.v, AF.Exp, rs.k(), rs.k(), scale=-0.5)
            rd_ = psk(pb) + ["SM"] + rs.k()
            if which == 0:
                for hh in range(2):
                    ps_ = slice(hh * 64, hh * 64 + 64)
                    g.stt(Qz.v[ps_, j, 2 * t:2 * t + 2, hh, :], PS[pb][ps_, :].rearrange("p (b q) -> p b q", b=2), SM[ps_, 0:1],
                          rs.v[ps_, :].rearrange("p (b q) -> p b q", b=2), ALU.mult, ALU.mult,
                          rd_, Qz.k(j, 2 * t) + Qz.k(j, 2 * t + 1))
            elif t == 0:
                g.stt(KFv, PS[pb][:], SM[:, 1:2], rs.v, ALU.mult, ALU.mult, rd_, KFk)
                g.cp(Kb.v[:, j, 0:512], KFv, KFk, Kb.k(j, e0=0, e1=512), eng="act")
                p3 = 6 + j % 2
                for a in range(4):
                    g.tr(PS[p3][:, a * 128:(a + 1) * 128], KFv[:, a * 128:(a + 1) * 128], IDENT[:], KFk + ["IDENT"], psk(p3))
                g.cp(KOUT.v[:, :, j * 128:(j + 1) * 128], PS[p3][:].rearrange("p (a f) -> p a f", a=4), psk(p3), KOUT.k())
            else:
                g.stt(Kb.v[:, j, tok], PS[pb][:], SM[:, 1:2], rs.v, ALU.mult, ALU.mult, rd_,
                      Kb.k(j, e0=t * 512, e1=t * 512 + 512))

        for n in range(len(its) + 1):
            if n < len(its):
                qk_a(n)
            if n >= 1:
                qk_b(n - 1)
        g.dma("sp", nk_o.rearrange("(a p) f -> p a f", p=128), KOUT.v, KOUT.k(), ())

        EBf = [scr(64 + i, [512], BF16) for i in range(8)]
        EFv = [wdv[0][:, 0:512], wdv[0][:, 512:1024]]
        EFk = [wdk(0, 0), wdk(0, 1)]
        RDv = [wdv[1][:, 0:512], wdv[1][:, 512:1024]]
        RDk = [wdk(1, 0), wdk(1, 1)]
        TBBs = [WG[i][:].rearrange("p a b -> p (a b)")[:, 0:3584].rearrange("p (h v e) -> p h v e", h=2, v=2) for i in range(2)]
        TBBks = [[("WG", 0)], [("WG", 1)]]

        def load_tables(j, part):
            tbb, tk_ = TBBs[j % 2], TBBks[j % 2]
            if part in (0, 2):
                for hh in range(2):
                    g.dma("pool", tbb[:, hh, :, :], btab_d[2 * j + hh], (), tk_)
            if part in (1, 2):
                g.act(tbb, tbb, AF.Exp, tk_, tk_)
        LOCAL = ((0, 4), (0, 6), (2, 8), (4, 8))
        calls = []
        for j in range(4):
            pc = [(j, t0, [(t0 + a * 128, t0 // 128 + a, None) for a in range(2)]) for (t0, L, n) in SEQS[:2]]
            sc_ = []
            for c in range(4):
                tl = []
                for jt in range(*LOCAL[c]):
                    tl.append((512 + jt * 128, 4 + jt, (0 if c in (0, 3) else 1, 6 - (2 * jt - 4 * c))))
                for a in range(2):
                    tl.append((T + a * 128, 12 + a, None))
                sc_.append((j, 512 + c * 256, tl))
            calls += [sc_[0], pc[0], sc_[1], sc_[2], pc[1], sc_[3]]
        items = [(ci, ti) for ci, cl in enumerate(calls) for ti in range(len(cl[2]))]
        LA = 4
        tb_loaded = set()
        nmask = [0]
        ebuf = {}

        def stage1(n):
            ci, ti = items[n]
            j, q0, tiles = calls[ci]
            if j not in tb_loaded:
                tb_loaded.add(j)
                if j == 0:
                    load_tables(0, 2)
                if j + 1 < 4:
                    load_tables(j + 1, 0)
            if n % 32 == 16 and j + 1 < 4:
                load_tables(j + 1, 1)
            TBB, TBBk = TBBs[j % 2], TBBks[j % 2]
            kc0, vt, bias = tiles[ti]
            sb_ = n % 4
            g.mm(PS[sb_][:], Kb.v[:, j, kc0:kc0 + 128], Qz.v[:, j, q0 // 256].rearrange("p h q -> p (h q)"), True, True,
                 Kb.k(j, e0=kc0, e1=kc0 + 128) + Qz.k(j, q0 // 256), psk(sb_))
            eb = EBf[n % 8]
            ebuf[n] = eb
            if bias is None:
                g.act(eb.v, PS[sb_][:], AF.Exp, psk(sb_), eb.k())
            else:
                fi_ = nmask[0] % 2
                nmask[0] += 1
                var, i0 = bias
                g.act(EFv[fi_], PS[sb_][:], AF.Exp, psk(sb_), EFk[fi_])
                g.tt(eb.v.rearrange("p (h q) -> p h q", h=2), EFv[fi_].rearrange("p (h q) -> p h q", h=2),
                     TBB[:, :, var, i0 * 64:(i0 + 4) * 64], ALU.mult, EFk[fi_] + TBBk, eb.k())

        def stage2(n):
            ci, ti = items[n]
            j, q0, tiles = calls[ci]
            kc0, vt, bias = tiles[ti]
            nt = len(tiles)
            po, pd = PS[4 + ci % 2], PS[6 + ci % 2]
            pok, pdk = psk(4 + ci % 2), psk(6 + ci % 2)
            eb = ebuf.pop(n)
            g.mm(pd[:], ONESB[:], eb.v, ti == 0, ti == nt - 1, ["ONESB"] + eb.k(), pdk, grp=False)
            g.mm(po[:], Vz.v[:, vt, j * 128:(j + 1) * 128], eb.v, ti == 0, ti == nt - 1, Vz.k(vt) + eb.k(), pok, grp=False)
            if ti == nt - 1:
                def fin(ci=ci, j=j, q0=q0, po=po, pd=pd, pok=pok, pdk=pdk):
                    rdv, rdk = RDv[ci % 2], RDk[ci % 2]
                    g.act(rdv, pd[:], AF.Ln, pdk, rdk)
                    g.act(rdv, rdv, AF.Exp, rdk, rdk, scale=-1.0)
                    ok_ = Ob.k(j, e0=q0, e1=q0 + 256)
                    g.tt(Ob.v[0:64, j, q0:q0 + 256], po[0:64, 0:256], rdv[0:64, 0:256], ALU.mult, pok + rdk, ok_)
                    g.tt(Ob.v[64:128, j, q0:q0 + 256], po[64:128, 256:512], rdv[64:128, 256:512], ALU.mult, pok + rdk, ok_)
                deferred.append([3, fin])

        deferred = []
        for n in range(len(items) + LA):
            if n < len(items):
                stage1(n)
            for dfr in list(deferred):
                dfr[0] -= 1
                if dfr[0] <= 0:
                    dfr[1]()
                    deferred.remove(dfr)
            if n >= LA:
                stage2(n - LA)
        for dfr in deferred:
            dfr[1]()

        if stage < 2.35:
            return
        YB = scr(0, [4, T], BF16)
        XB = scr(12, [T], F32)
        XC = scr(18, [T], F32)
        XCB = scr(24, [T], BF16)
        GG = scr(27, [T], F32)
        A_ = scr(33, [T], F32)
        TI = scr(39, [T], F32)
        T2 = scr(45, [T], F32)
        HS = [scr(64, [T], F32), scr(70, [T], F32)]
        wxb, wxk = load_slab(w2, 1536, 512)
        wgb, wgk = load_slab(w2, 2048, 512)
        tk = lambda s_, t: s_.k(e0=t * 512, e1=t * 512 + 512)
        for j in range(4):
            for t in range(NT):
                tok = slice(t * 512, (t + 1) * 512)
                pb = t % 4
                for k in range(8):
                    g.mm(PS[pb][:], wxb[:, k, j * 128:(j + 1) * 128], H[:, k, tok], k == 0, k == 7,
                         [wxk] + hk_all(t * 512, t * 512 + 512), psk(pb))
                g.cp(XB.v[:, tok], PS[pb][:], psk(pb), tk(XB, t), eng="act")
            cw = lambda jj: par("conv_w", jj * 4 + j)
            for (t0, L, n) in SEQS:
                sk_ = lambda s_, a, b: s_.k(e0=a, e1=b)
                g.ts(XC.v[:, t0:t0 + L], XB.v[:, t0:t0 + L], cw(1), par("conv_b", j), ALU.mult, ALU.add,
                     sk_(XB, t0, t0 + L) + ["PAR"], sk_(XC, t0, t0 + L))
                for (jj, so, do_, ln) in ((0, 0, 1, L - 1), (2, 1, 0, L - 1), (3, 2, 0, L - 2)):
                    g.stt(XC.v[:, t0 + do_:t0 + do_ + ln], XB.v[:, t0 + so:t0 + so + ln], cw(jj), XC.v[:, t0 + do_:t0 + do_ + ln],
                          ALU.mult, ALU.add, sk_(XB, t0, t0 + L) + sk_(XC, t0, t0 + L) + ["PAR"], sk_(XC, t0, t0 + L))
            g.cp(XCB.v, XC.v, XC.k(), XCB.k())
            for t in range(NT):
                tok = slice(t * 512, (t + 1) * 512)
                pb = 4 + t % 4
                for k in range(8):
                    g.mm(PS[pb][:], wgb[:, k, j * 128:(j + 1) * 128], H[:, k, tok], k == 0, k == 7,
                         [wgk] + hk_all(t * 512, t * 512 + 512), psk(pb))
                g.cp(GG.v[:, tok], PS[pb][:], psk(pb), tk(GG, t), eng="act")
                g.stt(T2.v[:, tok], GG.v[:, tok], 0.044715, GG.v[:, tok], ALU.mult, ALU.mult, tk(GG, t), tk(T2, t))
                g.stt(T2.v[:, tok], T2.v[:, tok], 1.0, GG.v[:, tok], ALU.add, ALU.mult, tk(T2, t) + tk(GG, t), tk(T2, t))
            for t in range(NT):
                tok = slice(t * 512, (t + 1) * 512)
                g.act(TI.v[:, tok], T2.v[:, tok], AF.Sigmoid, tk(T2, t), tk(TI, t), scale=1.5957691216)
                g.tt(GG.v[:, tok], TI.v[:, tok], GG.v[:, tok], ALU.mult, tk(TI, t) + tk(GG, t), tk(GG, t))
            toks = [slice(t * 512, (t + 1) * 512) for t in range(NT)]

            class _V:
                def __init__(s_, v, kf):
                    s_.v, s_.kf = v, kf
            wd_a = _V(wdv[0], lambda t: [("WD", 0, t)])
            wd_s = _V(wdv[1], lambda t: [("WD", 1, t)])
            sc_ = lambda b_: _V(b_.v, lambda t, b_=b_: tk(b_, t))
            Ad = [sc_(A_), wd_a]
            Sd = [sc_(T2), wd_s]
            Ud = [sc_(TI), sc_(XB)]
            for d in range(2):
                for t in range(NT):
                    g.mm(PS[t][:], BD[:, 0 * 8 + d * 4 + j, :], XCB.v[:, toks[t]], True, True, ["BD"] + tk(XCB, t), psk(t))
                    g.mm(PS[3 + t][:], BD[:, 1 * 8 + d * 4 + j, :], XCB.v[:, toks[t]], True, True, ["BD"] + tk(XCB, t), psk(3 + t))
                for t in range(NT):
                    g.act(Sd[d].v[:, toks[t]], PS[t][:], AF.Sigmoid, psk(t) + ["PAR"], Sd[d].kf(t), bias=par("b_r", d * 4 + j))
                for t in range(NT):
                    g.act(Ud[d].v[:, toks[t]], PS[3 + t][:], AF.Sigmoid, psk(3 + t) + ["PAR"], Ud[d].kf(t), bias=par("b_i", d * 4 + j))
            for d in range(2):
                for t in range(NT):
                    g.act(Ad[d].v[:, toks[t]], Sd[d].v[:, toks[t]], AF.Exp, Sd[d].kf(t) + ["SM"], Ad[d].kf(t), scale=C1(d, j))
                for t in range(NT):
                    g.act(Sd[d].v[:, toks[t]], Sd[d].v[:, toks[t]], AF.Exp, Sd[d].kf(t) + ["SM"], Sd[d].kf(t), scale=C2(d, j))
                    g.tt(Ud[d].v[:, toks[t]], Ud[d].v[:, toks[t]], XC.v[:, toks[t]], ALU.mult, Ud[d].kf(t) + tk(XC, t), Ud[d].kf(t))
            for d in range(2):
                for t in range(NT):
                    g.act(Sd[d].v[:, toks[t]], Sd[d].v[:, toks[t]], AF.Sqrt, Sd[d].kf(t) + ["SM"], Sd[d].kf(t), bias=SM[:, 41:42], scale=-1.0)
            for d in range(2):
                hs = HS[d]
                for t in range(NT):
                    g.tt(Ud[d].v[:, toks[t]], Ud[d].v[:, toks[t]], Sd[d].v[:, toks[t]], ALU.mult, Ud[d].kf(t) + Sd[d].kf(t), Ud[d].kf(t))
                for si, (t0, L, n) in enumerate(SEQS):
                    sl = slice(t0, t0 + L) if d == 0 else slice(t0 + L - 1, t0 - 1 if t0 > 0 else None, -1)
                    if n == 1:
                        init = par("stf" if d == 0 else "stb", j)
                    else:
                        init = 0.0
                    tiles_ = range(t0 // 512, (t0 + L - 1) // 512 + 1)
                    rk_ = [k_ for t in tiles_ for k_ in Ad[d].kf(t) + Ud[d].kf(t)] + ["PAR"]
                    g.P.op("dve", lambda e, sl=sl, init=init, hs=hs, d=d: e.tensor_tensor_scan(
                        out=hs.v[:, sl], data0=Ad[d].v[:, sl], data1=Ud[d].v[:, sl], initial=init, op0=ALU.mult, op1=ALU.add),
                        rk_, hs.k(e0=t0, e1=t0 + L))
                    if n == 0:
                        last = t0 + L - 1 if d == 0 else t0
                        col = d * 8 + si * 4 + j
                        g.cp(ST[:, col:col + 1], hs.v[:, last:last + 1], hs.k(e0=t0, e1=t0 + L), ["ST"])
            g.tt(HS[0].v, HS[0].v, HS[1].v, ALU.add, HS[0].k() + HS[1].k(), HS[0].k())
            g.tt(YB.v[:, j, :], HS[0].v, GG.v, ALU.mult, HS[0].k() + GG.k(), YB.k(j))

        if stage < 2.45:
            return
        def in_fn(k, t):
            b = Ob if k < 4 else YB
            return b.v[:, k % 4, t * 512:(t + 1) * 512], b.k(k % 4, e0=t * 512, e1=t * 512 + 512)
        out_proj(l, w_out_ab[0], in_fn, after_tile=(lambda t: norm_tile(l, 2, t, base=20, sq_on_act=True)) if stage >= 4 else None)

    g.ms(SM[:, 41:42], 1.0, ["SM"])

    def mix_fourier(l):
        CT = scr(0, [8, 1024], BF16)
        STt = scr(16, [8, 1024], BF16)
        AB = scr(32, [8, 4, 512], BF16)
        CS1 = scr(64, [2, 512], BF16)
        C256 = scr(66, [2, 256], BF16)
        S256 = scr(67, [2, 256], BF16)
        g.dma("pool", CS1.v, cs1_d.rearrange("(a p) f -> p a f", p=128), (), CS1.k())
        g.dma("pool", C256.v, ct256_d.rearrange("(a p) f -> p a f", p=128), (), C256.k())
        g.dma("pool", S256.v, st256_d.rearrange("(a p) f -> p a f", p=128), (), S256.k())
        g.dma("pool", CT.v, ct1k_d.rearrange("(a p) f -> p a f", p=128), (), CT.k())
        g.dma("pool", STt.v, st1k_d.rearrange("(a p) f -> p a f", p=128), (), STt.k())
        ev = [0]
        for (t0, L, n) in SEQS:
            ntt = L // 128
            ctab, stab = (C256, S256) if L == 256 else (CT, STt)
            for tt_ in range(ntt):
                for gi in range(4):
                    pb = ev[0] % 4
                    for cc in range(2):
                        g.mm(PS[pb][:], H[:, 2 * gi + cc, t0 + tt_ * 128:t0 + (tt_ + 1) * 128], CS1.v[:, cc, :], cc == 0, cc == 1,
                             kk("H", [2 * gi + cc], t0 + tt_ * 128, t0 + tt_ * 128 + 128) + CS1.k(), psk(pb))
                    g.cp(AB.v[:, tt_, gi, :], PS[pb][:], psk(pb), AB.k(tt_, gi), eng=("act" if ev[0] % 2 else "dve"))
                    ev[0] += 1
            N = min(L, 512)
            for gi in range(4):
                for cc in range(2):
                    for tb in range(L // N):
                        pb = 4 + ev[0] % 4
                        for tt_ in range(ntt):
                            g.mm(PS[pb][:, 0:N], AB.v[:, tt_, gi, cc * 128:(cc + 1) * 128], ctab.v[:, tt_, tb * N:(tb + 1) * N],
                                 tt_ == 0, False, AB.k(tt_, gi) + ctab.k(tt_), psk(pb))
                            g.mm(PS[pb][:, 0:N], AB.v[:, tt_, gi, 256 + cc * 128:256 + (cc + 1) * 128], stab.v[:, tt_, tb * N:(tb + 1) * N],
                                 False, tt_ == ntt - 1, AB.k(tt_, gi) + stab.k(tt_), psk(pb))
                        a0 = t0 + tb * N
                        g.cp(H[:, 2 * gi + cc, a0:a0 + N], PS[pb][:, 0:N], psk(pb), kk("H", [2 * gi + cc], a0, a0 + N),
                             eng=("act" if ev[0] % 2 else "dve"))
                        ev[0] += 1

        def in_fn(k, t):
            return H[:, k, t * 512:(t + 1) * 512], kk("H", [k], t * 512, t * 512 + 512)
        out_proj(l, w_out_c[0], in_fn, after_tile=(lambda t: norm_tile(l, 2, t, sq_on_act=True)) if stage >= 4 else None)

    def load_bd():
        for ti, wsrc in enumerate((lru_w_r, lru_w_i)):
            for d in range(2):
                for par_ in range(2):
                    pp = par_ * 64
                    g.dma("pool", BD[pp:pp + 64, ti * 8 + d * 4: ti * 8 + d * 4 + 4, pp:pp + 64],
                          wsrc[0, d, par_::2].rearrange("n i j -> i n j"), (), ["BD"])

    def mixer(l):
        if stage < 2.05:
            return
        norm_mod(l, 1, presum=True)
        if l == 0:
            load_bd()
            mix_ab(l)
        else:
            mix_fourier(l)

    bg = ada_gen(0)
    for _ in range(6):
        next(bg)
    for l in range(2):
        if stage < 10 and l == 1:
            break
        norm_mod(l, 0, presum=(l == 1))
        ffn(l, 0, fw["ffn1_gate"], fw["ffn1_up"], fw["ffn1_down"], bg=bg, next_norm=(stage >= 2.05))
        for _ in bg:
            pass
        dbg_dump()
        if stage < 2:
            break
        mixer(l)
        dbg_dump()
        if stage < 4:
            break
        bg = ada_gen(1) if l == 0 else iter(())
        ffn(l, 2, fw["ffn2_gate"], fw["ffn2_up"], fw["ffn2_down"], bg=bg, final=(l == 1), next_norm=(l == 0))
        if l == 0:
            for _ in range(6 - (18 - 19)):
                pass
        dbg_dump()

    YO = [scr(4 * i, [1024], F32) for i in range(2)]
    for tt_ in (range(12) if stage < 10 else ()):
        yo = YO[tt_ % 2].v
        yok = YO[tt_ % 2].k()
        for cg in range(2):
            pb = (tt_ * 2 + cg) % 4
            for ci in range(4):
                c = cg * 4 + ci
                g.tr(PS[pb][:, ci * 128:(ci + 1) * 128], X[:, c, tt_ * 128:(tt_ + 1) * 128], IDENT[:],
                     kk("X", [c], tt_ * 128, tt_ * 128 + 128) + ["IDENT"], psk(pb))
            g.cp(yo[:, cg * 512:(cg + 1) * 512], PS[pb][:], psk(pb), yok, eng=("dve" if cg else "act"))
        g.dma("sp", yout[tt_ * 128:(tt_ + 1) * 128, :], yo, yok, ())
    g.tr(PS[7][0:16, 0:128], ST[:, 0:16], IDENT[:], ["ST", "IDENT"], psk(7))
    STO = scr(8, [128], F32)
    g.cp(STO.v[0:16, :], PS[7][0:16, 0:128], psk(7), STO.k())
    g.dma("sp", nst_o, STO.v[0:16, :], STO.k(), ())
    P.finish()
    P.emit()
    g.es.close()
    return g


_CONST = {}


def _consts():
    if _CONST:
        return _CONST
    bf = ml_dtypes.bfloat16
    c = np.arange(256, dtype=np.float64)
    a = 2 * np.pi * np.outer(c, c) / 256.0
    _CONST["cs1"] = np.concatenate([np.cos(a), -np.sin(a)], axis=1).astype(np.float32) / 16.0
    _CONST["ct256"] = (np.cos(a) / 16.0).astype(np.float32)
    _CONST["st256"] = (np.sin(a) / 16.0).astype(np.float32)
    t = np.arange(1024, dtype=np.float64)
    a = 2 * np.pi * ((np.outer(t, t)) % 1024) / 1024.0
    _CONST["ct1k"] = (np.cos(a) / 32.0).astype(np.float32)
    _CONST["st1k"] = (np.sin(a) / 32.0).astype(np.float32)
    _CONST["ident"] = np.eye(128, dtype=np.float32)
    return _CONST


def _bias_table(rpb):
    r = np.asarray(rpb)[0]
    kc = np.arange(64)[:, None]
    qc = np.arange(64)[None, :]
    dc = np.clip(kc - qc + 15, 0, 30)
    cs = np.clip(qc - 8, 0, 48)
    col_in = (kc >= cs) & (kc < cs + 16)
    out = np.full((8, 2, 64, 2, 14, 64), -30000.0, np.float32)
    for par in range(2):
        for i in range(14):
            dr = 6 - i + par
            if abs(dr) > 7:
                continue
            vals = np.where(col_in[None], r[:, dr + 7][:, dc], np.float32(-30000.0))
            out[:, par, :, 0, i, :] = vals
            if -4 <= dr <= 3:
                out[:, par, :, 1, i, :] = vals
    return np.ascontiguousarray(out.reshape(8, 128, 2, 14 * 64))


_PROG = {}


def _get_prog(debug=0, stage=99):
    key = (debug, stage)
    if key not in _PROG:
        _PROG[key] = build(debug, stage)
    return _PROG[key]


def make_in_maps(inputs, cores):
    f = lambda a: np.ascontiguousarray(np.asarray(a, dtype=np.float32))
    shared = {k: f(inputs[k]) for k in ("w_ada", "b_ada", "norm_g", "ffn1_gate", "ffn1_up", "ffn1_down", "ffn2_gate",
                                         "ffn2_up", "ffn2_down", "w_in", "q_norm_g", "k_norm_g", "conv_w", "conv_b",
                                         "lru_w_r", "lru_b_r", "lru_w_i", "lru_b_i", "lru_lambda", "w_out_ab", "w_out_c")}
    shared.update(_consts())
    shared["btab"] = _bias_table(inputs["rpb"])
    xp, xs = f(inputs["x_prompt"]), f(inputs["x_sample"])
    maps = []
    for c in cores:
        s = c // 4
        m = dict(shared)
        m["xin"] = np.ascontiguousarray(np.concatenate([xp[2 * c], xp[2 * c + 1], xs[s]], axis=0))
        m["cond"] = np.ascontiguousarray(np.stack([f(inputs["c_ctx"]), f(inputs["c"])[s]], axis=0))
        m["ck"] = np.ascontiguousarray(f(inputs["cache_k"])[s, 0].reshape(256, 512))
        m["cv"] = np.ascontiguousarray(f(inputs["cache_v"])[s, 0].reshape(256, 512))
        m["stf"] = np.ascontiguousarray(f(inputs["state_lru_fwd"])[s, 0])
        m["stb"] = np.ascontiguousarray(f(inputs["state_lru_bwd"])[s, 0])
        maps.append(m)
    return maps


def kernel(**inputs):
    g = _get_prog()
    cores = list(range(8))
    res = run_bass_kernel_spmd(g.nc, make_in_maps(inputs, cores), core_ids=cores).results
    y_prompt = np.zeros((16, 256, 1024), np.float32)
    y_sample = np.zeros((2, 1024, 1024), np.float32)
    nk = np.zeros((16, 1, 256, 8, 64), np.float32)
    nv = np.zeros((16, 1, 256, 8, 64), np.float32)
    sf = np.zeros((16, 1, 512), np.float32)
    sbw = np.zeros((16, 1, 512), np.float32)
    for c in cores:
        r = res[c]
        yo = np.asarray(r["yout"])
        y_prompt[2 * c] = yo[0:256]
        y_prompt[2 * c + 1] = yo[256:512]
        if c % 4 == 0:
            y_sample[c // 4] = yo[512:]
        k_ = np.asarray(r["nk"]).reshape(2, 256, 8, 64)
        v_ = np.asarray(r["nv"]).reshape(2, 256, 8, 64)
        st = np.asarray(r["nst"]).reshape(2, 2, 512)
        for j in range(2):
            nk[2 * c + j, 0] = k_[j]
            nv[2 * c + j, 0] = v_[j]
            sf[2 * c + j, 0] = st[0, j]
            sbw[2 * c + j, 0] = st[1, j]
    return (y_prompt, y_sample, nk, nv, sf, sbw)
```

```python
import contextlib
import os
import numpy as np
import ml_dtypes
import concourse.bass as bass
import concourse.mybir as mybir
from concourse.bass_utils import run_bass_kernel_spmd

F32 = mybir.dt.float32
BF16 = mybir.dt.bfloat16
AF = mybir.ActivationFunctionType
ALU = mybir.AluOpType

N_DMA_SLOTS = 6


class Prog:
    COMPUTE = ("pe", "act", "dve", "pool")

    def __init__(self, nc):
        self.nc = nc
        self.streams = {e: [] for e in ("pe", "act", "dve", "pool", "sp")}
        self.cnt = {}
        self.seen = {e: {} for e in self.streams}
        self.state = {}
        self.dma_i = {"sp": 0, "pool": 0}
        self.sem_names = list(self.COMPUTE) + [f"{q}d{i}" for q in ("sp", "pool") for i in range(N_DMA_SLOTS)]
        for s in self.sem_names:
            self.cnt[s] = 0
        self.sems = {}
        self.nops = 0

    def _need(self, stream, waits, sem, val):
        if sem == "pe" and stream == "pe":
            return
        if self.seen[stream].get(sem, 0) >= val:
            return
        assert val <= self.cnt[sem], ("dependency on a silent op whose carrier is not issued yet", stream, sem, val)
        self.seen[stream][sem] = val
        waits[sem] = max(waits.get(sem, 0), val)

    def _deps(self, stream, reads, writes):
        waits = {}
        for k in reads:
            st = self.state.get(k)
            if st and st[0]:
                self._need(stream, waits, *st[0])
        for k in writes:
            st = self.state.get(k)
            if st:
                if st[0]:
                    self._need(stream, waits, *st[0])
                for s, v in st[1].items():
                    self._need(stream, waits, s, v)
        return waits

    def _commit(self, sem, val, reads, writes):
        for k in reads:
            st = self.state.setdefault(k, [None, {}])
            st[1][sem] = max(st[1].get(sem, 0), val)
        for k in writes:
            self.state[k] = [(sem, val), {}]

    def op(self, eng, fn, reads=(), writes=(), silent=False):
        waits = self._deps(eng, reads, writes)
        if silent:
            val = self.cnt[eng] + 1
        else:
            self.cnt[eng] += 1
            val = self.cnt[eng]
        self._commit(eng, val, reads, writes)
        self.streams[eng].append((waits, fn, eng, 0 if silent else 1))
        self.nops += 1

    def dma(self, q, fn, reads=(), writes=()):
        i = self.dma_i[q]
        self.dma_i[q] += 1
        sem = f"{q}d{i % N_DMA_SLOTS}"
        waits = self._deps(q, reads, writes)
        if self.cnt[sem] > 0:
            self._need(q, waits, sem, self.cnt[sem])
        self.cnt[sem] += 16
        val = self.cnt[sem]
        self._commit(sem, val, reads, writes)
        self.streams[q].append((waits, fn, sem, 16))
        self.nops += 1

    def finish(self):
        waits = {}
        for s in self.sem_names:
            if self.cnt[s] > 0:
                self._need("sp", waits, s, self.cnt[s])
        self.streams["sp"].append((waits, None, None, 0))

    def emit(self):
        nc = self.nc
        with contextlib.ExitStack() as es:
            for s in self.sem_names:
                self.sems[s] = es.enter_context(nc.semaphore(s))
            blk = es.enter_context(nc.Block())

            def run(stream):
                def body(e):
                    for waits, fn, sem, inc in self.streams[stream]:
                        for s, v in waits.items():
                            e.wait_ge(self.sems[s], v)
                        if fn is not None:
                            ins = fn(e)
                            if inc:
                                ins.then_inc(self.sems[sem], inc)
                return body

            blk.tensor(run("pe"))
            blk.scalar(run("act"))
            blk.vector(run("dve"))
            blk.gpsimd(run("pool"))
            blk.sync(run("sp"))


D = 1024
T = 1536
NT = 3
DFF = 2816
NF = 22
SEQS = ((0, 256, 0), (256, 256, 0), (512, 1024, 1))
EPS = 1e-6
HALVES = ((0, 12), (12, 10))


def _l(fn, *a, **k):
    return lambda e: fn(e, *a, **k)


class Gen:
    def __init__(self, debug=0):
        self.debug = debug
        self.nc = nc = bass.Bass("TRN2", target_bir_lowering=False)
        self.P = Prog(nc)
        self.es = contextlib.ExitStack()
        self.din = {}
        self.dout = {}

    def i(self, name, shape, dt=F32):
        ap = self.nc.dram_tensor(name, list(shape), dt, kind="ExternalInput").ap()
        self.din[name] = ap
        return ap

    def o(self, name, shape):
        ap = self.nc.dram_tensor(name, list(shape), F32, kind="ExternalOutput").ap()
        self.dout[name] = ap
        return ap

    def sb(self, name, shape, dt):
        return self.es.enter_context(self.nc.sbuf_tensor(name, list(shape), dt))

    def mm(self, out, lhsT, rhs, start, stop, r, w, grp=True):
        self.P.op("pe", lambda e: e.matmul(out, lhsT, rhs, start=start, stop=stop), r, w, silent=(grp and not stop))

    def tr(self, out, in_, ident, r, w):
        self.P.op("pe", lambda e: e.transpose(out, in_, ident), r, w)

    def act(self, out, in_, func, r, w, bias=None, scale=None):
        kw = {}
        if bias is not None:
            kw["bias"] = bias
        if scale is not None:
            kw["scale"] = scale
        self.P.op("act", lambda e: e.activation(out=out, in_=in_, func=func, **kw), r, w)

    def tt(self, out, in0, in1, op, r, w, eng="dve"):
        self.P.op(eng, lambda e: e.tensor_tensor(out=out, in0=in0, in1=in1, op=op), r, w)

    def ts(self, out, in0, s1, s2, op0, op1, r, w, eng="dve"):
        if s2 is None:
            self.P.op(eng, lambda e: e.tensor_scalar(out=out, in0=in0, scalar1=s1, scalar2=None, op0=op0), r, w)
        else:
            self.P.op(eng, lambda e: e.tensor_scalar(out=out, in0=in0, scalar1=s1, scalar2=s2, op0=op0, op1=op1), r, w)

    def stt(self, out, in0, scalar, in1, op0, op1, r, w, eng="dve"):
        self.P.op(eng, lambda e: e.scalar_tensor_tensor(out=out, in0=in0, scalar=scalar, in1=in1, op0=op0, op1=op1), r, w)

    def cp(self, out, in_, r, w, eng="dve"):
        if eng == "act":
            self.P.op("act", lambda e: e.copy(out=out, in_=in_), r, w)
        else:
            self.P.op(eng, lambda e: e.tensor_copy(out=out, in_=in_), r, w)

    def ms(self, ap, val, w, eng="dve"):
        self.P.op(eng, lambda e: e.memset(ap, val), (), w)

    def dma(self, q, out, in_, r, w, nc_ok=False):
        if nc_ok:
            self.P.dma(q, lambda e: e.dma_start(out=out, in_=in_, allow_slow_non_contiguous=True), r, w)
        else:
            self.P.dma(q, lambda e: e.dma_start(out=out, in_=in_), r, w)


def kk(name, cs, t0, t1):
    return [(name, c, b) for c in cs for b in range(t0 // 256, (t1 - 1) // 256 + 1)]


def build(debug=0, stage=99):
    g = Gen(debug)
    nc, P = g.nc, g.P
    xin = g.i("xin", [T, D])
    cond = g.i("cond", [2, D])
    ck_d = g.i("ck", [256, 512])
    cv_d = g.i("cv", [256, 512])
    stf_d = g.i("stf", [512])
    stb_d = g.i("stb", [512])
    w_ada = g.i("w_ada", [2, D, 9 * D])
    b_ada = g.i("b_ada", [2, 9 * D])
    norm_g = g.i("norm_g", [2, 3, D])
    fw = {}
    for nm in ("ffn1_gate", "ffn1_up", "ffn2_gate", "ffn2_up"):
        fw[nm] = g.i(nm, [2, D, DFF])
    for nm in ("ffn1_down", "ffn2_down"):
        fw[nm] = g.i(nm, [2, DFF, D])
    w_in = g.i("w_in", [1, D, 2560])
    qg_d = g.i("q_norm_g", [1, 64])
    kg_d = g.i("k_norm_g", [1, 64])
    btab_d = g.i("btab", [8, 128, 2, 14 * 64])
    conv_w = g.i("conv_w", [1, 4, 512])
    conv_b = g.i("conv_b", [1, 512])
    lru_w_r = g.i("lru_w_r", [1, 2, 8, 64, 64])
    lru_b_r = g.i("lru_b_r", [1, 2, 512])
    lru_w_i = g.i("lru_w_i", [1, 2, 8, 64, 64])
    lru_b_i = g.i("lru_b_i", [1, 2, 512])
    lru_lam = g.i("lru_lambda", [1, 2, 512])
    w_out_ab = g.i("w_out_ab", [1, D, D])
    w_out_c = g.i("w_out_c", [1, D, D])
    ident_d = g.i("ident", [128, 128])
    cs1_d = g.i("cs1", [256, 512])
    ct1k_d = g.i("ct1k", [1024, 1024])
    st1k_d = g.i("st1k", [1024, 1024])
    ct256_d = g.i("ct256", [256, 256])
    st256_d = g.i("st256", [256, 256])

    yout = g.o("yout", [T, D])
    nk_o = g.o("nk", [512, 512])
    nv_o = g.o("nv", [512, 512])
    nst_o = g.o("nst", [16, 128])
    if debug:
        dbg_o = g.o("dbg", [debug, 128, 8, T])

    X = g.sb("X", [128, 8, T], F32)
    H = g.sb("H", [128, 8, T], BF16)
    SCR = g.sb("SCR", [128, 38 * 1024], BF16)
    WG = [g.sb(f"WG{i}", [128, 16, 256], BF16) for i in range(2)]
    WD = [g.sb(f"WD{i}", [128, 12, 256], BF16) for i in range(2)]
    IDENT = g.sb("IDENT", [128, 128], F32)
    ONESB = g.sb("ONESB", [128, 128], BF16)
    BLK = g.sb("BLK", [128, 128], BF16)
    PAR = g.sb("PAR", [128, 384], F32)
    SC = g.sb("SC", [128, 8, 2], BF16)
    MOD = g.sb("MOD", [128, 2, 9, 8, 2], F32)
    COEF = g.sb("COEF", [128, 2, 3, 3, 8, 2], F32)
    SM = g.sb("SM", [128, 64], F32)
    ST = g.sb("ST", [128, 16], F32)
    BD = g.sb("BD", [128, 16, 128], BF16)
    PS = [g.es.enter_context(nc.psum_tensor(f"ps{i}", [128, 512], F32)) for i in range(8)]

    class Scr:
        def __init__(self, off_kib, shape, dt):
            self.off = int(off_kib * 1024)
            self.shape = list(shape)
            self.eb = 4 if dt == F32 else 2
            n = int(np.prod(shape))
            e0 = self.off // 2
            if dt == F32:
                v = SCR[:, e0: e0 + 2 * n].bitcast(F32)
            else:
                v = SCR[:, e0: e0 + n]
            if len(shape) > 1:
                names = " ".join(f"d{i}" for i in range(len(shape)))
                kw = {f"d{i}": s for i, s in enumerate(shape[:-1])}
                v = v.rearrange(f"p ({names}) -> p {names}", **kw)
            self.v = v

        def k(self, *idx, e0=0, e1=None):
            rest = int(np.prod(self.shape[len(idx):])) if len(idx) < len(self.shape) else 1
            base = 0
            for i, ix in enumerate(idx):
                base = base * self.shape[i] + ix
            base *= rest
            if e1 is None:
                e1 = rest
            b0 = self.off + (base + e0) * self.eb
            b1 = self.off + (base + e1) * self.eb
            return [("SCR", b) for b in range(b0 // 1024, (b1 - 1) // 1024 + 1)]

    def scr(off_kib, shape, dt):
        return Scr(off_kib, shape, dt)

    psk = lambda i: [("ps", i)]

    g.dma("sp", IDENT[:], ident_d, (), ["IDENT"])
    g.ms(ONESB[:], 1.0, ["ONESB"])
    g.ms(BLK[:], 0.0, ["BLK"])
    g.ms(BLK[0:64, 0:64], 1.0, ["BLK"])
    g.ms(BLK[64:128, 64:128], 1.0, ["BLK"])
    g.ms(BD[:], 0.0, ["BD"])
    g.ms(ST[:], 0.0, ["ST"])

    PSTs = scr(0, [3, 128], F32)
    PST = PSTs.v
    PSTk = PSTs.k()
    g.ms(PST, 0.0, PSTk)
    rows = {}
    r0 = [0]

    def prow(name, ap, n, tile=0):
        base = r0[0] if tile == 0 else 0
        g.dma("sp", PST[base:base + n, tile, :], ap, (), PSTk)
        rows[name] = tile * 128 + base
        if tile == 0:
            r0[0] += n

    prow("cond", cond.rearrange("n (c p) -> (n c) p", p=128), 16)
    prow("norm_g", norm_g.rearrange("l i (c p) -> (l i c) p", p=128), 48)
    prow("conv_w", conv_w[0].rearrange("j (c p) -> (j c) p", p=128), 16)
    prow("conv_b", conv_b[0].rearrange("(c p) -> c p", p=128), 4)
    prow("b_r", lru_b_r[0].rearrange("d (c p) -> (d c) p", p=128), 8)
    prow("b_i", lru_b_i[0].rearrange("d (c p) -> (d c) p", p=128), 8)
    prow("lam", lru_lam[0].rearrange("d (c p) -> (d c) p", p=128), 8)
    prow("stf", stf_d.rearrange("(c p) -> c p", p=128), 4)
    prow("stb", stb_d.rearrange("(c p) -> c p", p=128), 4)
    prow("b_ada0", b_ada[0].rearrange("(m p) -> m p", p=128), 72, tile=1)
    prow("b_ada1", b_ada[1].rearrange("(m p) -> m p", p=128), 72, tile=2)
    for t3 in range(3):
        g.tr(PS[7][:, t3 * 128:(t3 + 1) * 128], PST[:, t3, :], IDENT[:], PSTk + ["IDENT"], psk(7))
    g.cp(PAR[:], PS[7][:, 0:384], psk(7), ["PAR"])
    par = lambda name, k=0: PAR[:, rows[name] + k: rows[name] + k + 1]
    parn = lambda name, k, n: PAR[:, rows[name] + k: rows[name] + k + n]

    for hlf in range(2):
        g.dma("sp", SM[hlf * 64:(hlf + 1) * 64, 0:1], qg_d.rearrange("o d -> d o"), (), ["SM"], nc_ok=True)
        g.dma("sp", SM[hlf * 64:(hlf + 1) * 64, 1:2], kg_d.rearrange("o d -> d o"), (), ["SM"], nc_ok=True)
    g.ts(SM[:, 0:1], SM[:, 0:1], 0.125, None, ALU.mult, None, ["SM"], ["SM"])
    g.act(SM[:, 16:24], parn("lam", 0, 8), AF.Exp, ["PAR"], ["SM"], scale=-1.0)
    g.ts(SM[:, 24:32], SM[:, 16:24], 1.0, None, ALU.add, None, ["SM"], ["SM"])
    g.act(SM[:, 32:40], SM[:, 24:32], AF.Ln, ["SM"], ["SM"])
    g.ts(SM[:, 24:32], SM[:, 24:32], -1.0, 1e-30, ALU.add, ALU.max, ["SM"], ["SM"])
    g.P.op("dve", lambda e: e.reciprocal(out=SM[:, 24:32], in_=SM[:, 24:32]), ["SM"], ["SM"])
    g.tt(SM[:, 16:24], SM[:, 16:24], SM[:, 24:32], ALU.mult, ["SM"], ["SM"])
    g.tt(SM[:, 16:24], SM[:, 16:24], SM[:, 32:40], ALU.mult, ["SM"], ["SM"])
    g.ts(SM[:, 8:16], SM[:, 16:24], -8.0, None, ALU.mult, None, ["SM"], ["SM"])
    C1 = lambda d, c: SM[:, 8 + d * 4 + c: 9 + d * 4 + c]
    g.ts(SM[:, 48:56], SM[:, 8:16], 2.0, None, ALU.mult, None, ["SM"], ["SM"])
    C2 = lambda d, c: SM[:, 48 + d * 4 + c: 49 + d * 4 + c]

    for n in range(2):
        g.act(SC[:, :, n], parn("cond", n * 8, 8), AF.Silu, ["PAR"], ["SC"])

    XS = [scr(o_, [1024], F32) for o_ in (8, 12, 60, 64, 68, 72)]
    for tt_ in range(12):
        xs = XS[tt_ % 6].v
        xsk = XS[tt_ % 6].k()
        g.dma("sp", xs, xin[tt_ * 128:(tt_ + 1) * 128, :], (), xsk)
        for cg in range(2):
            pb = 4 + (tt_ * 2 + cg) % 4
            for ci in range(4):
                c = cg * 4 + ci
                g.tr(PS[pb][:, ci * 128:(ci + 1) * 128], xs[:, c * 128:(c + 1) * 128], IDENT[:],
                     xsk + ["IDENT"], psk(pb))
            g.cp(X[:, cg * 4:(cg + 1) * 4, tt_ * 128:(tt_ + 1) * 128],
                 PS[pb][:].rearrange("p (c t) -> p c t", c=4), psk(pb),
                 kk("X", range(cg * 4, cg * 4 + 4), tt_ * 128, tt_ * 128 + 128), eng=("dve" if cg else "act"))

    dbg_n = [0]

    def dbg_dump():
        if debug and dbg_n[0] < debug:
            g.dma("sp", dbg_o[dbg_n[0]], X[:], kk("X", range(8), 0, T), ())
            dbg_n[0] += 1

    wg_i = [0]
    wd_i = [0]

    def wg_next():
        i = wg_i[0] % 2
        wg_i[0] += 1
        return i

    def wd_next():
        i = wd_i[0] % 2
        wd_i[0] += 1
        return i

    def load_slab(dram2d, c0, ncols):
        i = wg_next()
        v = WG[i][:].rearrange("p a b -> p (a b)")[:, 0:8 * ncols].rearrange("p (k n) -> p k n", k=8)
        g.dma("pool", v, dram2d[:, c0:c0 + ncols].rearrange("(k p) n -> p k n", p=128), (), [("WG", i)])
        return v, ("WG", i)

    WA = [scr(60 + 8 * i, [8, 512], BF16) for i in range(2)]
    WAX = [scr(16 + 8 * i, [8, 512], BF16) for i in range(6)]

    def ada_gen(l):
        for s in range(18):
            wa = WAX[s] if (l == 0 and s < 6) else WA[s % 2]
            g.dma("pool", wa.v, w_ada[l][:, s * 512:(s + 1) * 512].rearrange("(k p) n -> p k n", p=128), (), wa.k())
            pb = 6 + s % 2
            for cc in range(4):
                for k in range(8):
                    g.mm(PS[pb][:, cc * 2:cc * 2 + 2], wa.v[:, k, cc * 128:(cc + 1) * 128], SC[:, k, :],
                         k == 0, k == 7, wa.k() + ["SC"], psk(pb))
            j, c0 = s // 2, (s % 2) * 4
            for n in range(2):
                g.tt(MOD[:, l, j, c0:c0 + 4, n], PS[pb][:, 0:8].rearrange("p (c n) -> p c n", n=2)[:, :, n],
                     parn(f"b_ada{l}", j * 8 + c0, 4), ALU.add, psk(pb) + ["PAR"], [("MOD", l, j)])
            if s % 6 == 5:
                sub = s // 6
                mk = [("MOD", l, 3 * sub + q) for q in range(3)]
                for n in range(2):
                    gn = parn("norm_g", l * 24 + sub * 8, 8)
                    g.stt(COEF[:, l, sub, 0, :, n], MOD[:, l, 3 * sub + 1, :, n], 1.0, gn, ALU.add, ALU.mult,
                          mk + ["PAR"], [("COEF", l, sub)])
                    g.cp(COEF[:, l, sub, 1, :, n], MOD[:, l, 3 * sub + 0, :, n], mk, [("COEF", l, sub)])
                    g.ts(COEF[:, l, sub, 2, :, n], MOD[:, l, 3 * sub + 2, :, n], (1.0 if sub == 1 else 0.5), None,
                         ALU.mult, None, mk, [("COEF", l, sub)])
            yield

    cf = lambda l, sub, which, c, n: COEF[:, l, sub, which, c, n:n + 1]


    def norm_mod(l, sub, presum=False):
        for t in range(NT):
            norm_tile(l, sub, t, presum=presum)

    def norm_tile(l, sub, t, base=0, sq_on_act=False, presum=False):
        SQ = scr(base, [8, 512], BF16)
        RS = [scr(base + 8 + 2 * i, [512], F32) for i in range(2)]
        TMP = [scr(base + 12 + 2 * i, [512], F32) for i in range(4)]
        if True:
            n = 0 if t == 0 else 1
            tok = slice(t * 512, (t + 1) * 512)
            for c in (() if presum else range(8)):
                xk = kk("X", [c], t * 512, t * 512 + 512)
                if sq_on_act:
                    g.act(SQ.v[:, c, :], X[:, c, tok], AF.Square, xk, SQ.k(c))
                else:
                    g.tt(SQ.v[:, c, :], X[:, c, tok], X[:, c, tok], ALU.mult, xk, SQ.k(c))
            pb = t if presum else 6 + t % 2
            for c in (() if presum else range(8)):
                g.mm(PS[pb][:], ONESB[:], SQ.v[:, c, :], c == 0, c == 7, ["ONESB"] + SQ.k(c), psk(pb))
            rs = RS[t % 2].v
            rsk = RS[t % 2].k()
            g.act(rs, PS[pb][:], AF.Ln, psk(pb) + ["SM"], rsk, bias=SM[:, 40:41], scale=1.0 / D)
            g.act(rs, rs, AF.Exp, rsk, rsk, scale=-0.5)
            for c in range(8):
                tm = TMP[c % 4].v
                tmk = TMP[c % 4].k()
                g.tt(tm, X[:, c, tok], rs, ALU.mult, kk("X", [c], t * 512, t * 512 + 512) + rsk,
                     tmk)
                g.act(H[:, c, tok], tm, AF.Identity, tmk + [("COEF", l, sub)], kk("H", [c], t * 512, t * 512 + 512),
                      bias=cf(l, sub, 1, c, n), scale=cf(l, sub, 0, c, n))

    g.ms(SM[:, 40:41], EPS, ["SM"])

    def ffn(l, sub, wgate, wup, wdown, bg=None, final=False, next_norm=False):
        HID = scr(20, [12, T], BF16)
        SG = [scr(56 + 2 * i, [512], F32) for i in range(2)]
        pair = [0]
        dbank = [0]
        fin_i = [0]
        YS = [scr(60 + 2 * i, [512], F32) for i in range(4)]
        SQN = [scr(i, [512], BF16) for i in range(4)]
        pend_sq = []

        pend_out = []

        def flush_sq():
            while pend_sq:
                t_, d_, sq_ = pend_sq.pop(0)
                g.mm(PS[t_][:], ONESB[:], sq_.v, d_ == 0, d_ == 7, ["ONESB"] + sq_.k(), psk(t_), grp=False)
            while pend_out:
                pend_out.pop(0)()

        for (f0, nf) in HALVES:
            for fg in range(nf // 2):
                i = wg_next()
                fa = f0 + fg * 2
                g.dma("pool", WG[i][:, 0:8, :], wgate[l][:, fa * 128:(fa + 2) * 128].rearrange("(k p) n -> p k n", p=128),
                      (), [("WG", i)])
                g.dma("pool", WG[i][:, 8:16, :], wup[l][:, fa * 128:(fa + 2) * 128].rearrange("(k p) n -> p k n", p=128),
                      (), [("WG", i)])
                for fi in range(2):
                    fl = fg * 2 + fi
                    for t in range(NT):
                        pg = (pair[0] % 2) * 2
                        pair[0] += 1
                        hk = kk("H", range(8), t * 512, t * 512 + 512)
                        for k in range(8):
                            g.mm(PS[pg][:], WG[i][:, k, fi * 128:(fi + 1) * 128], H[:, k, t * 512:(t + 1) * 512],
                                 k == 0, k == 7, [("WG", i)] + hk, psk(pg))
                        for k in range(8):
                            g.mm(PS[pg + 1][:], WG[i][:, 8 + k, fi * 128:(fi + 1) * 128], H[:, k, t * 512:(t + 1) * 512],
                                 k == 0, k == 7, [("WG", i)] + hk, psk(pg + 1))
                        sg = SG[(pair[0]) % 2].v
                        sgk = SG[(pair[0]) % 2].k()
                        g.act(sg, PS[pg][:], AF.Silu, psk(pg), sgk)
                        g.tt(HID.v[:, fl, t * 512:(t + 1) * 512], sg, PS[pg + 1][:], ALU.mult,
                             sgk + psk(pg + 1), HID.k(fl, e0=t * 512, e1=t * 512 + 512))
                if bg is not None:
                    next(bg, None)
            for dg in range(4):
                i = wd_next()
                g.dma("pool", WD[i][:, 0:nf, :],
                      wdown[l][f0 * 128:(f0 + nf) * 128, dg * 256:(dg + 1) * 256].rearrange("(f p) n -> p f n", p=128),
                      (), [("WD", i, s_) for s_ in range(3)])
                for di in range(2):
                    d = dg * 2 + di
                    for t in range(NT):
                        n = 0 if t == 0 else 1
                        pb = 4 + dbank[0] % 2
                        dbank[0] += 1
                        for f in range(nf):
                            g.mm(PS[pb][:], WD[i][:, f, di * 128:(di + 1) * 128], HID.v[:, f, t * 512:(t + 1) * 512],
                                 f == 0, f == nf - 1, [("WD", i, s_) for s_ in range(3)] + HID.k(f, e0=t * 512, e1=t * 512 + 512), psk(pb))
                        flush_sq()
                        xk = kk("X", [d], t * 512, t * 512 + 512)
                        g.stt(X[:, d, t * 512:(t + 1) * 512], PS[pb][:], cf(l, sub, 2, d, n), X[:, d, t * 512:(t + 1) * 512],
                              ALU.mult, ALU.add, psk(pb) + xk + [("COEF", l, sub)], xk)
                        if next_norm and f0 > 0:
                            sq_ = SQN[(d * NT + t) % 4]
                            g.tt(sq_.v, X[:, d, t * 512:(t + 1) * 512], X[:, d, t * 512:(t + 1) * 512], ALU.mult, xk, sq_.k())
                            pend_sq.append((t, d, sq_))
                        if final and f0 > 0:
                            def emit_out(d=d, t=t, xk=xk):
                                oi = fin_i[0]
                                fin_i[0] += 1
                                po_ = 6 + oi % 2
                                ys = YS[oi % 4]
                                for a_ in range(4):
                                    g.tr(PS[po_][:, a_ * 128:(a_ + 1) * 128], X[:, d, t * 512 + a_ * 128:t * 512 + (a_ + 1) * 128],
                                         IDENT[:], xk + ["IDENT"], psk(po_))
                                g.cp(ys.v, PS[po_][:], psk(po_), ys.k(), eng="act")
                                g.dma("sp", yout[t * 512:(t + 1) * 512, d * 128:(d + 1) * 128].rearrange("(a p) f -> p a f", p=128),
                                      ys.v.rearrange("p (a f) -> p a f", a=4), ys.k(), ())
                            pend_out.append(emit_out)
                if bg is not None:
                    next(bg, None)

        flush_sq()

    def out_proj(l, wdram, in_fn, after_tile=None):
        bank = [0]
        slabs = [load_slab(wdram, dg * 512, 512) for dg in range(2)]
        for t in range(NT):
            n = 0 if t == 0 else 1
            for dg in range(2):
                wv, wk = slabs[dg]
                for di in range(4):
                    d = dg * 4 + di
                    pb = bank[0] % 6
                    bank[0] += 1
                    for k in range(8):
                        a, ks = in_fn(k, t)
                        g.mm(PS[pb][:], wv[:, k, di * 128:(di + 1) * 128], a, k == 0, k == 7, [wk] + ks, psk(pb))
                    xk = kk("X", [d], t * 512, t * 512 + 512)
                    g.stt(X[:, d, t * 512:(t + 1) * 512], PS[pb][:], cf(l, 1, 2, d, n), X[:, d, t * 512:(t + 1) * 512],
                          ALU.mult, ALU.add, psk(pb) + xk + [("COEF", l, 1)], xk)
            if after_tile is not None:
                after_tile(t)

    def mix_ab(l):
        Vz = scr(0, [14, 512], BF16)
        Qz = scr(14, [4, 6, 2, 256], BF16)
        Kb = scr(38, [4, T + 256], BF16)
        Ob = scr(52, [4, T], BF16)
        KOUT = scr(52, [4, 512], F32)
        CK = scr(60, [2, 512], F32)
        SQq = [scr(64 + i, [512], BF16) for i in range(2)]
        RSq = [scr(66 + 2 * i, [512], F32) for i in range(2)]
        QRAW = [scr(70 + 2 * i, [512], F32) for i in range(2)]
        wdv = [WD[i][:].rearrange("p a b -> p (a b)").bitcast(F32) for i in range(2)]
        wdk = lambda i, s: [("WD", i, s)]
        KFv, KFk = wdv[0][:, 0:512], wdk(0, 0)
        VFv = [wdv[1][:, 0:512], wdv[1][:, 512:1024]]
        VFk = [wdk(1, 0), wdk(1, 1)]
        hk_all = lambda t0, t1: kk("H", range(8), t0, t1)
        w2 = w_in[0]

        for j_ in range(4):
            g.ms(Qz.v[64:128, j_, :, 0, :], 0.0, Qz.k(j_))
            g.ms(Qz.v[0:64, j_, :, 1, :], 0.0, Qz.k(j_))

        wv, wk = load_slab(w2, 1024, 512)
        for tt_ in range(12):
            pb = tt_ % 4
            for k in range(8):
                g.mm(PS[pb][:], H[:, k, tt_ * 128:(tt_ + 1) * 128], wv[:, k, :], k == 0, k == 7,
                     [wk] + hk_all(tt_ * 128, tt_ * 128 + 128), psk(pb))
            if tt_ < 4:
                g.cp(VFv[tt_ % 2], PS[pb][:], psk(pb), VFk[tt_ % 2])
                g.cp(Vz.v[:, tt_, :], VFv[tt_ % 2], VFk[tt_ % 2], Vz.k(tt_), eng="act")
                g.dma("sp", nv_o[tt_ * 128:(tt_ + 1) * 128, :], VFv[tt_ % 2], VFk[tt_ % 2], ())
            else:
                g.cp(Vz.v[:, tt_, :], PS[pb][:], psk(pb), Vz.k(tt_), eng="act")
        g.dma("pool", Vz.v[:, 12:14, :], cv_d.rearrange("(a p) f -> p a f", p=128), (), Vz.k(12) + Vz.k(13))

        g.dma("sp", CK.v, ck_d.rearrange("(a p) f -> p a f", p=128), (), CK.k())
        for j in range(4):
            pb = 4 + j % 2
            for a in range(2):
                g.tr(PS[pb][:, a * 128:(a + 1) * 128], CK.v[:, a, j * 128:(j + 1) * 128], IDENT[:], CK.k() + ["IDENT"], psk(pb))
            g.cp(Kb.v[:, j, T:T + 256], PS[pb][:, 0:256], psk(pb), Kb.k(j, e0=T, e1=T + 256))

        slabs_qk = [load_slab(w2, 0, 512), load_slab(w2, 512, 512)]
        its = [(which, j, t) for which in range(2) for j in range(4) for t in range(NT)]

        def qk_a(n):
            which, j, t = its[n]
            wv, wk = slabs_qk[which]
            tok = slice(t * 512, (t + 1) * 512)
            pb = n % 4
            for k in range(8):
                g.mm(PS[pb][:], wv[:, k, j * 128:(j + 1) * 128], H[:, k, tok], k == 0, k == 7,
                     [wk] + hk_all(t * 512, t * 512 + 512), psk(pb))
            sq = SQq[n % 2]
            g.act(sq.v, PS[pb][:], AF.Square, psk(pb), sq.k())
            p2 = 4 + n % 2
            g.mm(PS[p2][:], BLK[:], sq.v, True, True, ["BLK"] + sq.k(), psk(p2))

        def qk_b(n):
            which, j, t = its[n]
            tok = slice(t * 512, (t + 1) * 512)
            pb, p2 = n % 4, 4 + n % 2
            rs = RSq[n % 2]
            g.act(rs.v, PS[p2][:], AF.Ln, psk(p2) + ["SM"], rs.k(), bias=SM[:, 40:41], scale=1.0 / 64)
            g.act(rs.v, rs.v, AF.Exp, rs.k(), rs.k(), scale=-0.5)
            rd_ = psk(pb) + ["SM"] + rs.k()
            if which == 0:
                for hh in range(2):
                    ps_ = slice(hh * 64, hh * 64 + 64)
                    g.stt(Qz.v[ps_, j, 2 * t:2 * t + 2, hh, :], PS[pb][ps_, :].rearrange("p (b q) -> p b q", b=2), SM[ps_, 0:1],
                          rs.v[ps_, :].rearrange("p (b q) -> p b q", b=2), ALU.mult, ALU.mult,
                          rd_, Qz.k(j, 2 * t) + Qz.k(j, 2 * t + 1))
            elif t == 0:
                g.stt(KFv, PS[pb][:], SM[:, 1:2], rs.v, ALU.mult, ALU.mult, rd_, KFk)
                g.cp(Kb.v[:, j, 0:512], KFv, KFk, Kb.k(j, e0=0, e1=512), eng="act")
                p3 = 6 + j % 2
                for a in range(4):
                    g.tr(PS[p3][:, a * 128:(a + 1) * 128], KFv[:, a * 128:(a + 1) * 128], IDENT[:], KFk + ["IDENT"], psk(p3))
                g.cp(KOUT.v[:, :, j * 128:(j + 1) * 128], PS[p3][:].rearrange("p (a f) -> p a f", a=4), psk(p3), KOUT.k())
            else:
                g.stt(Kb.v[:, j, tok], PS[pb][:], SM[:, 1:2], rs.v, ALU.mult, ALU.mult, rd_,
                      Kb.k(j, e0=t * 512, e1=t * 512 + 512))

        for n in range(len(its) + 1):
            if n < len(its):
                qk_a(n)
            if n >= 1:
                qk_b(n - 1)
        g.dma("sp", nk_o.rearrange("(a p) f -> p a f", p=128), KOUT.v, KOUT.k(), ())

        EBf = [scr(64 + i, [512], BF16) for i in range(8)]
        EFv = [wdv[0][:, 0:512], wdv[0][:, 512:1024]]
        EFk = [wdk(0, 0), wdk(0, 1)]
        RDv = [wdv[1][:, 0:512], wdv[1][:, 512:1024]]
        RDk = [wdk(1, 0), wdk(1, 1)]
        TBBs = [WG[i][:].rearrange("p a b -> p (a b)")[:, 0:3584].rearrange("p (h v e) -> p h v e", h=2, v=2) for i in range(2)]
        TBBks = [[("WG", 0)], [("WG", 1)]]

        def load_tables(j, part):
            tbb, tk_ = TBBs[j % 2], TBBks[j % 2]
            if part in (0, 2):
                for hh in range(2):
                    g.dma("pool", tbb[:, hh, :, :], btab_d[2 * j + hh], (), tk_)
            if part in (1, 2):
                g.act(tbb, tbb, AF.Exp, tk_, tk_)
        LOCAL = ((0, 4), (0, 6), (2, 8), (4, 8))
        calls = []
        for j in range(4):
            pc = [(j, t0, [(t0 + a * 128, t0 // 128 + a, None) for a in range(2)]) for (t0, L, n) in SEQS[:2]]
            sc_ = []
            for c in range(4):
                tl = []
                for jt in range(*LOCAL[c]):
                    tl.append((512 + jt * 128, 4 + jt, (0 if c in (0, 3) else 1, 6 - (2 * jt - 4 * c))))
                for a in range(2):
                    tl.append((T + a * 128, 12 + a, None))
                sc_.append((j, 512 + c * 256, tl))
            calls += [sc_[0], pc[0], sc_[1], sc_[2], pc[1], sc_[3]]
        items = [(ci, ti) for ci, cl in enumerate(calls) for ti in range(len(cl[2]))]
        LA = 4
        tb_loaded = set()
        nmask = [0]
        ebuf = {}

        def stage1(n):
            ci, ti = items[n]
            j, q0, tiles = calls[ci]
            if j not in tb_loaded:
                tb_loaded.add(j)
                if j == 0:
                    load_tables(0, 2)
                if j + 1 < 4:
                    load_tables(j + 1, 0)
            if n % 32 == 16 and j + 1 < 4:
                load_tables(j + 1, 1)
            TBB, TBBk = TBBs[j % 2], TBBks[j % 2]
            kc0, vt, bias = tiles[ti]
            sb_ = n % 4
            g.mm(PS[sb_][:], Kb.v[:, j, kc0:kc0 + 128], Qz.v[:, j, q0 // 256].rearrange("p h q -> p (h q)"), True, True,
                 Kb.k(j, e0=kc0, e1=kc0 + 128) + Qz.k(j, q0 // 256), psk(sb_))
            eb = EBf[n % 8]
            ebuf[n] = eb
            if bias is None:
                g.act(eb.v, PS[sb_][:], AF.Exp, psk(sb_), eb.k())
            else:
                fi_ = nmask[0] % 2
                nmask[0] += 1
                var, i0 = bias
                g.act(EFv[fi_], PS[sb_][:], AF.Exp, psk(sb_), EFk[fi_])
                g.tt(eb.v.rearrange("p (h q) -> p h q", h=2), EFv[fi_].rearrange("p (h q) -> p h q", h=2),
                     TBB[:, :, var, i0 * 64:(i0 + 4) * 64], ALU.mult, EFk[fi_] + TBBk, eb.k())

        def stage2(n):
            ci, ti = items[n]
            j, q0, tiles = calls[ci]
            kc0, vt, bias = tiles[ti]
            nt = len(tiles)
            po, pd = PS[4 + ci % 2], PS[6 + ci % 2]
            pok, pdk = psk(4 + ci % 2), psk(6 + ci % 2)
            eb = ebuf.pop(n)
            g.mm(pd[:], ONESB[:], eb.v, ti == 0, ti == nt - 1, ["ONESB"] + eb.k(), pdk, grp=False)
            g.mm(po[:], Vz.v[:, vt, j * 128:(j + 1) * 128], eb.v, ti == 0, ti == nt - 1, Vz.k(vt) + eb.k(), pok, grp=False)
            if ti == nt - 1:
                def fin(ci=ci, j=j, q0=q0, po=po, pd=pd, pok=pok, pdk=pdk):
                    rdv, rdk = RDv[ci % 2], RDk[ci % 2]
                    g.act(rdv, pd[:], AF.Ln, pdk, rdk)
                    g.act(rdv, rdv, AF.Exp, rdk, rdk, scale=-1.0)
                    ok_ = Ob.k(j, e0=q0, e1=q0 + 256)
                    g.tt(Ob.v[0:64, j, q0:q0 + 256], po[0:64, 0:256], rdv[0:64, 0:256], ALU.mult, pok + rdk, ok_)
                    g.tt(Ob.v[64:128, j, q0:q0 + 256], po[64:128, 256:512], rdv[64:128, 256:512], ALU.mult, pok + rdk, ok_)
                deferred.append([2, fin])

        deferred = []
        for n in range(len(items) + LA):
            if n < len(items):
                stage1(n)
            for dfr in list(deferred):
                dfr[0] -= 1
                if dfr[0] <= 0:
                    dfr[1]()
                    deferred.remove(dfr)
            if n >= LA:
                stage2(n - LA)
        for dfr in deferred:
            dfr[1]()

        if stage < 2.35:
            return
        YB = scr(0, [4, T], BF16)
        XB = scr(12, [T], F32)
        XC = scr(18, [T], F32)
        XCB = scr(24, [T], BF16)
        GG = scr(27, [T], F32)
        A_ = scr(33, [T], F32)
        TI = scr(39, [T], F32)
        T2 = scr(45, [T], F32)
        HS = [scr(64, [T], F32), scr(70, [T], F32)]
        wxb, wxk = load_slab(w2, 1536, 512)
        wgb, wgk = load_slab(w2, 2048, 512)
        tk = lambda s_, t: s_.k(e0=t * 512, e1=t * 512 + 512)
        for j in range(4):
            for t in range(NT):
                tok = slice(t * 512, (t + 1) * 512)
                pb = t % 4
                for k in range(8):
                    g.mm(PS[pb][:], wxb[:, k, j * 128:(j + 1) * 128], H[:, k, tok], k == 0, k == 7,
                         [wxk] + hk_all(t * 512, t * 512 + 512), psk(pb))
                g.cp(XB.v[:, tok], PS[pb][:], psk(pb), tk(XB, t), eng="act")
            cw = lambda jj: par("conv_w", jj * 4 + j)
            for (t0, L, n) in SEQS:
                sk_ = lambda s_, a, b: s_.k(e0=a, e1=b)
                g.ts(XC.v[:, t0:t0 + L], XB.v[:, t0:t0 + L], cw(1), par("conv_b", j), ALU.mult, ALU.add,
                     sk_(XB, t0, t0 + L) + ["PAR"], sk_(XC, t0, t0 + L))
                for (jj, so, do_, ln) in ((0, 0, 1, L - 1), (2, 1, 0, L - 1), (3, 2, 0, L - 2)):
                    g.stt(XC.v[:, t0 + do_:t0 + do_ + ln], XB.v[:, t0 + so:t0 + so + ln], cw(jj), XC.v[:, t0 + do_:t0 + do_ + ln],
                          ALU.mult, ALU.add, sk_(XB, t0, t0 + L) + sk_(XC, t0, t0 + L) + ["PAR"], sk_(XC, t0, t0 + L))
            g.cp(XCB.v, XC.v, XC.k(), XCB.k())
            for t in range(NT):
                tok = slice(t * 512, (t + 1) * 512)
                pb = 4 + t % 4
                for k in range(8):
                    g.mm(PS[pb][:], wgb[:, k, j * 128:(j + 1) * 128], H[:, k, tok], k == 0, k == 7,
                         [wgk] + hk_all(t * 512, t * 512 + 512), psk(pb))
                g.cp(GG.v[:, tok], PS[pb][:], psk(pb), tk(GG, t), eng="act")
                g.stt(T2.v[:, tok], GG.v[:, tok], 0.044715, GG.v[:, tok], ALU.mult, ALU.mult, tk(GG, t), tk(T2, t))
                g.stt(T2.v[:, tok], T2.v[:, tok], 1.0, GG.v[:, tok], ALU.add, ALU.mult, tk(T2, t) + tk(GG, t), tk(T2, t))
            for t in range(NT):
                tok = slice(t * 512, (t + 1) * 512)
                g.act(TI.v[:, tok], T2.v[:, tok], AF.Sigmoid, tk(T2, t), tk(TI, t), scale=1.5957691216)
                g.tt(GG.v[:, tok], TI.v[:, tok], GG.v[:, tok], ALU.mult, tk(TI, t) + tk(GG, t), tk(GG, t))
            toks = [slice(t * 512, (t + 1) * 512) for t in range(NT)]

            class _V:
                def __init__(s_, v, kf):
                    s_.v, s_.kf = v, kf
            wd_a = _V(wdv[0], lambda t: [("WD", 0, t)])
            wd_s = _V(wdv[1], lambda t: [("WD", 1, t)])
            sc_ = lambda b_: _V(b_.v, lambda t, b_=b_: tk(b_, t))
            Ad = [sc_(A_), wd_a]
            Sd = [sc_(T2), wd_s]
            Ud = [sc_(TI), sc_(XB)]
            for d in range(2):
                for t in range(NT):
                    g.mm(PS[t][:], BD[:, 0 * 8 + d * 4 + j, :], XCB.v[:, toks[t]], True, True, ["BD"] + tk(XCB, t), psk(t))
                    g.mm(PS[3 + t][:], BD[:, 1 * 8 + d * 4 + j, :], XCB.v[:, toks[t]], True, True, ["BD"] + tk(XCB, t), psk(3 + t))
                for t in range(NT):
                    g.act(Sd[d].v[:, toks[t]], PS[t][:], AF.Sigmoid, psk(t) + ["PAR"], Sd[d].kf(t), bias=par("b_r", d * 4 + j))
                for t in range(NT):
                    g.act(Ud[d].v[:, toks[t]], PS[3 + t][:], AF.Sigmoid, psk(3 + t) + ["PAR"], Ud[d].kf(t), bias=par("b_i", d * 4 + j))
            for d in range(2):
                for t in range(NT):
                    g.act(Ad[d].v[:, toks[t]], Sd[d].v[:, toks[t]], AF.Exp, Sd[d].kf(t) + ["SM"], Ad[d].kf(t), scale=C1(d, j))
                for t in range(NT):
                    g.act(Sd[d].v[:, toks[t]], Sd[d].v[:, toks[t]], AF.Exp, Sd[d].kf(t) + ["SM"], Sd[d].kf(t), scale=C2(d, j))
                    g.tt(Ud[d].v[:, toks[t]], Ud[d].v[:, toks[t]], XC.v[:, toks[t]], ALU.mult, Ud[d].kf(t) + tk(XC, t), Ud[d].kf(t))
            for d in range(2):
                for t in range(NT):
                    g.act(Sd[d].v[:, toks[t]], Sd[d].v[:, toks[t]], AF.Sqrt, Sd[d].kf(t) + ["SM"], Sd[d].kf(t), bias=SM[:, 41:42], scale=-1.0)
            for d in range(2):
                hs = HS[d]
                for t in range(NT):
                    g.tt(Ud[d].v[:, toks[t]], Ud[d].v[:, toks[t]], Sd[d].v[:, toks[t]], ALU.mult, Ud[d].kf(t) + Sd[d].kf(t), Ud[d].kf(t))
                for si, (t0, L, n) in enumerate(SEQS):
                    sl = slice(t0, t0 + L) if d == 0 else slice(t0 + L - 1, t0 - 1 if t0 > 0 else None, -1)
                    if n == 1:
                        init = par("stf" if d == 0 else "stb", j)
                    else:
                        init = 0.0
                    tiles_ = range(t0 // 512, (t0 + L - 1) // 512 + 1)
                    rk_ = [k_ for t in tiles_ for k_ in Ad[d].kf(t) + Ud[d].kf(t)] + ["PAR"]
                    g.P.op("dve", lambda e, sl=sl, init=init, hs=hs, d=d: e.tensor_tensor_scan(
                        out=hs.v[:, sl], data0=Ad[d].v[:, sl], data1=Ud[d].v[:, sl], initial=init, op0=ALU.mult, op1=ALU.add),
                        rk_, hs.k(e0=t0, e1=t0 + L))
                    if n == 0:
                        last = t0 + L - 1 if d == 0 else t0
                        col = d * 8 + si * 4 + j
                        g.cp(ST[:, col:col + 1], hs.v[:, last:last + 1], hs.k(e0=t0, e1=t0 + L), ["ST"])
            g.tt(HS[0].v, HS[0].v, HS[1].v, ALU.add, HS[0].k() + HS[1].k(), HS[0].k())
            g.tt(YB.v[:, j, :], HS[0].v, GG.v, ALU.mult, HS[0].k() + GG.k(), YB.k(j))

        if stage < 2.45:
            return
        def in_fn(k, t):
            b = Ob if k < 4 else YB
            return b.v[:, k % 4, t * 512:(t + 1) * 512], b.k(k % 4, e0=t * 512, e1=t * 512 + 512)
        out_proj(l, w_out_ab[0], in_fn, after_tile=(lambda t: norm_tile(l, 2, t, base=20, sq_on_act=True)) if stage >= 4 else None)

    g.ms(SM[:, 41:42], 1.0, ["SM"])

    def mix_fourier(l):
        CT = scr(0, [8, 1024], BF16)
        STt = scr(16, [8, 1024], BF16)
        AB = scr(32, [8, 4, 512], BF16)
        CS1 = scr(64, [2, 512], BF16)
        C256 = scr(66, [2, 256], BF16)
        S256 = scr(67, [2, 256], BF16)
        g.dma("pool", CS1.v, cs1_d.rearrange("(a p) f -> p a f", p=128), (), CS1.k())
        g.dma("pool", C256.v, ct256_d.rearrange("(a p) f -> p a f", p=128), (), C256.k())
        g.dma("pool", S256.v, st256_d.rearrange("(a p) f -> p a f", p=128), (), S256.k())
        g.dma("pool", CT.v, ct1k_d.rearrange("(a p) f -> p a f", p=128), (), CT.k())
        g.dma("pool", STt.v, st1k_d.rearrange("(a p) f -> p a f", p=128), (), STt.k())
        ev = [0]
        for (t0, L, n) in SEQS:
            ntt = L // 128
            ctab, stab = (C256, S256) if L == 256 else (CT, STt)
            for tt_ in range(ntt):
                for gi in range(4):
                    pb = ev[0] % 4
                    for cc in range(2):
                        g.mm(PS[pb][:], H[:, 2 * gi + cc, t0 + tt_ * 128:t0 + (tt_ + 1) * 128], CS1.v[:, cc, :], cc == 0, cc == 1,
                             kk("H", [2 * gi + cc], t0 + tt_ * 128, t0 + tt_ * 128 + 128) + CS1.k(), psk(pb))
                    g.cp(AB.v[:, tt_, gi, :], PS[pb][:], psk(pb), AB.k(tt_, gi), eng=("act" if ev[0] % 2 else "dve"))
                    ev[0] += 1
            N = min(L, 512)
            for gi in range(4):
                for cc in range(2):
                    for tb in range(L // N):
                        pb = 4 + ev[0] % 4
                        for tt_ in range(ntt):
                            g.mm(PS[pb][:, 0:N], AB.v[:, tt_, gi, cc * 128:(cc + 1) * 128], ctab.v[:, tt_, tb * N:(tb + 1) * N],
                                 tt_ == 0, False, AB.k(tt_, gi) + ctab.k(tt_), psk(pb))
                            g.mm(PS[pb][:, 0:N], AB.v[:, tt_, gi, 256 + cc * 128:256 + (cc + 1) * 128], stab.v[:, tt_, tb * N:(tb + 1) * N],
                                 False, tt_ == ntt - 1, AB.k(tt_, gi) + stab.k(tt_), psk(pb))
                        a0 = t0 + tb * N
                        g.cp(H[:, 2 * gi + cc, a0:a0 + N], PS[pb][:, 0:N], psk(pb), kk("H", [2 * gi + cc], a0, a0 + N),
                             eng=("act" if ev[0] % 2 else "dve"))
                        ev[0] += 1

        def in_fn(k, t):
            return H[:, k, t * 512:(t + 1) * 512], kk("H", [k], t * 512, t * 512 + 512)
        out_proj(l, w_out_c[0], in_fn, after_tile=(lambda t: norm_tile(l, 2, t, sq_on_act=True)) if stage >= 4 else None)

    def load_bd():
        for ti, wsrc in enumerate((lru_w_r, lru_w_i)):
            for d in range(2):
                for par_ in range(2):
                    pp = par_ * 64
                    g.dma("pool", BD[pp:pp + 64, ti * 8 + d * 4: ti * 8 + d * 4 + 4, pp:pp + 64],
                          wsrc[0, d, par_::2].rearrange("n i j -> i n j"), (), ["BD"])

    def mixer(l):
        if stage < 2.05:
            return
        norm_mod(l, 1, presum=True)
        if l == 0:
            load_bd()
            mix_ab(l)
        else:
            mix_fourier(l)

    bg = ada_gen(0)
    for _ in range(6):
        next(bg)
    for l in range(2):
        if stage < 10 and l == 1:
            break
        norm_mod(l, 0, presum=(l == 1))
        ffn(l, 0, fw["ffn1_gate"], fw["ffn1_up"], fw["ffn1_down"], bg=bg, next_norm=(stage >= 2.05))
        for _ in bg:
            pass
        dbg_dump()
        if stage < 2:
            break
        mixer(l)
        dbg_dump()
        if stage < 4:
            break
        bg = ada_gen(1) if l == 0 else iter(())
        ffn(l, 2, fw["ffn2_gate"], fw["ffn2_up"], fw["ffn2_down"], bg=bg, final=(l == 1), next_norm=(l == 0))
        if l == 0:
            for _ in range(6 - (18 - 19)):
                pass
        dbg_dump()

    YO = [scr(4 * i, [1024], F32) for i in range(2)]
    for tt_ in (range(12) if stage < 10 else ()):
        yo = YO[tt_ % 2].v
        yok = YO[tt_ % 2].k()
        for cg in range(2):
            pb = (tt_ * 2 + cg) % 4
            for ci in range(4):
                c = cg * 4 + ci
                g.tr(PS[pb][:, ci * 128:(ci + 1) * 128], X[:, c, tt_ * 128:(tt_ + 1) * 128], IDENT[:],
                     kk("X", [c], tt_ * 128, tt_ * 128 + 128) + ["IDENT"], psk(pb))
            g.cp(yo[:, cg * 512:(cg + 1) * 512], PS[pb][:], psk(pb), yok, eng=("dve" if cg else "act"))
        g.dma("sp", yout[tt_ * 128:(tt_ + 1) * 128, :], yo, yok, ())
    g.tr(PS[7][0:16, 0:128], ST[:, 0:16], IDENT[:], ["ST", "IDENT"], psk(7))
    STO = scr(8, [128], F32)
    g.cp(STO.v[0:16, :], PS[7][0:16, 0:128], psk(7), STO.k())
    g.dma("sp", nst_o, STO.v[0:16, :], STO.k(), ())
    P.finish()
    P.emit()
    g.es.close()
    return g


_CONST = {}


def _consts():
    if _CONST:
        return _CONST
    bf = ml_dtypes.bfloat16
    c = np.arange(256, dtype=np.float64)
    a = 2 * np.pi * np.outer(c, c) / 256.0
    _CONST["cs1"] = np.concatenate([np.cos(a), -np.sin(a)], axis=1).astype(np.float32) / 16.0
    _CONST["ct256"] = (np.cos(a) / 16.0).astype(np.float32)
    _CONST["st256"] = (np.sin(a) / 16.0).astype(np.float32)
    t = np.arange(1024, dtype=np.float64)
    a = 2 * np.pi * ((np.outer(t, t)) % 1024) / 1024.0
    _CONST["ct1k"] = (np.cos(a) / 32.0).astype(np.float32)
    _CONST["st1k"] = (np.sin(a) / 32.0).astype(np.float32)
    _CONST["ident"] = np.eye(128, dtype=np.float32)
    return _CONST


def _bias_table(rpb):
    r = np.asarray(rpb)[0]
    kc = np.arange(64)[:, None]
    qc = np.arange(64)[None, :]
    dc = np.clip(kc - qc + 15, 0, 30)
    cs = np.clip(qc - 8, 0, 48)
    col_in = (kc >= cs) & (kc < cs + 16)
    out = np.full((8, 2, 64, 2, 14, 64), -30000.0, np.float32)
    for par in range(2):
        for i in range(14):
            dr = 6 - i + par
            if abs(dr) > 7:
                continue
            vals = np.where(col_in[None], r[:, dr + 7][:, dc], np.float32(-30000.0))
            out[:, par, :, 0, i, :] = vals
            if -4 <= dr <= 3:
                out[:, par, :, 1, i, :] = vals
    return np.ascontiguousarray(out.reshape(8, 128, 2, 14 * 64))


_PROG = {}


def _get_prog(debug=0, stage=99):
    key = (debug, stage)
    if key not in _PROG:
        _PROG[key] = build(debug, stage)
    return _PROG[key]


def make_in_maps(inputs, cores):
    f = lambda a: np.ascontiguousarray(np.asarray(a, dtype=np.float32))
    shared = {k: f(inputs[k]) for k in ("w_ada", "b_ada", "norm_g", "ffn1_gate", "ffn1_up", "ffn1_down", "ffn2_gate",
                                         "ffn2_up", "ffn2_down", "w_in", "q_norm_g", "k_norm_g", "conv_w", "conv_b",
                                         "lru_w_r", "lru_b_r", "lru_w_i", "lru_b_i", "lru_lambda", "w_out_ab", "w_out_c")}
    shared.update(_consts())
    shared["btab"] = _bias_table(inputs["rpb"])
    xp, xs = f(inputs["x_prompt"]), f(inputs["x_sample"])
    maps = []
    for c in cores:
        s = c // 4
        m = dict(shared)
        m["xin"] = np.ascontiguousarray(np.concatenate([xp[2 * c], xp[2 * c + 1], xs[s]], axis=0))
        m["cond"] = np.ascontiguousarray(np.stack([f(inputs["c_ctx"]), f(inputs["c"])[s]], axis=0))
        m["ck"] = np.ascontiguousarray(f(inputs["cache_k"])[s, 0].reshape(256, 512))
        m["cv"] = np.ascontiguousarray(f(inputs["cache_v"])[s, 0].reshape(256, 512))
        m["stf"] = np.ascontiguousarray(f(inputs["state_lru_fwd"])[s, 0])
        m["stb"] = np.ascontiguousarray(f(inputs["state_lru_bwd"])[s, 0])
        maps.append(m)
    return maps


def kernel(**inputs):
    g = _get_prog()
    cores = list(range(8))
    res = run_bass_kernel_spmd(g.nc, make_in_maps(inputs, cores), core_ids=cores).results
    y_prompt = np.zeros((16, 256, 1024), np.float32)
    y_sample = np.zeros((2, 1024, 1024), np.float32)
    nk = np.zeros((16, 1, 256, 8, 64), np.float32)
    nv = np.zeros((16, 1, 256, 8, 64), np.float32)
    sf = np.zeros((16, 1, 512), np.float32)
    sbw = np.zeros((16, 1, 512), np.float32)
    for c in cores:
        r = res[c]
        yo = np.asarray(r["yout"])
        y_prompt[2 * c] = yo[0:256]
        y_prompt[2 * c + 1] = yo[256:512]
        if c % 4 == 0:
            y_sample[c // 4] = yo[512:]
        k_ = np.asarray(r["nk"]).reshape(2, 256, 8, 64)
        v_ = np.asarray(r["nv"]).reshape(2, 256, 8, 64)
        st = np.asarray(r["nst"]).reshape(2, 2, 512)
        for j in range(2):
            nk[2 * c + j, 0] = k_[j]
            nv[2 * c + j, 0] = v_[j]
            sf[2 * c + j, 0] = st[0, j]
            sbw[2 * c + j, 0] = st[1, j]
    return (y_prompt, y_sample, nk, nv, sf, sbw)
```

```python
import contextlib
import os
import numpy as np
import ml_dtypes
import concourse.bass as bass
import concourse.mybir as mybir
from concourse.bass_utils import run_bass_kernel_spmd

F32 = mybir.dt.float32
BF16 = mybir.dt.bfloat16
AF = mybir.ActivationFunctionType
ALU = mybir.AluOpType

N_DMA_SLOTS = 6


class Prog:
    COMPUTE = ("pe", "act", "dve", "pool")

    def __init__(self, nc):
        self.nc = nc
        self.streams = {e: [] for e in ("pe", "act", "dve", "pool", "sp")}
        self.cnt = {}
        self.seen = {e: {} for e in self.streams}
        self.state = {}
        self.dma_i = {"sp": 0, "pool": 0}
        self.sem_names = list(self.COMPUTE) + [f"{q}d{i}" for q in ("sp", "pool") for i in range(N_DMA_SLOTS)]
        for s in self.sem_names:
            self.cnt[s] = 0
        self.sems = {}
        self.nops = 0

    def _need(self, stream, waits, sem, val):
        if sem == "pe" and stream == "pe":
            return
        if self.seen[stream].get(sem, 0) >= val:
            return
        assert val <= self.cnt[sem], ("dependency on a silent op whose carrier is not issued yet", stream, sem, val)
        self.seen[stream][sem] = val
        waits[sem] = max(waits.get(sem, 0), val)

    def _deps(self, stream, reads, writes):
        waits = {}
        for k in reads:
            st = self.state.get(k)
            if st and st[0]:
                self._need(stream, waits, *st[0])
        for k in writes:
            st = self.state.get(k)
            if st:
                if st[0]:
                    self._need(stream, waits, *st[0])
                for s, v in st[1].items():
                    self._need(stream, waits, s, v)
        return waits

    def _commit(self, sem, val, reads, writes):
        for k in reads:
            st = self.state.setdefault(k, [None, {}])
            st[1][sem] = max(st[1].get(sem, 0), val)
        for k in writes:
            self.state[k] = [(sem, val), {}]

    def op(self, eng, fn, reads=(), writes=(), silent=False):
        waits = self._deps(eng, reads, writes)
        if silent:
            val = self.cnt[eng] + 1
        else:
            self.cnt[eng] += 1
            val = self.cnt[eng]
        self._commit(eng, val, reads, writes)
        self.streams[eng].append((waits, fn, eng, 0 if silent else 1))
        self.nops += 1

    def dma(self, q, fn, reads=(), writes=()):
        i = self.dma_i[q]
        self.dma_i[q] += 1
        sem = f"{q}d{i % N_DMA_SLOTS}"
        waits = self._deps(q, reads, writes)
        if self.cnt[sem] > 0:
            self._need(q, waits, sem, self.cnt[sem])
        self.cnt[sem] += 16
        val = self.cnt[sem]
        self._commit(sem, val, reads, writes)
        self.streams[q].append((waits, fn, sem, 16))
        self.nops += 1

    def finish(self):
        waits = {}
        for s in self.sem_names:
            if self.cnt[s] > 0:
                self._need("sp", waits, s, self.cnt[s])
        self.streams["sp"].append((waits, None, None, 0))

    def emit(self):
        nc = self.nc
        with contextlib.ExitStack() as es:
            for s in self.sem_names:
                self.sems[s] = es.enter_context(nc.semaphore(s))
            blk = es.enter_context(nc.Block())

            def run(stream):
                def body(e):
                    for waits, fn, sem, inc in self.streams[stream]:
                        for s, v in waits.items():
                            e.wait_ge(self.sems[s], v)
                        if fn is not None:
                            ins = fn(e)
                            if inc:
                                ins.then_inc(self.sems[sem], inc)
                return body

            blk.tensor(run("pe"))
            blk.scalar(run("act"))
            blk.vector(run("dve"))
            blk.gpsimd(run("pool"))
            blk.sync(run("sp"))


D = 1024
T = 1536
NT = 3
DFF = 2816
NF = 22
SEQS = ((0, 256, 0), (256, 256, 0), (512, 1024, 1))
EPS = 1e-6
HALVES = ((0, 12), (12, 10))


def _l(fn, *a, **k):
    return lambda e: fn(e, *a, **k)


class Gen:
    def __init__(self, debug=0):
        self.debug = debug
        self.nc = nc = bass.Bass("TRN2", target_bir_lowering=False)
        self.P = Prog(nc)
        self.es = contextlib.ExitStack()
        self.din = {}
        self.dout = {}

    def i(self, name, shape, dt=F32):
        ap = self.nc.dram_tensor(name, list(shape), dt, kind="ExternalInput").ap()
        self.din[name] = ap
        return ap

    def o(self, name, shape):
        ap = self.nc.dram_tensor(name, list(shape), F32, kind="ExternalOutput").ap()
        self.dout[name] = ap
        return ap

    def sb(self, name, shape, dt):
        return self.es.enter_context(self.nc.sbuf_tensor(name, list(shape), dt))

    def mm(self, out, lhsT, rhs, start, stop, r, w, grp=True):
        self.P.op("pe", lambda e: e.matmul(out, lhsT, rhs, start=start, stop=stop), r, w, silent=(grp and not stop))

    def tr(self, out, in_, ident, r, w):
        self.P.op("pe", lambda e: e.transpose(out, in_, ident), r, w)

    def act(self, out, in_, func, r, w, bias=None, scale=None):
        kw = {}
        if bias is not None:
            kw["bias"] = bias
        if scale is not None:
            kw["scale"] = scale
        self.P.op("act", lambda e: e.activation(out=out, in_=in_, func=func, **kw), r, w)

    def tt(self, out, in0, in1, op, r, w, eng="dve"):
        self.P.op(eng, lambda e: e.tensor_tensor(out=out, in0=in0, in1=in1, op=op), r, w)

    def ts(self, out, in0, s1, s2, op0, op1, r, w, eng="dve"):
        if s2 is None:
            self.P.op(eng, lambda e: e.tensor_scalar(out=out, in0=in0, scalar1=s1, scalar2=None, op0=op0), r, w)
        else:
            self.P.op(eng, lambda e: e.tensor_scalar(out=out, in0=in0, scalar1=s1, scalar2=s2, op0=op0, op1=op1), r, w)

    def stt(self, out, in0, scalar, in1, op0, op1, r, w, eng="dve"):
        self.P.op(eng, lambda e: e.scalar_tensor_tensor(out=out, in0=in0, scalar=scalar, in1=in1, op0=op0, op1=op1), r, w)

    def cp(self, out, in_, r, w, eng="dve"):
        if eng == "act":
            self.P.op("act", lambda e: e.copy(out=out, in_=in_), r, w)
        else:
            self.P.op(eng, lambda e: e.tensor_copy(out=out, in_=in_), r, w)

    def ms(self, ap, val, w, eng="dve"):
        self.P.op(eng, lambda e: e.memset(ap, val), (), w)

    def dma(self, q, out, in_, r, w, nc_ok=False):
        if nc_ok:
            self.P.dma(q, lambda e: e.dma_start(out=out, in_=in_, allow_slow_non_contiguous=True), r, w)
        else:
            self.P.dma(q, lambda e: e.dma_start(out=out, in_=in_), r, w)


def kk(name, cs, t0, t1):
    return [(name, c, b) for c in cs for b in range(t0 // 256, (t1 - 1) // 256 + 1)]


def build(debug=0, stage=99):
    g = Gen(debug)
    nc, P = g.nc, g.P
    xin = g.i("xin", [T, D])
    cond = g.i("cond", [2, D])
    ck_d = g.i("ck", [256, 512])
    cv_d = g.i("cv", [256, 512])
    stf_d = g.i("stf", [512])
    stb_d = g.i("stb", [512])
    w_ada = g.i("w_ada", [2, D, 9 * D])
    b_ada = g.i("b_ada", [2, 9 * D])
    norm_g = g.i("norm_g", [2, 3, D])
    fw = {}
    for nm in ("ffn1_gate", "ffn1_up", "ffn2_gate", "ffn2_up"):
        fw[nm] = g.i(nm, [2, D, DFF])
    for nm in ("ffn1_down", "ffn2_down"):
        fw[nm] = g.i(nm, [2, DFF, D])
    w_in = g.i("w_in", [1, D, 2560])
    qg_d = g.i("q_norm_g", [1, 64])
    kg_d = g.i("k_norm_g", [1, 64])
    btab_d = g.i("btab", [8, 128, 2, 14 * 64])
    conv_w = g.i("conv_w", [1, 4, 512])
    conv_b = g.i("conv_b", [1, 512])
    lru_w_r = g.i("lru_w_r", [1, 2, 8, 64, 64])
    lru_b_r = g.i("lru_b_r", [1, 2, 512])
    lru_w_i = g.i("lru_w_i", [1, 2, 8, 64, 64])
    lru_b_i = g.i("lru_b_i", [1, 2, 512])
    lru_lam = g.i("lru_lambda", [1, 2, 512])
    w_out_ab = g.i("w_out_ab", [1, D, D])
    w_out_c = g.i("w_out_c", [1, D, D])
    ident_d = g.i("ident", [128, 128])
    cs1_d = g.i("cs1", [256, 512])
    ct1k_d = g.i("ct1k", [1024, 1024])
    st1k_d = g.i("st1k", [1024, 1024])
    ct256_d = g.i("ct256", [256, 256])
    st256_d = g.i("st256", [256, 256])

    yout = g.o("yout", [T, D])
    nk_o = g.o("nk", [512, 512])
    nv_o = g.o("nv", [512, 512])
    nst_o = g.o("nst", [16, 128])
    if debug:
        dbg_o = g.o("dbg", [debug, 128, 8, T])

    X = g.sb("X", [128, 8, T], F32)
    H = g.sb("H", [128, 8, T], BF16)
    SCR = g.sb("SCR", [128, 38 * 1024], BF16)
    WG = [g.sb(f"WG{i}", [128, 16, 256], BF16) for i in range(2)]
    WD = [g.sb(f"WD{i}", [128, 12, 256], BF16) for i in range(2)]
    IDENT = g.sb("IDENT", [128, 128], F32)
    ONESB = g.sb("ONESB", [128, 128], BF16)
    BLK = g.sb("BLK", [128, 128], BF16)
    PAR = g.sb("PAR", [128, 384], F32)
    SC = g.sb("SC", [128, 8, 2], BF16)
    MOD = g.sb("MOD", [128, 2, 9, 8, 2], F32)
    COEF = g.sb("COEF", [128, 2, 3, 3, 8, 2], F32)
    SM = g.sb("SM", [128, 64], F32)
    ST = g.sb("ST", [128, 16], F32)
    BD = g.sb("BD", [128, 16, 128], BF16)
    PS = [g.es.enter_context(nc.psum_tensor(f"ps{i}", [128, 512], F32)) for i in range(8)]

    class Scr:
        def __init__(self, off_kib, shape, dt):
            self.off = int(off_kib * 1024)
            self.shape = list(shape)
            self.eb = 4 if dt == F32 else 2
            n = int(np.prod(shape))
            e0 = self.off // 2
            if dt == F32:
                v = SCR[:, e0: e0 + 2 * n].bitcast(F32)
            else:
                v = SCR[:, e0: e0 + n]
            if len(shape) > 1:
                names = " ".join(f"d{i}" for i in range(len(shape)))
                kw = {f"d{i}": s for i, s in enumerate(shape[:-1])}
                v = v.rearrange(f"p ({names}) -> p {names}", **kw)
            self.v = v

        def k(self, *idx, e0=0, e1=None):
            rest = int(np.prod(self.shape[len(idx):])) if len(idx) < len(self.shape) else 1
            base = 0
            for i, ix in enumerate(idx):
                base = base * self.shape[i] + ix
            base *= rest
            if e1 is None:
                e1 = rest
            b0 = self.off + (base + e0) * self.eb
            b1 = self.off + (base + e1) * self.eb
            return [("SCR", b) for b in range(b0 // 1024, (b1 - 1) // 1024 + 1)]

    def scr(off_kib, shape, dt):
        return Scr(off_kib, shape, dt)

    psk = lambda i: [("ps", i)]

    g.dma("sp", IDENT[:], ident_d, (), ["IDENT"])
    g.ms(ONESB[:], 1.0, ["ONESB"])
    g.ms(BLK[:], 0.0, ["BLK"])
    g.ms(BLK[0:64, 0:64], 1.0, ["BLK"])
    g.ms(BLK[64:128, 64:128], 1.0, ["BLK"])
    g.ms(BD[:], 0.0, ["BD"])
    g.ms(ST[:], 0.0, ["ST"])

    PSTs = scr(0, [3, 128], F32)
    PST = PSTs.v
    PSTk = PSTs.k()
    g.ms(PST, 0.0, PSTk)
    rows = {}
    r0 = [0]

    def prow(name, ap, n, tile=0):
        base = r0[0] if tile == 0 else 0
        g.dma("sp", PST[base:base + n, tile, :], ap, (), PSTk)
        rows[name] = tile * 128 + base
        if tile == 0:
            r0[0] += n

    prow("cond", cond.rearrange("n (c p) -> (n c) p", p=128), 16)
    prow("norm_g", norm_g.rearrange("l i (c p) -> (l i c) p", p=128), 48)
    prow("conv_w", conv_w[0].rearrange("j (c p) -> (j c) p", p=128), 16)
    prow("conv_b", conv_b[0].rearrange("(c p) -> c p", p=128), 4)
    prow("b_r", lru_b_r[0].rearrange("d (c p) -> (d c) p", p=128), 8)
    prow("b_i", lru_b_i[0].rearrange("d (c p) -> (d c) p", p=128), 8)
    prow("lam", lru_lam[0].rearrange("d (c p) -> (d c) p", p=128), 8)
    prow("stf", stf_d.rearrange("(c p) -> c p", p=128), 4)
    prow("stb", stb_d.rearrange("(c p) -> c p", p=128), 4)
    prow("b_ada0", b_ada[0].rearrange("(m p) -> m p", p=128), 72, tile=1)
    prow("b_ada1", b_ada[1].rearrange("(m p) -> m p", p=128), 72, tile=2)
    for t3 in range(3):
        g.tr(PS[7][:, t3 * 128:(t3 + 1) * 128], PST[:, t3, :], IDENT[:], PSTk + ["IDENT"], psk(7))
    g.cp(PAR[:], PS[7][:, 0:384], psk(7), ["PAR"])
    par = lambda name, k=0: PAR[:, rows[name] + k: rows[name] + k + 1]
    parn = lambda name, k, n: PAR[:, rows[name] + k: rows[name] + k + n]

    for hlf in range(2):
        g.dma("sp", SM[hlf * 64:(hlf + 1) * 64, 0:1], qg_d.rearrange("o d -> d o"), (), ["SM"], nc_ok=True)
        g.dma("sp", SM[hlf * 64:(hlf + 1) * 64, 1:2], kg_d.rearrange("o d -> d o"), (), ["SM"], nc_ok=True)
    g.ts(SM[:, 0:1], SM[:, 0:1], 0.125, None, ALU.mult, None, ["SM"], ["SM"])
    g.act(SM[:, 16:24], parn("lam", 0, 8), AF.Exp, ["PAR"], ["SM"], scale=-1.0)
    g.ts(SM[:, 24:32], SM[:, 16:24], 1.0, None, ALU.add, None, ["SM"], ["SM"])
    g.act(SM[:, 32:40], SM[:, 24:32], AF.Ln, ["SM"], ["SM"])
    g.ts(SM[:, 24:32], SM[:, 24:32], -1.0, 1e-30, ALU.add, ALU.max, ["SM"], ["SM"])
    g.P.op("dve", lambda e: e.reciprocal(out=SM[:, 24:32], in_=SM[:, 24:32]), ["SM"], ["SM"])
    g.tt(SM[:, 16:24], SM[:, 16:24], SM[:, 24:32], ALU.mult, ["SM"], ["SM"])
    g.tt(SM[:, 16:24], SM[:, 16:24], SM[:, 32:40], ALU.mult, ["SM"], ["SM"])
    g.ts(SM[:, 8:16], SM[:, 16:24], -8.0, None, ALU.mult, None, ["SM"], ["SM"])
    C1 = lambda d, c: SM[:, 8 + d * 4 + c: 9 + d * 4 + c]
    g.ts(SM[:, 48:56], SM[:, 8:16], 2.0, None, ALU.mult, None, ["SM"], ["SM"])
    C2 = lambda d, c: SM[:, 48 + d * 4 + c: 49 + d * 4 + c]

    for n in range(2):
        g.act(SC[:, :, n], parn("cond", n * 8, 8), AF.Silu, ["PAR"], ["SC"])

    XS = [scr(o_, [1024], F32) for o_ in (8, 12, 60, 64, 68, 72)]
    for tt_ in range(12):
        xs = XS[tt_ % 6].v
        xsk = XS[tt_ % 6].k()
        g.dma("sp", xs, xin[tt_ * 128:(tt_ + 1) * 128, :], (), xsk)
        for cg in range(2):
            pb = 4 + (tt_ * 2 + cg) % 4
            for ci in range(4):
                c = cg * 4 + ci
                g.tr(PS[pb][:, ci * 128:(ci + 1) * 128], xs[:, c * 128:(c + 1) * 128], IDENT[:],
                     xsk + ["IDENT"], psk(pb))
            g.cp(X[:, cg * 4:(cg + 1) * 4, tt_ * 128:(tt_ + 1) * 128],
                 PS[pb][:].rearrange("p (c t) -> p c t", c=4), psk(pb),
                 kk("X", range(cg * 4, cg * 4 + 4), tt_ * 128, tt_ * 128 + 128), eng=("dve" if cg else "act"))

    dbg_n = [0]

    def dbg_dump():
        if debug and dbg_n[0] < debug:
            g.dma("sp", dbg_o[dbg_n[0]], X[:], kk("X", range(8), 0, T), ())
            dbg_n[0] += 1

    wg_i = [0]
    wd_i = [0]

    def wg_next():
        i = wg_i[0] % 2
        wg_i[0] += 1
        return i

    def wd_next():
        i = wd_i[0] % 2
        wd_i[0] += 1
        return i

    def load_slab(dram2d, c0, ncols):
        i = wg_next()
        v = WG[i][:].rearrange("p a b -> p (a b)")[:, 0:8 * ncols].rearrange("p (k n) -> p k n", k=8)
        g.dma("pool", v, dram2d[:, c0:c0 + ncols].rearrange("(k p) n -> p k n", p=128), (), [("WG", i)])
        return v, ("WG", i)

    WA = [scr(60 + 8 * i, [8, 512], BF16) for i in range(2)]
    WAX = [scr(16 + 8 * i, [8, 512], BF16) for i in range(6)]

    def ada_gen(l):
        for s in range(18):
            wa = WAX[s] if (l == 0 and s < 6) else WA[s % 2]
            g.dma("pool", wa.v, w_ada[l][:, s * 512:(s + 1) * 512].rearrange("(k p) n -> p k n", p=128), (), wa.k())
            pb = 6 + s % 2
            for cc in range(4):
                for k in range(8):
                    g.mm(PS[pb][:, cc * 2:cc * 2 + 2], wa.v[:, k, cc * 128:(cc + 1) * 128], SC[:, k, :],
                         k == 0, k == 7, wa.k() + ["SC"], psk(pb))
            j, c0 = s // 2, (s % 2) * 4
            for n in range(2):
                g.tt(MOD[:, l, j, c0:c0 + 4, n], PS[pb][:, 0:8].rearrange("p (c n) -> p c n", n=2)[:, :, n],
                     parn(f"b_ada{l}", j * 8 + c0, 4), ALU.add, psk(pb) + ["PAR"], [("MOD", l, j)])
            if s % 6 == 5:
                sub = s // 6
                mk = [("MOD", l, 3 * sub + q) for q in range(3)]
                for n in range(2):
                    gn = parn("norm_g", l * 24 + sub * 8, 8)
                    g.stt(COEF[:, l, sub, 0, :, n], MOD[:, l, 3 * sub + 1, :, n], 1.0, gn, ALU.add, ALU.mult,
                          mk + ["PAR"], [("COEF", l, sub)])
                    g.cp(COEF[:, l, sub, 1, :, n], MOD[:, l, 3 * sub + 0, :, n], mk, [("COEF", l, sub)])
                    g.ts(COEF[:, l, sub, 2, :, n], MOD[:, l, 3 * sub + 2, :, n], (1.0 if sub == 1 else 0.5), None,
                         ALU.mult, None, mk, [("COEF", l, sub)])
            yield

    cf = lambda l, sub, which, c, n: COEF[:, l, sub, which, c, n:n + 1]


    def norm_mod(l, sub, presum=False):
        for t in range(NT):
            norm_tile(l, sub, t, presum=presum)

    def norm_tile(l, sub, t, base=0, sq_on_act=False, presum=False):
        SQ = scr(base, [8, 512], BF16)
        RS = [scr(base + 8 + 2 * i, [512], F32) for i in range(2)]
        TMP = [scr(base + 12 + 2 * i, [512], F32) for i in range(4)]
        if True:
            n = 0 if t == 0 else 1
            tok = slice(t * 512, (t + 1) * 512)
            for c in (() if presum else range(8)):
                xk = kk("X", [c], t * 512, t * 512 + 512)
                if sq_on_act:
                    g.act(SQ.v[:, c, :], X[:, c, tok], AF.Square, xk, SQ.k(c))
                else:
                    g.tt(SQ.v[:, c, :], X[:, c, tok], X[:, c, tok], ALU.mult, xk, SQ.k(c))
            pb = t if presum else 6 + t % 2
            for c in (() if presum else range(8)):
                g.mm(PS[pb][:], ONESB[:], SQ.v[:, c, :], c == 0, c == 7, ["ONESB"] + SQ.k(c), psk(pb))
            rs = RS[t % 2].v
            rsk = RS[t % 2].k()
            g.act(rs, PS[pb][:], AF.Ln, psk(pb) + ["SM"], rsk, bias=SM[:, 40:41], scale=1.0 / D)
            g.act(rs, rs, AF.Exp, rsk, rsk, scale=-0.5)
            for c in range(8):
                tm = TMP[c % 4].v
                tmk = TMP[c % 4].k()
                g.tt(tm, X[:, c, tok], rs, ALU.mult, kk("X", [c], t * 512, t * 512 + 512) + rsk,
                     tmk)
                g.act(H[:, c, tok], tm, AF.Identity, tmk + [("COEF", l, sub)], kk("H", [c], t * 512, t * 512 + 512),
                      bias=cf(l, sub, 1, c, n), scale=cf(l, sub, 0, c, n))

    g.ms(SM[:, 40:41], EPS, ["SM"])

    def ffn(l, sub, wgate, wup, wdown, bg=None, final=False, next_norm=False):
        HID = scr(20, [12, T], BF16)
        SG = [scr(56 + 2 * i, [512], F32) for i in range(2)]
        pair = [0]
        dbank = [0]
        fin_i = [0]
        YS = [scr(60 + 2 * i, [512], F32) for i in range(4)]
        SQN = [scr(i, [512], BF16) for i in range(4)]
        pend_sq = []

        pend_out = []

        def flush_sq():
            while pend_sq:
                t_, d_, sq_ = pend_sq.pop(0)
                g.mm(PS[t_][:], ONESB[:], sq_.v, d_ == 0, d_ == 7, ["ONESB"] + sq_.k(), psk(t_), grp=False)
            while pend_out:
                pend_out.pop(0)()

        for (f0, nf) in HALVES:
            for fg in range(nf // 2):
                i = wg_next()
                fa = f0 + fg * 2
                g.dma("pool", WG[i][:, 0:8, :], wgate[l][:, fa * 128:(fa + 2) * 128].rearrange("(k p) n -> p k n", p=128),
                      (), [("WG", i)])
                g.dma("pool", WG[i][:, 8:16, :], wup[l][:, fa * 128:(fa + 2) * 128].rearrange("(k p) n -> p k n", p=128),
                      (), [("WG", i)])
                for fi in range(2):
                    fl = fg * 2 + fi
                    for t in range(NT):
                        pg = (pair[0] % 2) * 2
                        pair[0] += 1
                        hk = kk("H", range(8), t * 512, t * 512 + 512)
                        for k in range(8):
                            g.mm(PS[pg][:], WG[i][:, k, fi * 128:(fi + 1) * 128], H[:, k, t * 512:(t + 1) * 512],
                                 k == 0, k == 7, [("WG", i)] + hk, psk(pg))
                        for k in range(8):
                            g.mm(PS[pg + 1][:], WG[i][:, 8 + k, fi * 128:(fi + 1) * 128], H[:, k, t * 512:(t + 1) * 512],
                                 k == 0, k == 7, [("WG", i)] + hk, psk(pg + 1))
                        sg = SG[(pair[0]) % 2].v
                        sgk = SG[(pair[0]) % 2].k()
                        g.act(sg, PS[pg][:], AF.Silu, psk(pg), sgk)
                        g.tt(HID.v[:, fl, t * 512:(t + 1) * 512], sg, PS[pg + 1][:], ALU.mult,
                             sgk + psk(pg + 1), HID.k(fl, e0=t * 512, e1=t * 512 + 512))
                if bg is not None:
                    next(bg, None)
            for dg in range(4):
                i = wd_next()
                g.dma("pool", WD[i][:, 0:nf, :],
                      wdown[l][f0 * 128:(f0 + nf) * 128, dg * 256:(dg + 1) * 256].rearrange("(f p) n -> p f n", p=128),
                      (), [("WD", i, s_) for s_ in range(3)])
                for di in range(2):
                    d = dg * 2 + di
                    for t in range(NT):
                        n = 0 if t == 0 else 1
                        pb = 4 + dbank[0] % 2
                        dbank[0] += 1
                        for f in range(nf):
                            g.mm(PS[pb][:], WD[i][:, f, di * 128:(di + 1) * 128], HID.v[:, f, t * 512:(t + 1) * 512],
                                 f == 0, f == nf - 1, [("WD", i, s_) for s_ in range(3)] + HID.k(f, e0=t * 512, e1=t * 512 + 512), psk(pb))
                        flush_sq()
                        xk = kk("X", [d], t * 512, t * 512 + 512)
                        g.stt(X[:, d, t * 512:(t + 1) * 512], PS[pb][:], cf(l, sub, 2, d, n), X[:, d, t * 512:(t + 1) * 512],
                              ALU.mult, ALU.add, psk(pb) + xk + [("COEF", l, sub)], xk)
                        if next_norm and f0 > 0:
                            sq_ = SQN[(d * NT + t) % 4]
                            g.tt(sq_.v, X[:, d, t * 512:(t + 1) * 512], X[:, d, t * 512:(t + 1) * 512], ALU.mult, xk, sq_.k())
                            pend_sq.append((t, d, sq_))
                        if final and f0 > 0:
                            def emit_out(d=d, t=t, xk=xk):
                                oi = fin_i[0]
                                fin_i[0] += 1
                                po_ = 6 + oi % 2
                                ys = YS[oi % 4]
                                for a_ in range(4):
                                    g.tr(PS[po_][:, a_ * 128:(a_ + 1) * 128], X[:, d, t * 512 + a_ * 128:t * 512 + (a_ + 1) * 128],
                                         IDENT[:], xk + ["IDENT"], psk(po_))
                                g.cp(ys.v, PS[po_][:], psk(po_), ys.k(), eng="act")
                                g.dma("sp", yout[t * 512:(t + 1) * 512, d * 128:(d + 1) * 128].rearrange("(a p) f -> p a f", p=128),
                                      ys.v.rearrange("p (a f) -> p a f", a=4), ys.k(), ())
                            pend_out.append(emit_out)
                if bg is not None:
                    next(bg, None)

        flush_sq()

    def out_proj(l, wdram, in_fn, after_tile=None):
        bank = [0]
        slabs = [load_slab(wdram, dg * 512, 512) for dg in range(2)]
        for t in range(NT):
            n = 0 if t == 0 else 1
            for dg in range(2):
                wv, wk = slabs[dg]
                for di in range(4):
                    d = dg * 4 + di
                    pb = bank[0] % 6
                    bank[0] += 1
                    for k in range(8):
                        a, ks = in_fn(k, t)
                        g.mm(PS[pb][:], wv[:, k, di * 128:(di + 1) * 128], a, k == 0, k == 7, [wk] + ks, psk(pb))
                    xk = kk("X", [d], t * 512, t * 512 + 512)
                    g.stt(X[:, d, t * 512:(t + 1) * 512], PS[pb][:], cf(l, 1, 2, d, n), X[:, d, t * 512:(t + 1) * 512],
                          ALU.mult, ALU.add, psk(pb) + xk + [("COEF", l, 1)], xk)
            if after_tile is not None:
                after_tile(t)

    def mix_ab(l):
        Vz = scr(0, [14, 512], BF16)
        Qz = scr(14, [4, 6, 2, 256], BF16)
        Kb = scr(38, [4, T + 256], BF16)
        Ob = scr(52, [4, T], BF16)
        KOUT = scr(52, [4, 512], F32)
        CK = scr(60, [2, 512], F32)
        SQq = [scr(64 + i, [512], BF16) for i in range(2)]
        RSq = [scr(66 + 2 * i, [512], F32) for i in range(2)]
        QRAW = [scr(70 + 2 * i, [512], F32) for i in range(2)]
        wdv = [WD[i][:].rearrange("p a b -> p (a b)").bitcast(F32) for i in range(2)]
        wdk = lambda i, s: [("WD", i, s)]
        KFv, KFk = wdv[0][:, 0:512], wdk(0, 0)
        VFv = [wdv[1][:, 0:512], wdv[1][:, 512:1024]]
        VFk = [wdk(1, 0), wdk(1, 1)]
        hk_all = lambda t0, t1: kk("H", range(8), t0, t1)
        w2 = w_in[0]

        for j_ in range(4):
            g.ms(Qz.v[64:128, j_, :, 0, :], 0.0, Qz.k(j_))
            g.ms(Qz.v[0:64, j_, :, 1, :], 0.0, Qz.k(j_))

        wv, wk = load_slab(w2, 1024, 512)
        for tt_ in range(12):
            pb = tt_ % 4
            for k in range(8):
                g.mm(PS[pb][:], H[:, k, tt_ * 128:(tt_ + 1) * 128], wv[:, k, :], k == 0, k == 7,
                     [wk] + hk_all(tt_ * 128, tt_ * 128 + 128), psk(pb))
            if tt_ < 4:
                g.cp(VFv[tt_ % 2], PS[pb][:], psk(pb), VFk[tt_ % 2])
                g.cp(Vz.v[:, tt_, :], VFv[tt_ % 2], VFk[tt_ % 2], Vz.k(tt_), eng="act")
                g.dma("sp", nv_o[tt_ * 128:(tt_ + 1) * 128, :], VFv[tt_ % 2], VFk[tt_ % 2], ())
            else:
                g.cp(Vz.v[:, tt_, :], PS[pb][:], psk(pb), Vz.k(tt_), eng="act")
        g.dma("pool", Vz.v[:, 12:14, :], cv_d.rearrange("(a p) f -> p a f", p=128), (), Vz.k(12) + Vz.k(13))

        g.dma("sp", CK.v, ck_d.rearrange("(a p) f -> p a f", p=128), (), CK.k())
        for j in range(4):
            pb = 4 + j % 2
            for a in range(2):
                g.tr(PS[pb][:, a * 128:(a + 1) * 128], CK.v[:, a, j * 128:(j + 1) * 128], IDENT[:], CK.k() + ["IDENT"], psk(pb))
            g.cp(Kb.v[:, j, T:T + 256], PS[pb][:, 0:256], psk(pb), Kb.k(j, e0=T, e1=T + 256))

        slabs_qk = [load_slab(w2, 0, 512), load_slab(w2, 512, 512)]
        its = [(which, j, t) for which in range(2) for j in range(4) for t in range(NT)]

        def qk_a(n):
            which, j, t = its[n]
            wv, wk = slabs_qk[which]
            tok = slice(t * 512, (t + 1) * 512)
            pb = n % 4
            for k in range(8):
                g.mm(PS[pb][:], wv[:, k, j * 128:(j + 1) * 128], H[:, k, tok], k == 0, k == 7,
                     [wk] + hk_all(t * 512, t * 512 + 512), psk(pb))
            sq = SQq[n % 2]
            g.act(sq.v, PS[pb][:], AF.Square, psk(pb), sq.k())
            p2 = 4 + n % 2
            g.mm(PS[p2][:], BLK[:], sq.v, True, True, ["BLK"] + sq.k(), psk(p2))

        def qk_b(n):
            which, j, t = its[n]
            tok = slice(t * 512, (t + 1) * 512)
            pb, p2 = n % 4, 4 + n % 2
            rs = RSq[n % 2]
            g.act(rs.v, PS[p2][:], AF.Ln, psk(p2) + ["SM"], rs.k(), bias=SM[:, 40:41], scale=1.0 / 64)
            g.act(rs.v, rs.v, AF.Exp, rs.k(), rs.k(), scale=-0.5)
            rd_ = psk(pb) + ["SM"] + rs.k()
            if which == 0:
                for hh in range(2):
                    ps_ = slice(hh * 64, hh * 64 + 64)
                    g.stt(Qz.v[ps_, j, 2 * t:2 * t + 2, hh, :], PS[pb][ps_, :].rearrange("p (b q) -> p b q", b=2), SM[ps_, 0:1],
                          rs.v[ps_, :].rearrange("p (b q) -> p b q", b=2), ALU.mult, ALU.mult,
                          rd_, Qz.k(j, 2 * t) + Qz.k(j, 2 * t + 1))
            elif t == 0:
                g.stt(KFv, PS[pb][:], SM[:, 1:2], rs.v, ALU.mult, ALU.mult, rd_, KFk)
                g.cp(Kb.v[:, j, 0:512], KFv, KFk, Kb.k(j, e0=0, e1=512), eng="act")
                p3 = 6 + j % 2
                for a in range(4):
                    g.tr(PS[p3][:, a * 128:(a + 1) * 128], KFv[:, a * 128:(a + 1) * 128], IDENT[:], KFk + ["IDENT"], psk(p3))
                g.cp(KOUT.v[:, :, j * 128:(j + 1) * 128], PS[p3][:].rearrange("p (a f) -> p a f", a=4), psk(p3), KOUT.k())
            else:
                g.stt(Kb.v[:, j, tok], PS[pb][:], SM[:, 1:2], rs.v, ALU.mult, ALU.mult, rd_,
                      Kb.k(j, e0=t * 512, e1=t * 512 + 512))

        for n in range(len(its) + 1):
            if n < len(its):
                qk_a(n)
            if n >= 1:
                qk_b(n - 1)
        g.dma("sp", nk_o.rearrange("(a p) f -> p a f", p=128), KOUT.v, KOUT.k(), ())

        EBf = [scr(64 + i, [512], BF16) for i in range(8)]
        EFv = [wdv[0][:, 0:512], wdv[0][:, 512:1024]]
        EFk = [wdk(0, 0), wdk(0, 1)]
        RDv = [wdv[1][:, 0:512], wdv[1][:, 512:1024]]
        RDk = [wdk(1, 0), wdk(1, 1)]
        TBBs = [WG[i][:].rearrange("p a b -> p (a b)")[:, 0:3584].rearrange("p (h v e) -> p h v e", h=2, v=2) for i in range(2)]
        TBBks = [[("WG", 0)], [("WG", 1)]]

        def load_tables(j, part):
            tbb, tk_ = TBBs[j % 2], TBBks[j % 2]
            if part in (0, 2):
                for hh in range(2):
                    g.dma("pool", tbb[:, hh, :, :], btab_d[2 * j + hh], (), tk_)
            if part in (1, 2):
                g.act(tbb, tbb, AF.Exp, tk_, tk_)
        LOCAL = ((0, 4), (0, 6), (2, 8), (4, 8))
        calls = []
        for j in range(4):
            pc = [(j, t0, [(t0 + a * 128, t0 // 128 + a, None) for a in range(2)]) for (t0, L, n) in SEQS[:2]]
            sc_ = []
            for c in range(4):
                tl = []
                for jt in range(*LOCAL[c]):
                    tl.append((512 + jt * 128, 4 + jt, (0 if c in (0, 3) else 1, 6 - (2 * jt - 4 * c))))
                for a in range(2):
                    tl.append((T + a * 128, 12 + a, None))
                sc_.append((j, 512 + c * 256, tl))
            calls += [sc_[0], pc[0], sc_[1], sc_[2], pc[1], sc_[3]]
        items = [(ci, ti) for ci, cl in enumerate(calls) for ti in range(len(cl[2]))]
        LA = 4
        tb_loaded = set()
        nmask = [0]
        ebuf = {}

        def stage1(n):
            ci, ti = items[n]
            j, q0, tiles = calls[ci]
            if j not in tb_loaded:
                tb_loaded.add(j)
                if j == 0:
                    load_tables(0, 2)
                if j + 1 < 4:
                    load_tables(j + 1, 0)
            if n % 32 == 16 and j + 1 < 4:
                load_tables(j + 1, 1)
            TBB, TBBk = TBBs[j % 2], TBBks[j % 2]
            kc0, vt, bias = tiles[ti]
            sb_ = n % 4
            g.mm(PS[sb_][:], Kb.v[:, j, kc0:kc0 + 128], Qz.v[:, j, q0 // 256].rearrange("p h q -> p (h q)"), True, True,
                 Kb.k(j, e0=kc0, e1=kc0 + 128) + Qz.k(j, q0 // 256), psk(sb_))
            eb = EBf[n % 8]
            ebuf[n] = eb
            if bias is None:
                g.act(eb.v, PS[sb_][:], AF.Exp, psk(sb_), eb.k())
            else:
                fi_ = nmask[0] % 2
                nmask[0] += 1
                var, i0 = bias
                g.act(EFv[fi_], PS[sb_][:], AF.Exp, psk(sb_), EFk[fi_])
                g.tt(eb.v.rearrange("p (h q) -> p h q", h=2), EFv[fi_].rearrange("p (h q) -> p h q", h=2),
                     TBB[:, :, var, i0 * 64:(i0 + 4) * 64], ALU.mult, EFk[fi_] + TBBk, eb.k())

        def stage2(n):
            ci, ti = items[n]
            j, q0, tiles = calls[ci]
            kc0, vt, bias = tiles[ti]
            nt = len(tiles)
            po, pd = PS[4 + ci % 2], PS[6 + ci % 2]
            pok, pdk = psk(4 + ci % 2), psk(6 + ci % 2)
            eb = ebuf.pop(n)
            g.mm(pd[:], ONESB[:], eb.v, ti == 0, ti == nt - 1, ["ONESB"] + eb.k(), pdk, grp=False)
            g.mm(po[:], Vz.v[:, vt, j * 128:(j + 1) * 128], eb.v, ti == 0, ti == nt - 1, Vz.k(vt) + eb.k(), pok, grp=False)
            if ti == nt - 1:
                def fin(ci=ci, j=j, q0=q0, po=po, pd=pd, pok=pok, pdk=pdk):
                    rdv, rdk = RDv[ci % 2], RDk[ci % 2]
                    g.act(rdv, pd[:], AF.Ln, pdk, rdk)
                    g.act(rdv, rdv, AF.Exp, rdk, rdk, scale=-1.0)
                    ok_ = Ob.k(j, e0=q0, e1=q0 + 256)
                    g.tt(Ob.v[0:64, j, q0:q0 + 256], po[0:64, 0:256], rdv[0:64, 0:256], ALU.mult, pok + rdk, ok_)
                    g.tt(Ob.v[64:128, j, q0:q0 + 256], po[64:128, 256:512], rdv[64:128, 256:512], ALU.mult, pok + rdk, ok_)
                deferred.append([1, fin])

        deferred = []
        for n in range(len(items) + LA):
            if n < len(items):
                stage1(n)
            for dfr in list(deferred):
                dfr[0] -= 1
                if dfr[0] <= 0:
                    dfr[1]()
                    deferred.remove(dfr)
            if n >= LA:
                stage2(n - LA)
        for dfr in deferred:
            dfr[1]()

        if stage < 2.35:
            return
        YB = scr(0, [4, T], BF16)
        XB = scr(12, [T], F32)
        XC = scr(18, [T], F32)
        XCB = scr(24, [T], BF16)
        GG = scr(27, [T], F32)
        A_ = scr(33, [T], F32)
        TI = scr(39, [T], F32)
        T2 = scr(45, [T], F32)
        HS = [scr(64, [T], F32), scr(70, [T], F32)]
        wxb, wxk = load_slab(w2, 1536, 512)
        wgb, wgk = load_slab(w2, 2048, 512)
        tk = lambda s_, t: s_.k(e0=t * 512, e1=t * 512 + 512)
        for j in range(4):
            for t in range(NT):
                tok = slice(t * 512, (t + 1) * 512)
                pb = t % 4
                for k in range(8):
                    g.mm(PS[pb][:], wxb[:, k, j * 128:(j + 1) * 128], H[:, k, tok], k == 0, k == 7,
                         [wxk] + hk_all(t * 512, t * 512 + 512), psk(pb))
                g.cp(XB.v[:, tok], PS[pb][:], psk(pb), tk(XB, t), eng="act")
            cw = lambda jj: par("conv_w", jj * 4 + j)
            for (t0, L, n) in SEQS:
                sk_ = lambda s_, a, b: s_.k(e0=a, e1=b)
                g.ts(XC.v[:, t0:t0 + L], XB.v[:, t0:t0 + L], cw(1), par("conv_b", j), ALU.mult, ALU.add,
                     sk_(XB, t0, t0 + L) + ["PAR"], sk_(XC, t0, t0 + L))
                for (jj, so, do_, ln) in ((0, 0, 1, L - 1), (2, 1, 0, L - 1), (3, 2, 0, L - 2)):
                    g.stt(XC.v[:, t0 + do_:t0 + do_ + ln], XB.v[:, t0 + so:t0 + so + ln], cw(jj), XC.v[:, t0 + do_:t0 + do_ + ln],
                          ALU.mult, ALU.add, sk_(XB, t0, t0 + L) + sk_(XC, t0, t0 + L) + ["PAR"], sk_(XC, t0, t0 + L))
            g.cp(XCB.v, XC.v, XC.k(), XCB.k())
            for t in range(NT):
                tok = slice(t * 512, (t + 1) * 512)
                pb = 4 + t % 4
                for k in range(8):
                    g.mm(PS[pb][:], wgb[:, k, j * 128:(j + 1) * 128], H[:, k, tok], k == 0, k == 7,
                         [wgk] + hk_all(t * 512, t * 512 + 512), psk(pb))
                g.cp(GG.v[:, tok], PS[pb][:], psk(pb), tk(GG, t), eng="act")
                g.stt(T2.v[:, tok], GG.v[:, tok], 0.044715, GG.v[:, tok], ALU.mult, ALU.mult, tk(GG, t), tk(T2, t))
                g.stt(T2.v[:, tok], T2.v[:, tok], 1.0, GG.v[:, tok], ALU.add, ALU.mult, tk(T2, t) + tk(GG, t), tk(T2, t))
            for t in range(NT):
                tok = slice(t * 512, (t + 1) * 512)
                g.act(TI.v[:, tok], T2.v[:, tok], AF.Sigmoid, tk(T2, t), tk(TI, t), scale=1.5957691216)
                g.tt(GG.v[:, tok], TI.v[:, tok], GG.v[:, tok], ALU.mult, tk(TI, t) + tk(GG, t), tk(GG, t))
            toks = [slice(t * 512, (t + 1) * 512) for t in range(NT)]

            class _V:
                def __init__(s_, v, kf):
                    s_.v, s_.kf = v, kf
            wd_a = _V(wdv[0], lambda t: [("WD", 0, t)])
            wd_s = _V(wdv[1], lambda t: [("WD", 1, t)])
            sc_ = lambda b_: _V(b_.v, lambda t, b_=b_: tk(b_, t))
            Ad = [sc_(A_), wd_a]
            Sd = [sc_(T2), wd_s]
            Ud = [sc_(TI), sc_(XB)]
            for d in range(2):
                for t in range(NT):
                    g.mm(PS[t][:], BD[:, 0 * 8 + d * 4 + j, :], XCB.v[:, toks[t]], True, True, ["BD"] + tk(XCB, t), psk(t))
                    g.mm(PS[3 + t][:], BD[:, 1 * 8 + d * 4 + j, :], XCB.v[:, toks[t]], True, True, ["BD"] + tk(XCB, t), psk(3 + t))
                for t in range(NT):
                    g.act(Sd[d].v[:, toks[t]], PS[t][:], AF.Sigmoid, psk(t) + ["PAR"], Sd[d].kf(t), bias=par("b_r", d * 4 + j))
                for t in range(NT):
                    g.act(Ud[d].v[:, toks[t]], PS[3 + t][:], AF.Sigmoid, psk(3 + t) + ["PAR"], Ud[d].kf(t), bias=par("b_i", d * 4 + j))
            for d in range(2):
                for t in range(NT):
                    g.act(Ad[d].v[:, toks[t]], Sd[d].v[:, toks[t]], AF.Exp, Sd[d].kf(t) + ["SM"], Ad[d].kf(t), scale=C1(d, j))
                for t in range(NT):
                    g.act(Sd[d].v[:, toks[t]], Sd[d].v[:, toks[t]], AF.Exp, Sd[d].kf(t) + ["SM"], Sd[d].kf(t), scale=C2(d, j))
                    g.tt(Ud[d].v[:, toks[t]], Ud[d].v[:, toks[t]], XC.v[:, toks[t]], ALU.mult, Ud[d].kf(t) + tk(XC, t), Ud[d].kf(t))
            for d in range(2):
                for t in range(NT):
                    g.act(Sd[d].v[:, toks[t]], Sd[d].v[:, toks[t]], AF.Sqrt, Sd[d].kf(t) + ["SM"], Sd[d].kf(t), bias=SM[:, 41:42], scale=-1.0)
            for d in range(2):
                hs = HS[d]
                for t in range(NT):
                    g.tt(Ud[d].v[:, toks[t]], Ud[d].v[:, toks[t]], Sd[d].v[:, toks[t]], ALU.mult, Ud[d].kf(t) + Sd[d].kf(t), Ud[d].kf(t))
                for si, (t0, L, n) in enumerate(SEQS):
                    sl = slice(t0, t0 + L) if d == 0 else slice(t0 + L - 1, t0 - 1 if t0 > 0 else None, -1)
                    if n == 1:
                        init = par("stf" if d == 0 else "stb", j)
                    else:
                        init = 0.0
                    tiles_ = range(t0 // 512, (t0 + L - 1) // 512 + 1)
                    rk_ = [k_ for t in tiles_ for k_ in Ad[d].kf(t) + Ud[d].kf(t)] + ["PAR"]
                    g.P.op("dve", lambda e, sl=sl, init=init, hs=hs, d=d: e.tensor_tensor_scan(
                        out=hs.v[:, sl], data0=Ad[d].v[:, sl], data1=Ud[d].v[:, sl], initial=init, op0=ALU.mult, op1=ALU.add),
                        rk_, hs.k(e0=t0, e1=t0 + L))
                    if n == 0:
                        last = t0 + L - 1 if d == 0 else t0
                        col = d * 8 + si * 4 + j
                        g.cp(ST[:, col:col + 1], hs.v[:, last:last + 1], hs.k(e0=t0, e1=t0 + L), ["ST"])
            g.tt(HS[0].v, HS[0].v, HS[1].v, ALU.add, HS[0].k() + HS[1].k(), HS[0].k())
            g.tt(YB.v[:, j, :], HS[0].v, GG.v, ALU.mult, HS[0].k() + GG.k(), YB.k(j))

        if stage < 2.45:
            return
        def in_fn(k, t):
            b = Ob if k < 4 else YB
            return b.v[:, k % 4, t * 512:(t + 1) * 512], b.k(k % 4, e0=t * 512, e1=t * 512 + 512)
        out_proj(l, w_out_ab[0], in_fn, after_tile=(lambda t: norm_tile(l, 2, t, base=20, sq_on_act=True)) if stage >= 4 else None)

    g.ms(SM[:, 41:42], 1.0, ["SM"])

    def mix_fourier(l):
        CT = scr(0, [8, 1024], BF16)
        STt = scr(16, [8, 1024], BF16)
        AB = scr(32, [8, 4, 512], BF16)
        CS1 = scr(64, [2, 512], BF16)
        C256 = scr(66, [2, 256], BF16)
        S256 = scr(67, [2, 256], BF16)
        g.dma("pool", CS1.v, cs1_d.rearrange("(a p) f -> p a f", p=128), (), CS1.k())
        g.dma("pool", C256.v, ct256_d.rearrange("(a p) f -> p a f", p=128), (), C256.k())
        g.dma("pool", S256.v, st256_d.rearrange("(a p) f -> p a f", p=128), (), S256.k())
        g.dma("pool", CT.v, ct1k_d.rearrange("(a p) f -> p a f", p=128), (), CT.k())
        g.dma("pool", STt.v, st1k_d.rearrange("(a p) f -> p a f", p=128), (), STt.k())
        ev = [0]
        for (t0, L, n) in SEQS:
            ntt = L // 128
            ctab, stab = (C256, S256) if L == 256 else (CT, STt)
            for tt_ in range(ntt):
                for gi in range(4):
                    pb = ev[0] % 4
                    for cc in range(2):
                        g.mm(PS[pb][:], H[:, 2 * gi + cc, t0 + tt_ * 128:t0 + (tt_ + 1) * 128], CS1.v[:, cc, :], cc == 0, cc == 1,
                             kk("H", [2 * gi + cc], t0 + tt_ * 128, t0 + tt_ * 128 + 128) + CS1.k(), psk(pb))
                    g.cp(AB.v[:, tt_, gi, :], PS[pb][:], psk(pb), AB.k(tt_, gi), eng=("act" if ev[0] % 2 else "dve"))
                    ev[0] += 1
            N = min(L, 512)
            for gi in range(4):
                for cc in range(2):
                    for tb in range(L // N):
                        pb = 4 + ev[0] % 4
                        for tt_ in range(ntt):
                            g.mm(PS[pb][:, 0:N], AB.v[:, tt_, gi, cc * 128:(cc + 1) * 128], ctab.v[:, tt_, tb * N:(tb + 1) * N],
                                 tt_ == 0, False, AB.k(tt_, gi) + ctab.k(tt_), psk(pb))
                            g.mm(PS[pb][:, 0:N], AB.v[:, tt_, gi, 256 + cc * 128:256 + (cc + 1) * 128], stab.v[:, tt_, tb * N:(tb + 1) * N],
                                 False, tt_ == ntt - 1, AB.k(tt_, gi) + stab.k(tt_), psk(pb))
                        a0 = t0 + tb * N
                        g.cp(H[:, 2 * gi + cc, a0:a0 + N], PS[pb][:, 0:N], psk(pb), kk("H", [2 * gi + cc], a0, a0 + N),
                             eng=("act" if ev[0] % 2 else "dve"))
                        ev[0] += 1

        def in_fn(k, t):
            return H[:, k, t * 512:(t + 1) * 512], kk("H", [k], t * 512, t * 512 + 512)
        out_proj(l, w_out_c[0], in_fn, after_tile=(lambda t: norm_tile(l, 2, t, sq_on_act=True)) if stage >= 4 else None)

    def load_bd():
        for ti, wsrc in enumerate((lru_w_r, lru_w_i)):
            for d in range(2):
                for par_ in range(2):
                    pp = par_ * 64
                    g.dma("pool", BD[pp:pp + 64, ti * 8 + d * 4: ti * 8 + d * 4 + 4, pp:pp + 64],
                          wsrc[0, d, par_::2].rearrange("n i j -> i n j"), (), ["BD"])

    def mixer(l):
        if stage < 2.05:
            return
        norm_mod(l, 1, presum=True)
        if l == 0:
            load_bd()
            mix_ab(l)
        else:
            mix_fourier(l)

    bg = ada_gen(0)
    for _ in range(6):
        next(bg)
    for l in range(2):
        if stage < 10 and l == 1:
            break
        norm_mod(l, 0, presum=(l == 1))
        ffn(l, 0, fw["ffn1_gate"], fw["ffn1_up"], fw["ffn1_down"], bg=bg, next_norm=(stage >= 2.05))
        for _ in bg:
            pass
        dbg_dump()
        if stage < 2:
            break
        mixer(l)
        dbg_dump()
        if stage < 4:
            break
        bg = ada_gen(1) if l == 0 else iter(())
        ffn(l, 2, fw["ffn2_gate"], fw["ffn2_up"], fw["ffn2_down"], bg=bg, final=(l == 1), next_norm=(l == 0))
        if l == 0:
            for _ in range(6 - (18 - 19)):
                pass
        dbg_dump()

    YO = [scr(4 * i, [1024], F32) for i in range(2)]
    for tt_ in (range(12) if stage < 10 else ()):
        yo = YO[tt_ % 2].v
        yok = YO[tt_ % 2].k()
        for cg in range(2):
            pb = (tt_ * 2 + cg) % 4
            for ci in range(4):
                c = cg * 4 + ci
                g.tr(PS[pb][:, ci * 128:(ci + 1) * 128], X[:, c, tt_ * 128:(tt_ + 1) * 128], IDENT[:],
                     kk("X", [c], tt_ * 128, tt_ * 128 + 128) + ["IDENT"], psk(pb))
            g.cp(yo[:, cg * 512:(cg + 1) * 512], PS[pb][:], psk(pb), yok, eng=("dve" if cg else "act"))
        g.dma("sp", yout[tt_ * 128:(tt_ + 1) * 128, :], yo, yok, ())
    g.tr(PS[7][0:16, 0:128], ST[:, 0:16], IDENT[:], ["ST", "IDENT"], psk(7))
    STO = scr(8, [128], F32)
    g.cp(STO.v[0:16, :], PS[7][0:16, 0:128], psk(7), STO.k())
    g.dma("sp", nst_o, STO.v[0:16, :], STO.k(), ())
    P.finish()
    P.emit()
    g.es.close()
    return g


_CONST = {}


def _consts():
    if _CONST:
        return _CONST
    bf = ml_dtypes.bfloat16
    c = np.arange(256, dtype=np.float64)
    a = 2 * np.pi * np.outer(c, c) / 256.0
    _CONST["cs1"] = np.concatenate([np.cos(a), -np.sin(a)], axis=1).astype(np.float32) / 16.0
    _CONST["ct256"] = (np.cos(a) / 16.0).astype(np.float32)
    _CONST["st256"] = (np.sin(a) / 16.0).astype(np.float32)
    t = np.arange(1024, dtype=np.float64)
    a = 2 * np.pi * ((np.outer(t, t)) % 1024) / 1024.0
    _CONST["ct1k"] = (np.cos(a) / 32.0).astype(np.float32)
    _CONST["st1k"] = (np.sin(a) / 32.0).astype(np.float32)
    _CONST["ident"] = np.eye(128, dtype=np.float32)
    return _CONST


def _bias_table(rpb):
    r = np.asarray(rpb)[0]
    kc = np.arange(64)[:, None]
    qc = np.arange(64)[None, :]
    dc = np.clip(kc - qc + 15, 0, 30)
    cs = np.clip(qc - 8, 0, 48)
    col_in = (kc >= cs) & (kc < cs + 16)
    out = np.full((8, 2, 64, 2, 14, 64), -30000.0, np.float32)
    for par in range(2):
        for i in range(14):
            dr = 6 - i + par
            if abs(dr) > 7:
                continue
            vals = np.where(col_in[None], r[:, dr + 7][:, dc], np.float32(-30000.0))
            out[:, par, :, 0, i, :] = vals
            if -4 <= dr <= 3:
                out[:, par, :, 1, i, :] = vals
    return np.ascontiguousarray(out.reshape(8, 128, 2, 14 * 64))


_PROG = {}


def _get_prog(debug=0, stage=99):
    key = (debug, stage)
    if key not in _PROG:
        _PROG[key] = build(debug, stage)
    return _PROG[key]


def make_in_maps(inputs, cores):
    f = lambda a: np.ascontiguousarray(np.asarray(a, dtype=np.float32))
    shared = {k: f(inputs[k]) for k in ("w_ada", "b_ada", "norm_g", "ffn1_gate", "ffn1_up", "ffn1_down", "ffn2_gate",
                                         "ffn2_up", "ffn2_down", "w_in", "q_norm_g", "k_norm_g", "conv_w", "conv_b",
                                         "lru_w_r", "lru_b_r", "lru_w_i", "lru_b_i", "lru_lambda", "w_out_ab", "w_out_c")}
    shared.update(_consts())
    shared["btab"] = _bias_table(inputs["rpb"])
    xp, xs = f(inputs["x_prompt"]), f(inputs["x_sample"])
    maps = []
    for c in cores:
        s = c // 4
        m = dict(shared)
        m["xin"] = np.ascontiguousarray(np.concatenate([xp[2 * c], xp[2 * c + 1], xs[s]], axis=0))
        m["cond"] = np.ascontiguousarray(np.stack([f(inputs["c_ctx"]), f(inputs["c"])[s]], axis=0))
        m["ck"] = np.ascontiguousarray(f(inputs["cache_k"])[s, 0].reshape(256, 512))
        m["cv"] = np.ascontiguousarray(f(inputs["cache_v"])[s, 0].reshape(256, 512))
        m["stf"] = np.ascontiguousarray(f(inputs["state_lru_fwd"])[s, 0])
        m["stb"] = np.ascontiguousarray(f(inputs["state_lru_bwd"])[s, 0])
        maps.append(m)
    return maps


def kernel(**inputs):
    g = _get_prog()
    cores = list(range(8))
    res = run_bass_kernel_spmd(g.nc, make_in_maps(inputs, cores), core_ids=cores).results
    y_prompt = np.zeros((16, 256, 1024), np.float32)
    y_sample = np.zeros((2, 1024, 1024), np.float32)
    nk = np.zeros((16, 1, 256, 8, 64), np.float32)
    nv = np.zeros((16, 1, 256, 8, 64), np.float32)
    sf = np.zeros((16, 1, 512), np.float32)
    sbw = np.zeros((16, 1, 512), np.float32)
    for c in cores:
        r = res[c]
        yo = np.asarray(r["yout"])
        y_prompt[2 * c] = yo[0:256]
        y_prompt[2 * c + 1] = yo[256:512]
        if c % 4 == 0:
            y_sample[c // 4] = yo[512:]
        k_ = np.asarray(r["nk"]).reshape(2, 256, 8, 64)
        v_ = np.asarray(r["nv"]).reshape(2, 256, 8, 64)
        st = np.asarray(r["nst"]).reshape(2, 2, 512)
        for j in range(2):
            nk[2 * c + j, 0] = k_[j]
            nv[2 * c + j, 0] = v_[j]
            sf[2 * c + j, 0] = st[0, j]
            sbw[2 * c + j, 0] = st[1, j]
    return (y_prompt, y_sample, nk, nv, sf, sbw)
```

```python
import contextlib
import os
import numpy as np
import ml_dtypes
import concourse.bass as bass
import concourse.mybir as mybir
from concourse.bass_utils import run_bass_kernel_spmd

F32 = mybir.dt.float32
BF16 = mybir.dt.bfloat16
AF = mybir.ActivationFunctionType
ALU = mybir.AluOpType

N_DMA_SLOTS = 6


class Prog:
    COMPUTE = ("pe", "act", "dve", "pool")

    def __init__(self, nc):
        self.nc = nc
        self.streams = {e: [] for e in ("pe", "act", "dve", "pool", "sp")}
        self.cnt = {}
        self.seen = {e: {} for e in self.streams}
        self.state = {}
        self.dma_i = {"sp": 0, "pool": 0}
        self.sem_names = list(self.COMPUTE) + [f"{q}d{i}" for q in ("sp", "pool") for i in range(N_DMA_SLOTS)]
        for s in self.sem_names:
            self.cnt[s] = 0
        self.sems = {}
        self.nops = 0

    def _need(self, stream, waits, sem, val):
        if sem == "pe" and stream == "pe":
            return
        if self.seen[stream].get(sem, 0) >= val:
            return
        assert val <= self.cnt[sem], ("dependency on a silent op whose carrier is not issued yet", stream, sem, val)
        self.seen[stream][sem] = val
        waits[sem] = max(waits.get(sem, 0), val)

    def _deps(self, stream, reads, writes):
        waits = {}
        for k in reads:
            st = self.state.get(k)
            if st and st[0]:
                self._need(stream, waits, *st[0])
        for k in writes:
            st = self.state.get(k)
            if st:
                if st[0]:
                    self._need(stream, waits, *st[0])
                for s, v in st[1].items():
                    self._need(stream, waits, s, v)
        return waits

    def _commit(self, sem, val, reads, writes):
        for k in reads:
            st = self.state.setdefault(k, [None, {}])
            st[1][sem] = max(st[1].get(sem, 0), val)
        for k in writes:
            self.state[k] = [(sem, val), {}]

    def op(self, eng, fn, reads=(), writes=(), silent=False):
        waits = self._deps(eng, reads, writes)
        if silent:
            val = self.cnt[eng] + 1
        else:
            self.cnt[eng] += 1
            val = self.cnt[eng]
        self._commit(eng, val, reads, writes)
        self.streams[eng].append((waits, fn, eng, 0 if silent else 1))
        self.nops += 1

    def dma(self, q, fn, reads=(), writes=()):
        i = self.dma_i[q]
        self.dma_i[q] += 1
        sem = f"{q}d{i % N_DMA_SLOTS}"
        waits = self._deps(q, reads, writes)
        if self.cnt[sem] > 0:
            self._need(q, waits, sem, self.cnt[sem])
        self.cnt[sem] += 16
        val = self.cnt[sem]
        self._commit(sem, val, reads, writes)
        self.streams[q].append((waits, fn, sem, 16))
        self.nops += 1

    def finish(self):
        waits = {}
        for s in self.sem_names:
            if self.cnt[s] > 0:
                self._need("sp", waits, s, self.cnt[s])
        self.streams["sp"].append((waits, None, None, 0))

    def emit(self):
        nc = self.nc
        with contextlib.ExitStack() as es:
            for s in self.sem_names:
                self.sems[s] = es.enter_context(nc.semaphore(s))
            blk = es.enter_context(nc.Block())

            def run(stream):
                def body(e):
                    for waits, fn, sem, inc in self.streams[stream]:
                        for s, v in waits.items():
                            e.wait_ge(self.sems[s], v)
                        if fn is not None:
                            ins = fn(e)
                            if inc:
                                ins.then_inc(self.sems[sem], inc)
                return body

            blk.tensor(run("pe"))
            blk.scalar(run("act"))
            blk.vector(run("dve"))
            blk.gpsimd(run("pool"))
            blk.sync(run("sp"))


D = 1024
T = 1536
NT = 3
DFF = 2816
NF = 22
SEQS = ((0, 256, 0), (256, 256, 0), (512, 1024, 1))
EPS = 1e-6
HALVES = ((0, 12), (12, 10))


def _l(fn, *a, **k):
    return lambda e: fn(e, *a, **k)


class Gen:
    def __init__(self, debug=0):
        self.debug = debug
        self.nc = nc = bass.Bass("TRN2", target_bir_lowering=False)
        self.P = Prog(nc)
        self.es = contextlib.ExitStack()
        self.din = {}
        self.dout = {}

    def i(self, name, shape, dt=F32):
        ap = self.nc.dram_tensor(name, list(shape), dt, kind="ExternalInput").ap()
        self.din[name] = ap
        return ap

    def o(self, name, shape):
        ap = self.nc.dram_tensor(name, list(shape), F32, kind="ExternalOutput").ap()
        self.dout[name] = ap
        return ap

    def sb(self, name, shape, dt):
        return self.es.enter_context(self.nc.sbuf_tensor(name, list(shape), dt))

    def mm(self, out, lhsT, rhs, start, stop, r, w, grp=True):
        self.P.op("pe", lambda e: e.matmul(out, lhsT, rhs, start=start, stop=stop), r, w, silent=(grp and not stop))

    def tr(self, out, in_, ident, r, w):
        self.P.op("pe", lambda e: e.transpose(out, in_, ident), r, w)

    def act(self, out, in_, func, r, w, bias=None, scale=None):
        kw = {}
        if bias is not None:
            kw["bias"] = bias
        if scale is not None:
            kw["scale"] = scale
        self.P.op("act", lambda e: e.activation(out=out, in_=in_, func=func, **kw), r, w)

    def tt(self, out, in0, in1, op, r, w, eng="dve"):
        self.P.op(eng, lambda e: e.tensor_tensor(out=out, in0=in0, in1=in1, op=op), r, w)

    def ts(self, out, in0, s1, s2, op0, op1, r, w, eng="dve"):
        if s2 is None:
            self.P.op(eng, lambda e: e.tensor_scalar(out=out, in0=in0, scalar1=s1, scalar2=None, op0=op0), r, w)
        else:
            self.P.op(eng, lambda e: e.tensor_scalar(out=out, in0=in0, scalar1=s1, scalar2=s2, op0=op0, op1=op1), r, w)

    def stt(self, out, in0, scalar, in1, op0, op1, r, w, eng="dve"):
        self.P.op(eng, lambda e: e.scalar_tensor_tensor(out=out, in0=in0, scalar=scalar, in1=in1, op0=op0, op1=op1), r, w)

    def cp(self, out, in_, r, w, eng="dve"):
        if eng == "act":
            self.P.op("act", lambda e: e.copy(out=out, in_=in_), r, w)
        else:
            self.P.op(eng, lambda e: e.tensor_copy(out=out, in_=in_), r, w)

    def ms(self, ap, val, w, eng="dve"):
        self.P.op(eng, lambda e: e.memset(ap, val), (), w)

    def dma(self, q, out, in_, r, w, nc_ok=False):
        if nc_ok:
            self.P.dma(q, lambda e: e.dma_start(out=out, in_=in_, allow_slow_non_contiguous=True), r, w)
        else:
            self.P.dma(q, lambda e: e.dma_start(out=out, in_=in_), r, w)


def kk(name, cs, t0, t1):
    return [(name, c, b) for c in cs for b in range(t0 // 256, (t1 - 1) // 256 + 1)]


def build(debug=0, stage=99):
    g = Gen(debug)
    nc, P = g.nc, g.P
    xin = g.i("xin", [T, D])
    cond = g.i("cond", [2, D])
    ck_d = g.i("ck", [256, 512])
    cv_d = g.i("cv", [256, 512])
    stf_d = g.i("stf", [512])
    stb_d = g.i("stb", [512])
    w_ada = g.i("w_ada", [2, D, 9 * D])
    b_ada = g.i("b_ada", [2, 9 * D])
    norm_g = g.i("norm_g", [2, 3, D])
    fw = {}
    for nm in ("ffn1_gate", "ffn1_up", "ffn2_gate", "ffn2_up"):
        fw[nm] = g.i(nm, [2, D, DFF])
    for nm in ("ffn1_down", "ffn2_down"):
        fw[nm] = g.i(nm, [2, DFF, D])
    w_in = g.i("w_in", [1, D, 2560])
    qg_d = g.i("q_norm_g", [1, 64])
    kg_d = g.i("k_norm_g", [1, 64])
    btab_d = g.i("btab", [8, 128, 2, 14 * 64])
    conv_w = g.i("conv_w", [1, 4, 512])
    conv_b = g.i("conv_b", [1, 512])
    lru_w_r = g.i("lru_w_r", [1, 2, 8, 64, 64])
    lru_b_r = g.i("lru_b_r", [1, 2, 512])
    lru_w_i = g.i("lru_w_i", [1, 2, 8, 64, 64])
    lru_b_i = g.i("lru_b_i", [1, 2, 512])
    lru_lam = g.i("lru_lambda", [1, 2, 512])
    w_out_ab = g.i("w_out_ab", [1, D, D])
    w_out_c = g.i("w_out_c", [1, D, D])
    ident_d = g.i("ident", [128, 128])
    cs1_d = g.i("cs1", [256, 512])
    ct1k_d = g.i("ct1k", [1024, 1024])
    st1k_d = g.i("st1k", [1024, 1024])
    ct256_d = g.i("ct256", [256, 256])
    st256_d = g.i("st256", [256, 256])

    yout = g.o("yout", [T, D])
    nk_o = g.o("nk", [512, 512])
    nv_o = g.o("nv", [512, 512])
    nst_o = g.o("nst", [16, 128])
    if debug:
        dbg_o = g.o("dbg", [debug, 128, 8, T])

    X = g.sb("X", [128, 8, T], F32)
    H = g.sb("H", [128, 8, T], BF16)
    SCR = g.sb("SCR", [128, 38 * 1024], BF16)
    WG = [g.sb(f"WG{i}", [128, 16, 256], BF16) for i in range(2)]
    WD = [g.sb(f"WD{i}", [128, 12, 256], BF16) for i in range(2)]
    IDENT = g.sb("IDENT", [128, 128], F32)
    ONESB = g.sb("ONESB", [128, 128], BF16)
    BLK = g.sb("BLK", [128, 128], BF16)
    PAR = g.sb("PAR", [128, 384], F32)
    SC = g.sb("SC", [128, 8, 2], BF16)
    MOD = g.sb("MOD", [128, 2, 9, 8, 2], F32)
    COEF = g.sb("COEF", [128, 2, 3, 3, 8, 2], F32)
    SM = g.sb("SM", [128, 64], F32)
    ST = g.sb("ST", [128, 16], F32)
    BD = g.sb("BD", [128, 16, 128], BF16)
    PS = [g.es.enter_context(nc.psum_tensor(f"ps{i}", [128, 512], F32)) for i in range(8)]

    class Scr:
        def __init__(self, off_kib, shape, dt):
            self.off = int(off_kib * 1024)
            self.shape = list(shape)
            self.eb = 4 if dt == F32 else 2
            n = int(np.prod(shape))
            e0 = self.off // 2
            if dt == F32:
                v = SCR[:, e0: e0 + 2 * n].bitcast(F32)
            else:
                v = SCR[:, e0: e0 + n]
            if len(shape) > 1:
                names = " ".join(f"d{i}" for i in range(len(shape)))
                kw = {f"d{i}": s for i, s in enumerate(shape[:-1])}
                v = v.rearrange(f"p ({names}) -> p {names}", **kw)
            self.v = v

        def k(self, *idx, e0=0, e1=None):
            rest = int(np.prod(self.shape[len(idx):])) if len(idx) < len(self.shape) else 1
            base = 0
            for i, ix in enumerate(idx):
                base = base * self.shape[i] + ix
            base *= rest
            if e1 is None:
                e1 = rest
            b0 = self.off + (base + e0) * self.eb
            b1 = self.off + (base + e1) * self.eb
            return [("SCR", b) for b in range(b0 // 1024, (b1 - 1) // 1024 + 1)]

    def scr(off_kib, shape, dt):
        return Scr(off_kib, shape, dt)

    psk = lambda i: [("ps", i)]

    g.dma("sp", IDENT[:], ident_d, (), ["IDENT"])
    g.ms(ONESB[:], 1.0, ["ONESB"])
    g.ms(BLK[:], 0.0, ["BLK"])
    g.ms(BLK[0:64, 0:64], 1.0, ["BLK"])
    g.ms(BLK[64:128, 64:128], 1.0, ["BLK"])
    g.ms(BD[:], 0.0, ["BD"])
    g.ms(ST[:], 0.0, ["ST"])

    PSTs = scr(0, [3, 128], F32)
    PST = PSTs.v
    PSTk = PSTs.k()
    g.ms(PST, 0.0, PSTk)
    rows = {}
    r0 = [0]

    def prow(name, ap, n, tile=0):
        base = r0[0] if tile == 0 else 0
        g.dma("sp", PST[base:base + n, tile, :], ap, (), PSTk)
        rows[name] = tile * 128 + base
        if tile == 0:
            r0[0] += n

    prow("cond", cond.rearrange("n (c p) -> (n c) p", p=128), 16)
    prow("norm_g", norm_g.rearrange("l i (c p) -> (l i c) p", p=128), 48)
    prow("conv_w", conv_w[0].rearrange("j (c p) -> (j c) p", p=128), 16)
    prow("conv_b", conv_b[0].rearrange("(c p) -> c p", p=128), 4)
    prow("b_r", lru_b_r[0].rearrange("d (c p) -> (d c) p", p=128), 8)
    prow("b_i", lru_b_i[0].rearrange("d (c p) -> (d c) p", p=128), 8)
    prow("lam", lru_lam[0].rearrange("d (c p) -> (d c) p", p=128), 8)
    prow("stf", stf_d.rearrange("(c p) -> c p", p=128), 4)
    prow("stb", stb_d.rearrange("(c p) -> c p", p=128), 4)
    prow("b_ada0", b_ada[0].rearrange("(m p) -> m p", p=128), 72, tile=1)
    prow("b_ada1", b_ada[1].rearrange("(m p) -> m p", p=128), 72, tile=2)
    for t3 in range(3):
        g.tr(PS[7][:, t3 * 128:(t3 + 1) * 128], PST[:, t3, :], IDENT[:], PSTk + ["IDENT"], psk(7))
    g.cp(PAR[:], PS[7][:, 0:384], psk(7), ["PAR"])
    par = lambda name, k=0: PAR[:, rows[name] + k: rows[name] + k + 1]
    parn = lambda name, k, n: PAR[:, rows[name] + k: rows[name] + k + n]

    for hlf in range(2):
        g.dma("sp", SM[hlf * 64:(hlf + 1) * 64, 0:1], qg_d.rearrange("o d -> d o"), (), ["SM"], nc_ok=True)
        g.dma("sp", SM[hlf * 64:(hlf + 1) * 64, 1:2], kg_d.rearrange("o d -> d o"), (), ["SM"], nc_ok=True)
    g.ts(SM[:, 0:1], SM[:, 0:1], 0.125, None, ALU.mult, None, ["SM"], ["SM"])
    g.act(SM[:, 16:24], parn("lam", 0, 8), AF.Exp, ["PAR"], ["SM"], scale=-1.0)
    g.ts(SM[:, 24:32], SM[:, 16:24], 1.0, None, ALU.add, None, ["SM"], ["SM"])
    g.act(SM[:, 32:40], SM[:, 24:32], AF.Ln, ["SM"], ["SM"])
    g.ts(SM[:, 24:32], SM[:, 24:32], -1.0, 1e-30, ALU.add, ALU.max, ["SM"], ["SM"])
    g.P.op("dve", lambda e: e.reciprocal(out=SM[:, 24:32], in_=SM[:, 24:32]), ["SM"], ["SM"])
    g.tt(SM[:, 16:24], SM[:, 16:24], SM[:, 24:32], ALU.mult, ["SM"], ["SM"])
    g.tt(SM[:, 16:24], SM[:, 16:24], SM[:, 32:40], ALU.mult, ["SM"], ["SM"])
    g.ts(SM[:, 8:16], SM[:, 16:24], -8.0, None, ALU.mult, None, ["SM"], ["SM"])
    C1 = lambda d, c: SM[:, 8 + d * 4 + c: 9 + d * 4 + c]
    g.ts(SM[:, 48:56], SM[:, 8:16], 2.0, None, ALU.mult, None, ["SM"], ["SM"])
    C2 = lambda d, c: SM[:, 48 + d * 4 + c: 49 + d * 4 + c]

    for n in range(2):
        g.act(SC[:, :, n], parn("cond", n * 8, 8), AF.Silu, ["PAR"], ["SC"])

    XS = [scr(o_, [1024], F32) for o_ in (8, 12, 60, 64, 68, 72)]
    for tt_ in range(12):
        xs = XS[tt_ % 6].v
        xsk = XS[tt_ % 6].k()
        g.dma("sp", xs, xin[tt_ * 128:(tt_ + 1) * 128, :], (), xsk)
        for cg in range(2):
            pb = 4 + (tt_ * 2 + cg) % 4
            for ci in range(4):
                c = cg * 4 + ci
                g.tr(PS[pb][:, ci * 128:(ci + 1) * 128], xs[:, c * 128:(c + 1) * 128], IDENT[:],
                     xsk + ["IDENT"], psk(pb))
            g.cp(X[:, cg * 4:(cg + 1) * 4, tt_ * 128:(tt_ + 1) * 128],
                 PS[pb][:].rearrange("p (c t) -> p c t", c=4), psk(pb),
                 kk("X", range(cg * 4, cg * 4 + 4), tt_ * 128, tt_ * 128 + 128), eng=("dve" if cg else "act"))

    dbg_n = [0]

    def dbg_dump():
        if debug and dbg_n[0] < debug:
            g.dma("sp", dbg_o[dbg_n[0]], X[:], kk("X", range(8), 0, T), ())
            dbg_n[0] += 1

    wg_i = [0]
    wd_i = [0]

    def wg_next():
        i = wg_i[0] % 2
        wg_i[0] += 1
        return i

    def wd_next():
        i = wd_i[0] % 2
        wd_i[0] += 1
        return i

    def load_slab(dram2d, c0, ncols):
        i = wg_next()
        v = WG[i][:].rearrange("p a b -> p (a b)")[:, 0:8 * ncols].rearrange("p (k n) -> p k n", k=8)
        g.dma("pool", v, dram2d[:, c0:c0 + ncols].rearrange("(k p) n -> p k n", p=128), (), [("WG", i)])
        return v, ("WG", i)

    WA = [scr(60 + 8 * i, [8, 512], BF16) for i in range(2)]
    WAX = [scr(16 + 8 * i, [8, 512], BF16) for i in range(6)]

    def ada_gen(l):
        for s in range(18):
            wa = WAX[s] if (l == 0 and s < 6) else WA[s % 2]
            g.dma("pool", wa.v, w_ada[l][:, s * 512:(s + 1) * 512].rearrange("(k p) n -> p k n", p=128), (), wa.k())
            pb = 6 + s % 2
            for cc in range(4):
                for k in range(8):
                    g.mm(PS[pb][:, cc * 2:cc * 2 + 2], wa.v[:, k, cc * 128:(cc + 1) * 128], SC[:, k, :],
                         k == 0, k == 7, wa.k() + ["SC"], psk(pb))
            j, c0 = s // 2, (s % 2) * 4
            for n in range(2):
                g.tt(MOD[:, l, j, c0:c0 + 4, n], PS[pb][:, 0:8].rearrange("p (c n) -> p c n", n=2)[:, :, n],
                     parn(f"b_ada{l}", j * 8 + c0, 4), ALU.add, psk(pb) + ["PAR"], [("MOD", l, j)])
            if s % 6 == 5:
                sub = s // 6
                mk = [("MOD", l, 3 * sub + q) for q in range(3)]
                for n in range(2):
                    gn = parn("norm_g", l * 24 + sub * 8, 8)
                    g.stt(COEF[:, l, sub, 0, :, n], MOD[:, l, 3 * sub + 1, :, n], 1.0, gn, ALU.add, ALU.mult,
                          mk + ["PAR"], [("COEF", l, sub)])
                    g.cp(COEF[:, l, sub, 1, :, n], MOD[:, l, 3 * sub + 0, :, n], mk, [("COEF", l, sub)])
                    g.ts(COEF[:, l, sub, 2, :, n], MOD[:, l, 3 * sub + 2, :, n], (1.0 if sub == 1 else 0.5), None,
                         ALU.mult, None, mk, [("COEF", l, sub)])
            yield

    cf = lambda l, sub, which, c, n: COEF[:, l, sub, which, c, n:n + 1]


    def norm_mod(l, sub, presum=False):
        for t in range(NT):
            norm_tile(l, sub, t, presum=presum)

    def norm_tile(l, sub, t, base=0, sq_on_act=False, presum=False):
        SQ = scr(base, [8, 512], BF16)
        RS = [scr(base + 8 + 2 * i, [512], F32) for i in range(2)]
        TMP = [scr(base + 12 + 2 * i, [512], F32) for i in range(4)]
        if True:
            n = 0 if t == 0 else 1
            tok = slice(t * 512, (t + 1) * 512)
            for c in (() if presum else range(8)):
                xk = kk("X", [c], t * 512, t * 512 + 512)
                if sq_on_act:
                    g.act(SQ.v[:, c, :], X[:, c, tok], AF.Square, xk, SQ.k(c))
                else:
                    g.tt(SQ.v[:, c, :], X[:, c, tok], X[:, c, tok], ALU.mult, xk, SQ.k(c))
            pb = t if presum else 6 + t % 2
            for c in (() if presum else range(8)):
                g.mm(PS[pb][:], ONESB[:], SQ.v[:, c, :], c == 0, c == 7, ["ONESB"] + SQ.k(c), psk(pb))
            rs = RS[t % 2].v
            rsk = RS[t % 2].k()
            g.act(rs, PS[pb][:], AF.Ln, psk(pb) + ["SM"], rsk, bias=SM[:, 40:41], scale=1.0 / D)
            g.act(rs, rs, AF.Exp, rsk, rsk, scale=-0.5)
            for c in range(8):
                tm = TMP[c % 4].v
                tmk = TMP[c % 4].k()
                g.tt(tm, X[:, c, tok], rs, ALU.mult, kk("X", [c], t * 512, t * 512 + 512) + rsk,
                     tmk)
                g.act(H[:, c, tok], tm, AF.Identity, tmk + [("COEF", l, sub)], kk("H", [c], t * 512, t * 512 + 512),
                      bias=cf(l, sub, 1, c, n), scale=cf(l, sub, 0, c, n))

    g.ms(SM[:, 40:41], EPS, ["SM"])

    def ffn(l, sub, wgate, wup, wdown, bg=None, final=False, next_norm=False):
        HID = scr(20, [12, T], BF16)
        SG = [scr(56 + 2 * i, [512], F32) for i in range(2)]
        pair = [0]
        dbank = [0]
        fin_i = [0]
        YS = [scr(60 + 2 * i, [512], F32) for i in range(4)]
        SQN = [scr(i, [512], BF16) for i in range(4)]
        pend_sq = []

        pend_out = []

        def flush_sq():
            while pend_sq:
                t_, d_, sq_ = pend_sq.pop(0)
                g.mm(PS[t_][:], ONESB[:], sq_.v, d_ == 0, d_ == 7, ["ONESB"] + sq_.k(), psk(t_), grp=False)
            while pend_out:
                pend_out.pop(0)()

        for (f0, nf) in HALVES:
            for fg in range(nf // 2):
                i = wg_next()
                fa = f0 + fg * 2
                g.dma("pool", WG[i][:, 0:8, :], wgate[l][:, fa * 128:(fa + 2) * 128].rearrange("(k p) n -> p k n", p=128),
                      (), [("WG", i)])
                g.dma("pool", WG[i][:, 8:16, :], wup[l][:, fa * 128:(fa + 2) * 128].rearrange("(k p) n -> p k n", p=128),
                      (), [("WG", i)])
                for fi in range(2):
                    fl = fg * 2 + fi
                    for t in range(NT):
                        pg = (pair[0] % 2) * 2
                        pair[0] += 1
                        hk = kk("H", range(8), t * 512, t * 512 + 512)
                        for k in range(8):
                            g.mm(PS[pg][:], WG[i][:, k, fi * 128:(fi + 1) * 128], H[:, k, t * 512:(t + 1) * 512],
                                 k == 0, k == 7, [("WG", i)] + hk, psk(pg))
                        for k in range(8):
                            g.mm(PS[pg + 1][:], WG[i][:, 8 + k, fi * 128:(fi + 1) * 128], H[:, k, t * 512:(t + 1) * 512],
                                 k == 0, k == 7, [("WG", i)] + hk, psk(pg + 1))
                        sg = SG[(pair[0]) % 2].v
                        sgk = SG[(pair[0]) % 2].k()
                        g.act(sg, PS[pg][:], AF.Silu, psk(pg), sgk)
                        g.tt(HID.v[:, fl, t * 512:(t + 1) * 512], sg, PS[pg + 1][:], ALU.mult,
                             sgk + psk(pg + 1), HID.k(fl, e0=t * 512, e1=t * 512 + 512))
                if bg is not None:
                    next(bg, None)
            for dg in range(4):
                i = wd_next()
                g.dma("pool", WD[i][:, 0:nf, :],
                      wdown[l][f0 * 128:(f0 + nf) * 128, dg * 256:(dg + 1) * 256].rearrange("(f p) n -> p f n", p=128),
                      (), [("WD", i, s_) for s_ in range(3)])
                for di in range(2):
                    d = dg * 2 + di
                    for t in range(NT):
                        n = 0 if t == 0 else 1
                        pb = 4 + dbank[0] % 2
                        dbank[0] += 1
                        for f in range(nf):
                            g.mm(PS[pb][:], WD[i][:, f, di * 128:(di + 1) * 128], HID.v[:, f, t * 512:(t + 1) * 512],
                                 f == 0, f == nf - 1, [("WD", i, s_) for s_ in range(3)] + HID.k(f, e0=t * 512, e1=t * 512 + 512), psk(pb))
                        flush_sq()
                        xk = kk("X", [d], t * 512, t * 512 + 512)
                        g.stt(X[:, d, t * 512:(t + 1) * 512], PS[pb][:], cf(l, sub, 2, d, n), X[:, d, t * 512:(t + 1) * 512],
                              ALU.mult, ALU.add, psk(pb) + xk + [("COEF", l, sub)], xk)
                        if next_norm and f0 > 0:
                            sq_ = SQN[(d * NT + t) % 4]
                            g.tt(sq_.v, X[:, d, t * 512:(t + 1) * 512], X[:, d, t * 512:(t + 1) * 512], ALU.mult, xk, sq_.k())
                            pend_sq.append((t, d, sq_))
                        if final and f0 > 0:
                            def emit_out(d=d, t=t, xk=xk):
                                oi = fin_i[0]
                                fin_i[0] += 1
                                po_ = 6 + oi % 2
                                ys = YS[oi % 4]
                                for a_ in range(4):
                                    g.tr(PS[po_][:, a_ * 128:(a_ + 1) * 128], X[:, d, t * 512 + a_ * 128:t * 512 + (a_ + 1) * 128],
                                         IDENT[:], xk + ["IDENT"], psk(po_))
                                g.cp(ys.v, PS[po_][:], psk(po_), ys.k(), eng="act")
                                g.dma("sp", yout[t * 512:(t + 1) * 512, d * 128:(d + 1) * 128].rearrange("(a p) f -> p a f", p=128),
                                      ys.v.rearrange("p (a f) -> p a f", a=4), ys.k(), ())
                            pend_out.append(emit_out)
                if bg is not None:
                    next(bg, None)

        flush_sq()

    def out_proj(l, wdram, in_fn, after_tile=None):
        bank = [0]
        slabs = [load_slab(wdram, dg * 512, 512) for dg in range(2)]
        for t in range(NT):
            n = 0 if t == 0 else 1
            for dg in range(2):
                wv, wk = slabs[dg]
                for di in range(4):
                    d = dg * 4 + di
                    pb = bank[0] % 6
                    bank[0] += 1
                    for k in range(8):
                        a, ks = in_fn(k, t)
                        g.mm(PS[pb][:], wv[:, k, di * 128:(di + 1) * 128], a, k == 0, k == 7, [wk] + ks, psk(pb))
                    xk = kk("X", [d], t * 512, t * 512 + 512)
                    g.stt(X[:, d, t * 512:(t + 1) * 512], PS[pb][:], cf(l, 1, 2, d, n), X[:, d, t * 512:(t + 1) * 512],
                          ALU.mult, ALU.add, psk(pb) + xk + [("COEF", l, 1)], xk)
            if after_tile is not None:
                after_tile(t)

    def mix_ab(l):
        Vz = scr(0, [14, 512], BF16)
        Qz = scr(14, [4, 6, 2, 256], BF16)
        Kb = scr(38, [4, T + 256], BF16)
        Ob = scr(52, [4, T], BF16)
        KOUT = scr(52, [4, 512], F32)
        CK = scr(60, [2, 512], F32)
        SQq = [scr(64 + i, [512], BF16) for i in range(2)]
        RSq = [scr(66 + 2 * i, [512], F32) for i in range(2)]
        QRAW = [scr(70 + 2 * i, [512], F32) for i in range(2)]
        wdv = [WD[i][:].rearrange("p a b -> p (a b)").bitcast(F32) for i in range(2)]
        wdk = lambda i, s: [("WD", i, s)]
        KFv, KFk = wdv[0][:, 0:512], wdk(0, 0)
        VFv = [wdv[1][:, 0:512], wdv[1][:, 512:1024]]
        VFk = [wdk(1, 0), wdk(1, 1)]
        hk_all = lambda t0, t1: kk("H", range(8), t0, t1)
        w2 = w_in[0]

        for j_ in range(4):
            g.ms(Qz.v[64:128, j_, :, 0, :], 0.0, Qz.k(j_))
            g.ms(Qz.v[0:64, j_, :, 1, :], 0.0, Qz.k(j_))

        wv, wk = load_slab(w2, 1024, 512)
        for tt_ in range(12):
            pb = tt_ % 4
            for k in range(8):
                g.mm(PS[pb][:], H[:, k, tt_ * 128:(tt_ + 1) * 128], wv[:, k, :], k == 0, k == 7,
                     [wk] + hk_all(tt_ * 128, tt_ * 128 + 128), psk(pb))
            if tt_ < 4:
                g.cp(VFv[tt_ % 2], PS[pb][:], psk(pb), VFk[tt_ % 2])
                g.cp(Vz.v[:, tt_, :], VFv[tt_ % 2], VFk[tt_ % 2], Vz.k(tt_), eng="act")
                g.dma("sp", nv_o[tt_ * 128:(tt_ + 1) * 128, :], VFv[tt_ % 2], VFk[tt_ % 2], ())
            else:
                g.cp(Vz.v[:, tt_, :], PS[pb][:], psk(pb), Vz.k(tt_), eng="act")
        g.dma("pool", Vz.v[:, 12:14, :], cv_d.rearrange("(a p) f -> p a f", p=128), (), Vz.k(12) + Vz.k(13))

        g.dma("sp", CK.v, ck_d.rearrange("(a p) f -> p a f", p=128), (), CK.k())
        for j in range(4):
            pb = 4 + j % 2
            for a in range(2):
                g.tr(PS[pb][:, a * 128:(a + 1) * 128], CK.v[:, a, j * 128:(j + 1) * 128], IDENT[:], CK.k() + ["IDENT"], psk(pb))
            g.cp(Kb.v[:, j, T:T + 256], PS[pb][:, 0:256], psk(pb), Kb.k(j, e0=T, e1=T + 256))

        slabs_qk = [load_slab(w2, 0, 512), load_slab(w2, 512, 512)]
        its = [(which, j, t) for which in range(2) for j in range(4) for t in range(NT)]

        def qk_a(n):
            which, j, t = its[n]
            wv, wk = slabs_qk[which]
            tok = slice(t * 512, (t + 1) * 512)
            pb = n % 4
            for k in range(8):
                g.mm(PS[pb][:], wv[:, k, j * 128:(j + 1) * 128], H[:, k, tok], k == 0, k == 7,
                     [wk] + hk_all(t * 512, t * 512 + 512), psk(pb))
            sq = SQq[n % 2]
            g.act(sq.v, PS[pb][:], AF.Square, psk(pb), sq.k())
            p2 = 4 + n % 2
            g.mm(PS[p2][:], BLK[:], sq.v, True, True, ["BLK"] + sq.k(), psk(p2))

        def qk_b(n):
            which, j, t = its[n]
            tok = slice(t * 512, (t + 1) * 512)
            pb, p2 = n % 4, 4 + n % 2
            rs = RSq[n % 2]
            g.act(rs.v, PS[p2][:], AF.Ln, psk(p2) + ["SM"], rs.k(), bias=SM[:, 40:41], scale=1.0 / 64)
            g.act(rs.v, rs.v, AF.Exp, rs.k(), rs.k(), scale=-0.5)
            rd_ = psk(pb) + ["SM"] + rs.k()
            if which == 0:
                for hh in range(2):
                    ps_ = slice(hh * 64, hh * 64 + 64)
                    g.stt(Qz.v[ps_, j, 2 * t:2 * t + 2, hh, :], PS[pb][ps_, :].rearrange("p (b q) -> p b q", b=2), SM[ps_, 0:1],
                          rs.v[ps_, :].rearrange("p (b q) -> p b q", b=2), ALU.mult, ALU.mult,
                          rd_, Qz.k(j, 2 * t) + Qz.k(j, 2 * t + 1))
            elif t == 0:
                g.stt(KFv, PS[pb][:], SM[:, 1:2], rs.v, ALU.mult, ALU.mult, rd_, KFk)
                g.cp(Kb.v[:, j, 0:512], KFv, KFk, Kb.k(j, e0=0, e1=512), eng="act")
                p3 = 6 + j % 2
                for a in range(4):
                    g.tr(PS[p3][:, a * 128:(a + 1) * 128], KFv[:, a * 128:(a + 1) * 128], IDENT[:], KFk + ["IDENT"], psk(p3))
                g.cp(KOUT.v[:, :, j * 128:(j + 1) * 128], PS[p3][:].rearrange("p (a f) -> p a f", a=4), psk(p3), KOUT.k())
            else:
                g.stt(Kb.v[:, j, tok], PS[pb][:], SM[:, 1:2], rs.v, ALU.mult, ALU.mult, rd_,
                      Kb.k(j, e0=t * 512, e1=t * 512 + 512))

        for n in range(len(its) + 1):
            if n < len(its):
                qk_a(n)
            if n >= 1:
                qk_b(n - 1)
        g.dma("sp", nk_o.rearrange("(a p) f -> p a f", p=128), KOUT.v, KOUT.k(), ())

        EBf = [scr(64 + i, [512], BF16) for i in range(8)]
        EFv = [wdv[0][:, 0:512], wdv[0][:, 512:1024]]
        EFk = [wdk(0, 0), wdk(0, 1)]
        RDv = [wdv[1][:, 0:512], wdv[1][:, 512:1024]]
        RDk = [wdk(1, 0), wdk(1, 1)]
        TBBs = [WG[i][:].rearrange("p a b -> p (a b)")[:, 0:3584].rearrange("p (h v e) -> p h v e", h=2, v=2) for i in range(2)]
        TBBks = [[("WG", 0)], [("WG", 1)]]

        def load_tables(j, part):
            tbb, tk_ = TBBs[j % 2], TBBks[j % 2]
            if part in (0, 2):
                for hh in range(2):
                    g.dma("pool", tbb[:, hh, :, :], btab_d[2 * j + hh], (), tk_)
            if part in (1, 2):
                g.act(tbb, tbb, AF.Exp, tk_, tk_)
        LOCAL = ((0, 4), (0, 6), (2, 8), (4, 8))
        calls = []
        for j in range(4):
            pc = [(j, t0, [(t0 + a * 128, t0 // 128 + a, None) for a in range(2)]) for (t0, L, n) in SEQS[:2]]
            sc_ = []
            for c in range(4):
                tl = []
                for jt in range(*LOCAL[c]):
                    tl.append((512 + jt * 128, 4 + jt, (0 if c in (0, 3) else 1, 6 - (2 * jt - 4 * c))))
                for a in range(2):
                    tl.append((T + a * 128, 12 + a, None))
                sc_.append((j, 512 + c * 256, tl))
            calls += [sc_[0], pc[0], sc_[1], sc_[2], pc[1], sc_[3]]
        items = [(ci, ti) for ci, cl in enumerate(calls) for ti in range(len(cl[2]))]
        LA = 4
        tb_loaded = set()
        nmask = [0]
        ebuf = {}

        def stage1(n):
            ci, ti = items[n]
            j, q0, tiles = calls[ci]
            if j not in tb_loaded:
                tb_loaded.add(j)
                if j == 0:
                    load_tables(0, 2)
                if j + 1 < 4:
                    load_tables(j + 1, 0)
            if n % 32 == 16 and j + 1 < 4:
                load_tables(j + 1, 1)
            TBB, TBBk = TBBs[j % 2], TBBks[j % 2]
            kc0, vt, bias = tiles[ti]
            sb_ = n % 4
            g.mm(PS[sb_][:], Kb.v[:, j, kc0:kc0 + 128], Qz.v[:, j, q0 // 256].rearrange("p h q -> p (h q)"), True, True,
                 Kb.k(j, e0=kc0, e1=kc0 + 128) + Qz.k(j, q0 // 256), psk(sb_))
            eb = EBf[n % 8]
            ebuf[n] = eb
            if bias is None:
                g.act(eb.v, PS[sb_][:], AF.Exp, psk(sb_), eb.k())
            else:
                fi_ = nmask[0] % 2
                nmask[0] += 1
                var, i0 = bias
                g.act(EFv[fi_], PS[sb_][:], AF.Exp, psk(sb_), EFk[fi_])
                g.tt(eb.v.rearrange("p (h q) -> p h q", h=2), EFv[fi_].rearrange("p (h q) -> p h q", h=2),
                     TBB[:, :, var, i0 * 64:(i0 + 4) * 64], ALU.mult, EFk[fi_] + TBBk, eb.k())

        def stage2(n):
            ci, ti = items[n]
            j, q0, tiles = calls[ci]
            kc0, vt, bias = tiles[ti]
            nt = len(tiles)
            po, pd = PS[4 + ci % 2], PS[6 + ci % 2]
            pok, pdk = psk(4 + ci % 2), psk(6 + ci % 2)
            eb = ebuf.pop(n)
            g.mm(pd[:], ONESB[:], eb.v, ti == 0, ti == nt - 1, ["ONESB"] + eb.k(), pdk, grp=False)
            g.mm(po[:], Vz.v[:, vt, j * 128:(j + 1) * 128], eb.v, ti == 0, ti == nt - 1, Vz.k(vt) + eb.k(), pok, grp=False)
            if ti == nt - 1:
                def fin(ci=ci, j=j, q0=q0, po=po, pd=pd, pok=pok, pdk=pdk):
                    rdv, rdk = RDv[ci % 2], RDk[ci % 2]
                    g.act(rdv, pd[:], AF.Ln, pdk, rdk)
                    g.act(rdv, rdv, AF.Exp, rdk, rdk, scale=-1.0)
                    ok_ = Ob.k(j, e0=q0, e1=q0 + 256)
                    g.tt(Ob.v[0:64, j, q0:q0 + 256], po[0:64, 0:256], rdv[0:64, 0:256], ALU.mult, pok + rdk, ok_)
                    g.tt(Ob.v[64:128, j, q0:q0 + 256], po[64:128, 256:512], rdv[64:128, 256:512], ALU.mult, pok + rdk, ok_)
                fin()

        deferred = []
        for n in range(len(items) + LA):
            if n < len(items):
                stage1(n)
            for dfr in list(deferred):
                dfr[0] -= 1
                if dfr[0] <= 0:
                    dfr[1]()
                    deferred.remove(dfr)
            if n >= LA:
                stage2(n - LA)
        for dfr in deferred:
            dfr[1]()

        if stage < 2.35:
            return
        YB = scr(0, [4, T], BF16)
        XB = scr(12, [T], F32)
        XC = scr(18, [T], F32)
        XCB = scr(24, [T], BF16)
        GG = scr(27, [T], F32)
        A_ = scr(33, [T], F32)
        TI = scr(39, [T], F32)
        T2 = scr(45, [T], F32)
        HS = [scr(64, [T], F32), scr(70, [T], F32)]
        wxb, wxk = load_slab(w2, 1536, 512)
        wgb, wgk = load_slab(w2, 2048, 512)
        tk = lambda s_, t: s_.k(e0=t * 512, e1=t * 512 + 512)
        for j in range(4):
            for t in range(NT):
                tok = slice(t * 512, (t + 1) * 512)
                pb = t % 4
                for k in range(8):
                    g.mm(PS[pb][:], wxb[:, k, j * 128:(j + 1) * 128], H[:, k, tok], k == 0, k == 7,
                         [wxk] + hk_all(t * 512, t * 512 + 512), psk(pb))
                g.cp(XB.v[:, tok], PS[pb][:], psk(pb), tk(XB, t), eng="act")
            cw = lambda jj: par("conv_w", jj * 4 + j)
            for (t0, L, n) in SEQS:
                sk_ = lambda s_, a, b: s_.k(e0=a, e1=b)
                g.ts(XC.v[:, t0:t0 + L], XB.v[:, t0:t0 + L], cw(1), par("conv_b", j), ALU.mult, ALU.add,
                     sk_(XB, t0, t0 + L) + ["PAR"], sk_(XC, t0, t0 + L))
                for (jj, so, do_, ln) in ((0, 0, 1, L - 1), (2, 1, 0, L - 1), (3, 2, 0, L - 2)):
                    g.stt(XC.v[:, t0 + do_:t0 + do_ + ln], XB.v[:, t0 + so:t0 + so + ln], cw(jj), XC.v[:, t0 + do_:t0 + do_ + ln],
                          ALU.mult, ALU.add, sk_(XB, t0, t0 + L) + sk_(XC, t0, t0 + L) + ["PAR"], sk_(XC, t0, t0 + L))
            g.cp(XCB.v, XC.v, XC.k(), XCB.k())
            for t in range(NT):
                tok = slice(t * 512, (t + 1) * 512)
                pb = 4 + t % 4
                for k in range(8):
                    g.mm(PS[pb][:], wgb[:, k, j * 128:(j + 1) * 128], H[:, k, tok], k == 0, k == 7,
                         [wgk] + hk_all(t * 512, t * 512 + 512), psk(pb))
                g.cp(GG.v[:, tok], PS[pb][:], psk(pb), tk(GG, t), eng="act")
                g.stt(T2.v[:, tok], GG.v[:, tok], 0.044715, GG.v[:, tok], ALU.mult, ALU.mult, tk(GG, t), tk(T2, t))
                g.stt(T2.v[:, tok], T2.v[:, tok], 1.0, GG.v[:, tok], ALU.add, ALU.mult, tk(T2, t) + tk(GG, t), tk(T2, t))
            for t in range(NT):
                tok = slice(t * 512, (t + 1) * 512)
                g.act(TI.v[:, tok], T2.v[:, tok], AF.Sigmoid, tk(T2, t), tk(TI, t), scale=1.5957691216)
                g.tt(GG.v[:, tok], TI.v[:, tok], GG.v[:, tok], ALU.mult, tk(TI, t) + tk(GG, t), tk(GG, t))
            toks = [slice(t * 512, (t + 1) * 512) for t in range(NT)]

            class _V:
                def __init__(s_, v, kf):
                    s_.v, s_.kf = v, kf
            wd_a = _V(wdv[0], lambda t: [("WD", 0, t)])
            wd_s = _V(wdv[1], lambda t: [("WD", 1, t)])
            sc_ = lambda b_: _V(b_.v, lambda t, b_=b_: tk(b_, t))
            Ad = [sc_(A_), wd_a]
            Sd = [sc_(T2), wd_s]
            Ud = [sc_(TI), sc_(XB)]
            for d in range(2):
                for t in range(NT):
                    g.mm(PS[t][:], BD[:, 0 * 8 + d * 4 + j, :], XCB.v[:, toks[t]], True, True, ["BD"] + tk(XCB, t), psk(t))
                    g.mm(PS[3 + t][:], BD[:, 1 * 8 + d * 4 + j, :], XCB.v[:, toks[t]], True, True, ["BD"] + tk(XCB, t), psk(3 + t))
                for t in range(NT):
                    g.act(Sd[d].v[:, toks[t]], PS[t][:], AF.Sigmoid, psk(t) + ["PAR"], Sd[d].kf(t), bias=par("b_r", d * 4 + j))
                for t in range(NT):
                    g.act(Ud[d].v[:, toks[t]], PS[3 + t][:], AF.Sigmoid, psk(3 + t) + ["PAR"], Ud[d].kf(t), bias=par("b_i", d * 4 + j))
            for d in range(2):
                for t in range(NT):
                    g.act(Ad[d].v[:, toks[t]], Sd[d].v[:, toks[t]], AF.Exp, Sd[d].kf(t) + ["SM"], Ad[d].kf(t), scale=C1(d, j))
                for t in range(NT):
                    g.act(Sd[d].v[:, toks[t]], Sd[d].v[:, toks[t]], AF.Exp, Sd[d].kf(t) + ["SM"], Sd[d].kf(t), scale=C2(d, j))
                    g.tt(Ud[d].v[:, toks[t]], Ud[d].v[:, toks[t]], XC.v[:, toks[t]], ALU.mult, Ud[d].kf(t) + tk(XC, t), Ud[d].kf(t))
            for d in range(2):
                for t in range(NT):
                    g.act(Sd[d].v[:, toks[t]], Sd[d].v[:, toks[t]], AF.Sqrt, Sd[d].kf(t) + ["SM"], Sd[d].kf(t), bias=SM[:, 41:42], scale=-1.0)
            for d in range(2):
                hs = HS[d]
                for t in range(NT):
                    g.tt(Ud[d].v[:, toks[t]], Ud[d].v[:, toks[t]], Sd[d].v[:, toks[t]], ALU.mult, Ud[d].kf(t) + Sd[d].kf(t), Ud[d].kf(t))
                for si, (t0, L, n) in enumerate(SEQS):
                    sl = slice(t0, t0 + L) if d == 0 else slice(t0 + L - 1, t0 - 1 if t0 > 0 else None, -1)
                    if n == 1:
                        init = par("stf" if d == 0 else "stb", j)
                    else:
                        init = 0.0
                    tiles_ = range(t0 // 512, (t0 + L - 1) // 512 + 1)
                    rk_ = [k_ for t in tiles_ for k_ in Ad[d].kf(t) + Ud[d].kf(t)] + ["PAR"]
                    g.P.op("dve", lambda e, sl=sl, init=init, hs=hs, d=d: e.tensor_tensor_scan(
                        out=hs.v[:, sl], data0=Ad[d].v[:, sl], data1=Ud[d].v[:, sl], initial=init, op0=ALU.mult, op1=ALU.add),
                        rk_, hs.k(e0=t0, e1=t0 + L))
                    if n == 0:
                        last = t0 + L - 1 if d == 0 else t0
                        col = d * 8 + si * 4 + j
                        g.cp(ST[:, col:col + 1], hs.v[:, last:last + 1], hs.k(e0=t0, e1=t0 + L), ["ST"])
            g.tt(HS[0].v, HS[0].v, HS[1].v, ALU.add, HS[0].k() + HS[1].k(), HS[0].k())
            g.tt(YB.v[:, j, :], HS[0].v, GG.v, ALU.mult, HS[0].k() + GG.k(), YB.k(j))

        if stage < 2.45:
            return
        def in_fn(k, t):
            b = Ob if k < 4 else YB
            return b.v[:, k % 4, t * 512:(t + 1) * 512], b.k(k % 4, e0=t * 512, e1=t * 512 + 512)
        out_proj(l, w_out_ab[0], in_fn, after_tile=(lambda t: norm_tile(l, 2, t, base=20, sq_on_act=True)) if stage >= 4 else None)

    g.ms(SM[:, 41:42], 1.0, ["SM"])

    def mix_fourier(l):
        CT = scr(0, [8, 1024], BF16)
        STt = scr(16, [8, 1024], BF16)
        AB = scr(32, [8, 4, 512], BF16)
        CS1 = scr(64, [2, 512], BF16)
        C256 = scr(66, [2, 256], BF16)
        S256 = scr(67, [2, 256], BF16)
        g.dma("pool", CS1.v, cs1_d.rearrange("(a p) f -> p a f", p=128), (), CS1.k())
        g.dma("pool", C256.v, ct256_d.rearrange("(a p) f -> p a f", p=128), (), C256.k())
        g.dma("pool", S256.v, st256_d.rearrange("(a p) f -> p a f", p=128), (), S256.k())
        g.dma("pool", CT.v, ct1k_d.rearrange("(a p) f -> p a f", p=128), (), CT.k())
        g.dma("pool", STt.v, st1k_d.rearrange("(a p) f -> p a f", p=128), (), STt.k())
        ev = [0]
        for (t0, L, n) in SEQS:
            ntt = L // 128
            ctab, stab = (C256, S256) if L == 256 else (CT, STt)
            for tt_ in range(ntt):
                for gi in range(4):
                    pb = ev[0] % 4
                    for cc in range(2):
                        g.mm(PS[pb][:], H[:, 2 * gi + cc, t0 + tt_ * 128:t0 + (tt_ + 1) * 128], CS1.v[:, cc, :], cc == 0, cc == 1,
                             kk("H", [2 * gi + cc], t0 + tt_ * 128, t0 + tt_ * 128 + 128) + CS1.k(), psk(pb))
                    g.cp(AB.v[:, tt_, gi, :], PS[pb][:], psk(pb), AB.k(tt_, gi), eng=("act" if ev[0] % 2 else "dve"))
                    ev[0] += 1
            N = min(L, 512)
            for gi in range(4):
                for cc in range(2):
                    for tb in range(L // N):
                        pb = 4 + ev[0] % 4
                        for tt_ in range(ntt):
                            g.mm(PS[pb][:, 0:N], AB.v[:, tt_, gi, cc * 128:(cc + 1) * 128], ctab.v[:, tt_, tb * N:(tb + 1) * N],
                                 tt_ == 0, False, AB.k(tt_, gi) + ctab.k(tt_), psk(pb))
                            g.mm(PS[pb][:, 0:N], AB.v[:, tt_, gi, 256 + cc * 128:256 + (cc + 1) * 128], stab.v[:, tt_, tb * N:(tb + 1) * N],
                                 False, tt_ == ntt - 1, AB.k(tt_, gi) + stab.k(tt_), psk(pb))
                        a0 = t0 + tb * N
                        g.cp(H[:, 2 * gi + cc, a0:a0 + N], PS[pb][:, 0:N], psk(pb), kk("H", [2 * gi + cc], a0, a0 + N),
                             eng=("act" if ev[0] % 2 else "dve"))
                        ev[0] += 1

        def in_fn(k, t):
            return H[:, k, t * 512:(t + 1) * 512], kk("H", [k], t * 512, t * 512 + 512)
        out_proj(l, w_out_c[0], in_fn, after_tile=(lambda t: norm_tile(l, 2, t, sq_on_act=True)) if stage >= 4 else None)

    def load_bd():
        for ti, wsrc in enumerate((lru_w_r, lru_w_i)):
            for d in range(2):
                for par_ in range(2):
                    pp = par_ * 64
                    g.dma("pool", BD[pp:pp + 64, ti * 8 + d * 4: ti * 8 + d * 4 + 4, pp:pp + 64],
                          wsrc[0, d, par_::2].rearrange("n i j -> i n j"), (), ["BD"])

    def mixer(l):
        if stage < 2.05:
            return
        norm_mod(l, 1, presum=True)
        if l == 0:
            load_bd()
            mix_ab(l)
        else:
            mix_fourier(l)

    bg = ada_gen(0)
    for _ in range(6):
        next(bg)
    for l in range(2):
        if stage < 10 and l == 1:
            break
        norm_mod(l, 0, presum=(l == 1))
        ffn(l, 0, fw["ffn1_gate"], fw["ffn1_up"], fw["ffn1_down"], bg=bg, next_norm=(stage >= 2.05))
        for _ in bg:
            pass
        dbg_dump()
        if stage < 2:
            break
        mixer(l)
        dbg_dump()
        if stage < 4:
            break
        bg = ada_gen(1) if l == 0 else iter(())
        ffn(l, 2, fw["ffn2_gate"], fw["ffn2_up"], fw["ffn2_down"], bg=bg, final=(l == 1), next_norm=(l == 0))
        if l == 0:
            for _ in range(6 - (18 - 19)):
                pass
        dbg_dump()

    YO = [scr(4 * i, [1024], F32) for i in range(2)]
    for tt_ in (range(12) if stage < 10 else ()):
        yo = YO[tt_ % 2].v
        yok = YO[tt_ % 2].k()
        for cg in range(2):
            pb = (tt_ * 2 + cg) % 4
            for ci in range(4):
                c = cg * 4 + ci
                g.tr(PS[pb][:, ci * 128:(ci + 1) * 128], X[:, c, tt_ * 128:(tt_ + 1) * 128], IDENT[:],
                     kk("X", [c], tt_ * 128, tt_ * 128 + 128) + ["IDENT"], psk(pb))
            g.cp(yo[:, cg * 512:(cg + 1) * 512], PS[pb][:], psk(pb), yok, eng=("dve" if cg else "act"))
        g.dma("sp", yout[tt_ * 128:(tt_ + 1) * 128, :], yo, yok, ())
    g.tr(PS[7][0:16, 0:128], ST[:, 0:16], IDENT[:], ["ST", "IDENT"], psk(7))
    STO = scr(8, [128], F32)
    g.cp(STO.v[0:16, :], PS[7][0:16, 0:128], psk(7), STO.k())
    g.dma("sp", nst_o, STO.v[0:16, :], STO.k(), ())
    P.finish()
    P.emit()
    g.es.close()
    return g


_CONST = {}


def _consts():
    if _CONST:
        return _CONST
    bf = ml_dtypes.bfloat16
    c = np.arange(256, dtype=np.float64)
    a = 2 * np.pi * np.outer(c, c) / 256.0
    _CONST["cs1"] = np.concatenate([np.cos(a), -np.sin(a)], axis=1).astype(np.float32) / 16.0
    _CONST["ct256"] = (np.cos(a) / 16.0).astype(np.float32)
    _CONST["st256"] = (np.sin(a) / 16.0).astype(np.float32)
    t = np.arange(1024, dtype=np.float64)
    a = 2 * np.pi * ((np.outer(t, t)) % 1024) / 1024.0
    _CONST["ct1k"] = (np.cos(a) / 32.0).astype(np.float32)
    _CONST["st1k"] = (np.sin(a) / 32.0).astype(np.float32)
    _CONST["ident"] = np.eye(128, dtype=np.float32)
    return _CONST


def _bias_table(rpb):
    r = np.asarray(rpb)[0]
    kc = np.arange(64)[:, None]
    qc = np.arange(64)[None, :]
    dc = np.clip(kc - qc + 15, 0, 30)
    cs = np.clip(qc - 8, 0, 48)
    col_in = (kc >= cs) & (kc < cs + 16)
    out = np.full((8, 2, 64, 2, 14, 64), -30000.0, np.float32)
    for par in range(2):
        for i in range(14):
            dr = 6 - i + par
            if abs(dr) > 7:
                continue
            vals = np.where(col_in[None], r[:, dr + 7][:, dc], np.float32(-30000.0))
            out[:, par, :, 0, i, :] = vals
            if -4 <= dr <= 3:
                out[:, par, :, 1, i, :] = vals
    return np.ascontiguousarray(out.reshape(8, 128, 2, 14 * 64))


_PROG = {}


def _get_prog(debug=0, stage=99):
    key = (debug, stage)
    if key not in _PROG:
        _PROG[key] = build(debug, stage)
    return _PROG[key]


def make_in_maps(inputs, cores):
    f = lambda a: np.ascontiguousarray(np.asarray(a, dtype=np.float32))
    shared = {k: f(inputs[k]) for k in ("w_ada", "b_ada", "norm_g", "ffn1_gate", "ffn1_up", "ffn1_down", "ffn2_gate",
                                         "ffn2_up", "ffn2_down", "w_in", "q_norm_g", "k_norm_g", "conv_w", "conv_b",
                                         "lru_w_r", "lru_b_r", "lru_w_i", "lru_b_i", "lru_lambda", "w_out_ab", "w_out_c")}
    shared.update(_consts())
    shared["btab"] = _bias_table(inputs["rpb"])
    xp, xs = f(inputs["x_prompt"]), f(inputs["x_sample"])
    maps = []
    for c in cores:
        s = c // 4
        m = dict(shared)
        m["xin"] = np.ascontiguousarray(np.concatenate([xp[2 * c], xp[2 * c + 1], xs[s]], axis=0))
        m["cond"] = np.ascontiguousarray(np.stack([f(inputs["c_ctx"]), f(inputs["c"])[s]], axis=0))
        m["ck"] = np.ascontiguousarray(f(inputs["cache_k"])[s, 0].reshape(256, 512))
        m["cv"] = np.ascontiguousarray(f(inputs["cache_v"])[s, 0].reshape(256, 512))
        m["stf"] = np.ascontiguousarray(f(inputs["state_lru_fwd"])[s, 0])
        m["stb"] = np.ascontiguousarray(f(inputs["state_lru_bwd"])[s, 0])
        maps.append(m)
    return maps


def kernel(**inputs):
    g = _get_prog()
    cores = list(range(8))
    res = run_bass_kernel_spmd(g.nc, make_in_maps(inputs, cores), core_ids=cores).results
    y_prompt = np.zeros((16, 256, 1024), np.float32)
    y_sample = np.zeros((2, 1024, 1024), np.float32)
    nk = np.zeros((16, 1, 256, 8, 64), np.float32)
    nv = np.zeros((16, 1, 256, 8, 64), np.float32)
    sf = np.zeros((16, 1, 512), np.float32)
    sbw = np.zeros((16, 1, 512), np.float32)
    for c in cores:
        r = res[c]
        yo = np.asarray(r["yout"])
        y_prompt[2 * c] = yo[0:256]
        y_prompt[2 * c + 1] = yo[256:512]
        if c % 4 == 0:
            y_sample[c // 4] = yo[512:]
        k_ = np.asarray(r["nk"]).reshape(2, 256, 8, 64)
        v_ = np.asarray(r["nv"]).reshape(2, 256, 8, 64)
        st = np.asarray(r["nst"]).reshape(2, 2, 512)
        for j in range(2):
            nk[2 * c + j, 0] = k_[j]
            nv[2 * c + j, 0] = v_[j]
            sf[2 * c + j, 0] = st[0, j]
            sbw[2 * c + j, 0] = st[1, j]
    return (y_prompt, y_sample, nk, nv, sf, sbw)
```
